# Optimizing a Trainium2 kernel written in Bass

```python
import jax, jax.numpy as jnp
from jax import lax
import numpy as np

D_MODEL = 1024
BATCH = 8
SEQ = 2048
DEPTH = 1
DEC_BATCH = 128
DEC_SEQ = 1
PAST_LEN = 8192
PAGE_SIZE = 128

POOL_WINDOWS = (2, 4, 8, 16)
POOL_GROUPS = len(POOL_WINDOWS)
POOL_GROUP_WIDTH = 128
POOL_WIDTH = POOL_GROUPS * POOL_GROUP_WIDTH
POOL_BUF = max(POOL_WINDOWS) - 1
N_HEADS = 8
N_KV_HEADS = 2
HEAD_DIM = 64
GROUP = N_HEADS // N_KV_HEADS
Q_WIDTH = N_HEADS * HEAD_DIM
KV_WIDTH = N_KV_HEADS * HEAD_DIM
WINDOW = 128
BLOCK = WINDOW
D_FF = ((8 * D_MODEL + 3 * 256 - 1) // (3 * 256)) * 256
IN_WIDTH = POOL_WIDTH + Q_WIDTH + 2 * KV_WIDTH + 2 * D_MODEL
EPS = 1e-6
NEG = -1e30

kernel_name = "gated_pool_swa_hybrid_step"


def rms_norm(x, g):
    xf = x.astype(jnp.float32)
    y = xf * lax.rsqrt(jnp.mean(xf * xf, axis=-1, keepdims=True) + EPS)
    return (y * g.astype(jnp.float32)).astype(x.dtype)


def split_in(h, w_in, q_norm, k_norm):
    z = h @ w_in
    idx = np.cumsum([POOL_WIDTH, Q_WIDTH, KV_WIDTH, KV_WIDTH, D_MODEL]).tolist()
    u, q, k, v, ga, gb = jnp.split(z, idx, axis=-1)
    lead = z.shape[:-1]
    q = rms_norm(q.reshape(*lead, N_HEADS, HEAD_DIM), q_norm)
    k = rms_norm(k.reshape(*lead, N_KV_HEADS, HEAD_DIM), k_norm)
    v = v.reshape(*lead, N_KV_HEADS, HEAD_DIM)
    return u, q, k, v, ga, gb


def pool_mix(u_ext, pos, mix_w, scale):
    n, _, c = u_ext.shape
    t = pos.shape[0]
    uf = u_ext.astype(jnp.float32)
    cs = jnp.concatenate([jnp.zeros((n, 1, c), jnp.float32), jnp.cumsum(uf, axis=1)], axis=1)
    end = cs[:, POOL_BUF + 1:]
    u_new = uf[:, POOL_BUF:]
    outs = []
    for g, w in enumerate(POOL_WINDOWS):
        sl = slice(g * POOL_GROUP_WIDTH, (g + 1) * POOL_GROUP_WIDTH)
        start = cs[:, POOL_BUF + 1 - w: POOL_BUF + 1 - w + t, sl]
        cnt = jnp.minimum(pos + 1, w).astype(jnp.float32)[None, :, None]
        outs.append((end[..., sl] - start) / cnt - u_new[..., sl])
    y = jnp.stack(outs, axis=2).astype(u_ext.dtype)
    y = jnp.einsum("ntgc,gcd->ntgd", y, mix_w).reshape(n, t, POOL_WIDTH)
    return y * scale


def sink_attention(q, k, v, mask, sinks):
    n, tq = q.shape[:2]
    qg = q.reshape(n, tq, N_KV_HEADS, GROUP, HEAD_DIM)
    s = jnp.einsum("nqkgd,nskd->nkgqs", qg, k).astype(jnp.float32) * (HEAD_DIM ** -0.5)
    s = jnp.where(mask, s, NEG)
    sink = sinks.astype(jnp.float32).reshape(N_KV_HEADS, GROUP)[None, :, :, None, None]
    m = jnp.maximum(jnp.max(s, axis=-1, keepdims=True), sink)
    p = jnp.exp(s - m)
    denom = jnp.sum(p, axis=-1, keepdims=True) + jnp.exp(sink - m)
    o = jnp.einsum("nkgqs,nskd->nqkgd", (p / denom).astype(v.dtype), v)
    return o.reshape(n, tq, Q_WIDTH)


def merge_branches(y_pool, y_attn, ga, gb, w_pool_proj, w_attn_proj, w_out):
    merged = jax.nn.sigmoid(ga) * (y_pool @ w_pool_proj) + jax.nn.sigmoid(gb) * (y_attn @ w_attn_proj)
    return merged @ w_out


def swiglu(h, w_gate, w_up, w_down):
    return (jax.nn.silu(h @ w_gate) * (h @ w_up)) @ w_down


def prompt_layer(x, w_buf, norm1, w_in, q_norm, k_norm, sinks, pool_mix_w, pool_scale,
                 w_pool_proj, w_attn_proj, w_out, norm2, w_gate, w_up, w_down):
    b, s, _ = x.shape
    h = rms_norm(x, norm1)
    u, q, k, v, ga, gb = split_in(h, w_in, q_norm, k_norm)
    pos = jnp.arange(s)
    u_ext = jnp.concatenate([jnp.zeros((b, POOL_BUF, POOL_WIDTH), u.dtype), u], axis=1)
    y_pool = pool_mix(u_ext, pos, pool_mix_w, pool_scale)
    nb = s // BLOCK
    qb = q.reshape(b * nb, BLOCK, N_HEADS, HEAD_DIM)

    def band(a):
        ab = a.reshape(b, nb, BLOCK, N_KV_HEADS, HEAD_DIM)
        prev = jnp.concatenate([jnp.zeros_like(ab[:, :1]), ab[:, :-1]], axis=1)
        return jnp.concatenate([prev, ab], axis=2).reshape(b * nb, 2 * BLOCK, N_KV_HEADS, HEAD_DIM)

    kk, vv = band(k), band(v)
    blk = jnp.arange(nb)[:, None]
    qpos = blk * BLOCK + jnp.arange(BLOCK)[None, :]
    kpos = (blk - 1) * BLOCK + jnp.arange(2 * BLOCK)[None, :]
    d = qpos[:, :, None] - kpos[:, None, :]
    mask = (kpos[:, None, :] >= 0) & (d >= 0) & (d <= WINDOW)
    mask = jnp.broadcast_to(mask[None], (b, nb, BLOCK, 2 * BLOCK)).reshape(b * nb, 1, 1, BLOCK, 2 * BLOCK)
    y_attn = sink_attention(qb, kk, vv, mask, sinks).reshape(b, s, Q_WIDTH)
    x = x + merge_branches(y_pool, y_attn, ga, gb, w_pool_proj, w_attn_proj, w_out)
    x = x + swiglu(rms_norm(x, norm2), w_gate, w_up, w_down)
    return x, k[:, s - w_buf:], v[:, s - w_buf:], u[:, s - POOL_BUF:]


def sample_layer(x, cache_k, cache_v, state_pool, norm1, w_in, q_norm, k_norm, sinks, pool_mix_w,
                 pool_scale, w_pool_proj, w_attn_proj, w_out, norm2, w_gate, w_up, w_down):
    _, t, _ = x.shape
    w_buf = cache_k.shape[1]
    h = rms_norm(x, norm1)
    u, q, k, v, ga, gb = split_in(h, w_in, q_norm, k_norm)
    pos = PAST_LEN + jnp.arange(t)
    u_ext = jnp.concatenate([state_pool, u], axis=1)
    y_pool = pool_mix(u_ext, pos, pool_mix_w, pool_scale)
    kk = jnp.concatenate([cache_k, k], axis=1)
    vv = jnp.concatenate([cache_v, v], axis=1)
    kpos = jnp.concatenate([PAST_LEN - w_buf + jnp.arange(w_buf), pos])
    d = pos[:, None] - kpos[None, :]
    mask = ((d >= 0) & (d <= WINDOW))[None, None, None]
    y_attn = sink_attention(q, kk, vv, mask, sinks)
    x = x + merge_branches(y_pool, y_attn, ga, gb, w_pool_proj, w_attn_proj, w_out)
    x = x + swiglu(rms_norm(x, norm2), w_gate, w_up, w_down)
    return x, kk[:, -w_buf:], vv[:, -w_buf:], u_ext[:, -POOL_BUF:]


def setup_inputs(seed: int = 0) -> dict:
    key = jax.random.key(seed)
    ks = jax.random.split(key, 20)
    w_buf = min(WINDOW, PAST_LEN)
    f32 = jnp.float32

    def nrm(k, shape, scale):
        return jax.random.normal(k, shape, f32) * scale

    return {
        "x_prompt": nrm(ks[0], (BATCH, SEQ, D_MODEL), 1.0),
        "x_sample": nrm(ks[1], (DEC_BATCH, DEC_SEQ, D_MODEL), 1.0),
        "cache_k": nrm(ks[2], (DEPTH, DEC_BATCH, w_buf, N_KV_HEADS, HEAD_DIM), 1.0),
        "cache_v": nrm(ks[3], (DEPTH, DEC_BATCH, w_buf, N_KV_HEADS, HEAD_DIM), 1.0),
        "state_pool": nrm(ks[4], (DEPTH, DEC_BATCH, POOL_BUF, POOL_WIDTH), 1.0),
        "norm1": 1.0 + nrm(ks[5], (DEPTH, D_MODEL), 0.05),
        "w_in": nrm(ks[6], (DEPTH, D_MODEL, IN_WIDTH), D_MODEL ** -0.5),
        "q_norm": 1.0 + nrm(ks[7], (DEPTH, HEAD_DIM), 0.05),
        "k_norm": 1.0 + nrm(ks[8], (DEPTH, HEAD_DIM), 0.05),
        "sinks": nrm(ks[9], (DEPTH, N_HEADS), 0.5),
        "pool_mix_w": nrm(ks[10], (DEPTH, POOL_GROUPS, POOL_GROUP_WIDTH, POOL_GROUP_WIDTH), POOL_GROUP_WIDTH ** -0.5),
        "pool_scale": 1.0 + nrm(ks[11], (DEPTH, POOL_WIDTH), 0.1),
        "w_pool_proj": nrm(ks[12], (DEPTH, POOL_WIDTH, D_MODEL), POOL_WIDTH ** -0.5),
        "w_attn_proj": nrm(ks[13], (DEPTH, Q_WIDTH, D_MODEL), Q_WIDTH ** -0.5),
        "w_out": nrm(ks[14], (DEPTH, D_MODEL, D_MODEL), D_MODEL ** -0.5),
        "norm2": 1.0 + nrm(ks[15], (DEPTH, D_MODEL), 0.05),
        "w_gate": nrm(ks[16], (DEPTH, D_MODEL, D_FF), D_MODEL ** -0.5),
        "w_up": nrm(ks[17], (DEPTH, D_MODEL, D_FF), D_MODEL ** -0.5),
        "w_down": nrm(ks[18], (DEPTH, D_FF, D_MODEL), D_FF ** -0.5),
    }


def reference(x_prompt, x_sample, cache_k, cache_v, state_pool, norm1, w_in, q_norm, k_norm, sinks,
              pool_mix_w, pool_scale, w_pool_proj, w_attn_proj, w_out, norm2, w_gate, w_up, w_down):
    w_buf = cache_k.shape[2]
    xp, xs = x_prompt, x_sample
    kp, vp, pp, ksn, vsn, psn = [], [], [], [], [], []
    for l in range(DEPTH):
        shared = (norm1[l], w_in[l], q_norm[l], k_norm[l], sinks[l], pool_mix_w[l], pool_scale[l],
                  w_pool_proj[l], w_attn_proj[l], w_out[l], norm2[l], w_gate[l], w_up[l], w_down[l])
        xp, k_l, v_l, p_l = prompt_layer(xp, w_buf, *shared)
        xs, ks_l, vs_l, ps_l = sample_layer(xs, cache_k[l], cache_v[l], state_pool[l], *shared)
        kp.append(k_l); vp.append(v_l); pp.append(p_l)
        ksn.append(ks_l); vsn.append(vs_l); psn.append(ps_l)
    return (xp, xs, jnp.stack(kp), jnp.stack(vp), jnp.stack(pp), jnp.stack(ksn), jnp.stack(vsn), jnp.stack(psn))
```

```python
import numpy as np
import concourse.bass as bass
import concourse.mybir as mybir
from concourse.bass_utils import run_bass_kernel_spmd

F32 = mybir.dt.float32
BF16 = mybir.dt.bfloat16
AF = mybir.ActivationFunctionType
ALU = mybir.AluOpType
AX = mybir.AxisListType

NCORES = 8
D = 1024
SEQ = 2048
NSMP = 16
T = 1024
NPASS = SEQ // T
NT = T // 128
DFF = 2816
INW = 3328
EPS = 1e-6
SAMPLE_PASS = 0
NSLOT = 7
SLOT = 2048
POOL_W = (2, 4, 8, 16)
NCOLMAX = T + NSMP
FFN_GROUPS = ((0, 1, 2, 3), (4, 5, 6, 7), (8, 9, 10))
ENABLE_SAMPLE = True
SINK_VIA_PE = True
SMP_MODE = "serial"
USE_ACT_RECIP = True
MASK_NEG = -30000.0
DBG_SKIP_SATTN = False
DBG_SATTN_LEVEL = 9


class _Op:
    __slots__ = ("eng", "fn", "reads", "writes", "dsem", "idx", "deps", "signal", "token", "extra")


class _Recorder:
    def __init__(self):
        self.call = None

    def __getattr__(self, name):
        def f(*args, **kwargs):
            assert self.call is None
            self.call = (name, args, kwargs)
            return None
        return f


class Prog:
    ENGS = ("pe", "act", "dve", "pool", "sp")

    def __init__(self, nc):
        self.nc = nc
        self.ops = []
        self.sems = {}
        self.dcount = {}

    def sem(self, name):
        if name not in self.sems:
            self.sems[name] = self.nc.alloc_semaphore(name)
        return self.sems[name]

    def add(self, eng, fn, reads=(), writes=(), dsem=None, extra=()):
        if fn is not None:
            rec = _Recorder()
            fn(rec)
            assert rec.call is not None
            name, args, kwargs = rec.call
            fn = lambda e, name=name, args=args, kwargs=kwargs: getattr(e, name)(*args, **kwargs)
        op = _Op()
        op.eng, op.fn, op.reads, op.writes, op.dsem = eng, fn, tuple(reads), tuple(writes), dsem
        op.extra = tuple(extra)
        op.idx = len(self.ops)
        op.signal = dsem is not None
        op.token = None
        self.ops.append(op)
        return op

    def dma(self, eng, out, in_, reads=(), writes=(), dsem=None, extra=()):
        return self.add(eng, lambda e, o=out, i=in_: e.dma_start(out=o, in_=i), reads, writes, dsem=dsem, extra=extra)

    def analyze(self):
        last_w = {}
        readers = {}
        for op in self.ops:
            deps = set(o.idx for o in op.extra)
            for k in op.reads:
                if k in last_w:
                    deps.add(last_w[k])
            for k in op.writes:
                if k in last_w:
                    deps.add(last_w[k])
                deps.update(readers.get(k, ()))
            deps.discard(op.idx)
            if op.eng == "pe":
                deps = set(d for d in deps if self.ops[d].eng != "pe" or self.ops[d].dsem is not None)
            latest = {}
            pruned = set()
            for d in deps:
                od = self.ops[d]
                if od.dsem is not None:
                    pruned.add(d)
                elif latest.get(od.eng, -1) < d:
                    latest[od.eng] = d
            pruned.update(latest.values())
            deps = pruned
            op.deps = deps
            for d in deps:
                self.ops[d].signal = True
            for k in op.reads:
                readers.setdefault(k, []).append(op.idx)
            for k in op.writes:
                last_w[k] = op.idx
                readers[k] = []
        cnt = {e: 0 for e in self.ENGS}
        for op in self.ops:
            if op.dsem is not None:
                self.dcount[op.dsem] = self.dcount.get(op.dsem, 0) + 16
                op.token = (op.dsem, self.dcount[op.dsem])
            elif op.signal:
                cnt[op.eng] += 1
                op.token = ("eng_" + op.eng, cnt[op.eng])

    def emit_engine(self, eng, e):
        waited = {}
        for op in self.ops:
            if op.eng != eng:
                continue
            need = {}
            for d in op.deps:
                s, v = self.ops[d].token
                if need.get(s, 0) < v:
                    need[s] = v
            for s, v in need.items():
                if waited.get(s, 0) < v:
                    e.wait_ge(self.sem(s), v)
                    waited[s] = v
            if op.fn is None:
                continue
            ins = op.fn(e)
            if op.token is not None:
                ins.then_inc(self.sem(op.token[0]), 16 if op.dsem is not None else 1)

    def emit(self):
        self.analyze()
        for op in self.ops:
            if op.token is not None:
                self.sem(op.token[0])
        nc = self.nc
        with nc.Block() as block:
            @block.tensor
            def _(e):
                self.emit_engine("pe", e)

            @block.scalar
            def _(e):
                self.emit_engine("act", e)

            @block.vector
            def _(e):
                self.emit_engine("dve", e)

            @block.gpsimd
            def _(e):
                self.emit_engine("pool", e)

            @block.sync
            def _(e):
                self.emit_engine("sp", e)


def _tile_cols(i):
    return (128 * i, 128) if i < NT else (T, NSMP)


def ckeys(name, c0, n, *pre):
    keys = []
    for i in range(NT + 1):
        lo, w = _tile_cols(i)
        if lo < c0 + n and c0 < lo + w:
            keys.append((name,) + pre + (i,))
    return keys


def build_program():
    nc = bass.Bass("TRN2", target_bir_lowering=False)
    P = Prog(nc)

    def din(name, shape):
        return nc.dram_tensor(name, list(shape), F32, kind="ExternalInput").ap()

    def dout(name, shape):
        return nc.dram_tensor(name, list(shape), F32, kind="ExternalOutput").ap()

    x_d = din("x", [SEQ, D])
    xs_d = din("xs", [NSMP, D])
    ck_d = din("ck", [NSMP, 128, 128])
    cv_d = din("cv", [NSMP, 128, 128])
    ckT_d = din("ckT", [NSMP, 128, 128])
    st_d = din("st", [NSMP, 15, 512])
    n1_d = din("norm1", [1, D])
    win_d = din("w_in", [D, INW])
    qn_d = din("q_norm", [1, 64])
    kn_d = din("k_norm", [1, 64])
    sk_d = din("sinks", [1, 8])
    pm_d = din("pool_mix_w", [4, 128, 128])
    psc_d = din("pool_scale", [512])
    wp_d = din("w_pool_proj", [512, D])
    wa_d = din("w_attn_proj", [512, D])
    wo_d = din("w_out", [D, D])
    n2_d = din("norm2", [1, D])
    wg_d = din("w_gate", [D, DFF])
    wu_d = din("w_up", [D, DFF])
    wd_d = din("w_down", [DFF, D])

    y_d = dout("y", [SEQ, D])
    ys_d = dout("ys", [NSMP, D])
    kp_d = dout("kp", [128, 128])
    vp_d = dout("vp", [128, 128])
    pp_d = dout("pp", [15, 512])
    ksn_d = dout("ksn", [NSMP, 128, 128])
    vsn_d = dout("vsn", [NSMP, 128, 128])
    psn_d = dout("psn", [NSMP, 15, 512])

    def sb(name, shape, dt):
        return nc.alloc_sbuf_tensor(name, list(shape), dt)

    xres = sb("xres", [128, NT + 1, D], F32)
    hT = sb("hT", [128, 8, NCOLMAX], BF16)
    gb = sb("gb", [128, D], F32)
    ident = sb("ident", [128, 128], BF16)
    maskP = sb("maskP", [128, 128], BF16)
    maskC = sb("maskC", [128, 128], BF16)
    ones64 = sb("ones64", [128, 64], BF16)
    ring = sb("ring", [128, NSLOT * 8, 256], BF16)
    wo = sb("wo", [128, 8, D], BF16)
    wmix = sb("wmix", [128, 4, 128], BF16)
    r1 = sb("r1", [128, 8, NCOLMAX], BF16)
    hn = sb("hn", [128, 2, D], BF16)
    UW = 16 + NCOLMAX
    um = sb("um", [128, 4 * UW], F32)
    uT = um[:, :].rearrange("p (g n) -> p g n", g=4)
    mT = um[:, :].bitcast(BF16).rearrange("p (c n) -> p c n", c=8)
    qT = sb("qT", [128, 4, NCOLMAX], BF16)
    kT = sb("kT", [128, 128 + T], BF16)
    vv = sb("vv", [128, NT + 1, 128], BF16)
    poolA = sb("poolA", [128, 16 + 512], F32)
    poolB = sb("poolB", [128, 16 + 512], F32)
    pooledT = sb("pooledT", [128, 4, 528], BF16)
    junk = poolA[:, 0:512].bitcast(BF16)
    kwb = sb("kwb", [128, NSMP, 128], BF16)
    vwb = sb("vwb", [128, NSMP, 128], BF16)
    ypT = sb("ypT", [128, 4, NCOLMAX], BF16)
    yaT = sb("yaT", [128, 4, NCOLMAX], BF16)
    NTF, NTB = 5, 8
    tmpF = sb("tmpF", [128, NTF, 512], F32)
    tmpB = sb("tmpB", [128, NTB, 512], BF16)
    stat = sb("stat", [128, 128], F32)
    acc = sb("acc", [128, 64], F32)
    uhist = sb("uhist", [128, 4, 16], F32)
    rc = sb("rc", [128, 4, 16], F32)
    psc = sb("psc", [128, 4], F32)
    gqb = sb("gqb", [128, 64], F32)
    gkb = sb("gkb", [128, 64], F32)
    skb = sb("skb", [128, 8], F32)
    sk4 = sb("sk4", [128, 4], F32)
    E4 = sb("E4", [128, 4], F32)
    E8 = sb("E8", [128, 8], F32)
    E8b = sb("E8b", [128, 8], BF16)
    negM = sb("negM", [128, 1], F32)
    mtmp = sb("mtmp", [128, 8], F32)
    sel = sb("sel", [128, 4, 8], F32)
    v0b = sb("v0b", [NSMP, 128], BF16)
    qn_s = sb("qn_s", [NSMP, 512], BF16)
    gkq = sb("gkq", [128, 64], F32)
    ssum = sb("ssum", [128, 64], F32)
    knf_s = sb("knf_s", [NSMP, 128], F32)
    vf_s = sb("vf_s", [NSMP, 128], F32)

    wp = r1[:, 0:4, 0:D]
    wa = r1
    wa = r1[:, 4:8, 0:D]
    actT = r1

    NB = 6
    pb = [nc.alloc_psum_tensor("pb%d" % i, [128, 512], F32) for i in range(NB)]
    tpf = [nc.alloc_psum_tensor("tp%d" % i, [128, 512], F32) for i in range(2)]
    tp = [t[:].bitcast(BF16) for t in tpf]

    state = {"bank": 0, "tf": 0, "tb": 0, "stat": 0, "tp": 0, "acc": 0}

    def nbank(exclude=()):
        while True:
            b = state["bank"]
            state["bank"] = (b + 1) % NB
            if b not in exclude:
                return b

    def ntf():
        b = state["tf"]
        state["tf"] = (b + 1) % NTF
        return b

    def ntb():
        b = state["tb"]
        state["tb"] = (b + 1) % NTB
        return b

    def nstat(n=1):
        assert n <= 32
        b = state["stat"]
        state["stat"] = (b + 1) % 4
        return 32 * b

    def nacc():
        b = state["acc"]
        state["acc"] = b + 1
        assert state["acc"] <= 64
        return b

    def ntp():
        b = state["tp"]
        state["tp"] = 1 - b
        return b

    def fence(eng, reads, writes):
        if eng == "dve":
            P.add("dve", lambda e: e.engine_nop(), reads, writes)
        else:
            raise ValueError(eng)

    chunks = []

    def win_slab(c0):
        return win_d[:, c0:c0 + 256].rearrange("(kc p) n -> p kc n", p=128)

    for p in range(NPASS):
        for c0 in (512, 768, 1024, 0, 256):
            chunks.append(win_slab(c0))
        for m in range(4):
            chunks.append(win_slab(1280 + 256 * m))
            chunks.append(win_slab(2304 + 256 * m))
        for grp in FFN_GROUPS:
            for s in grp:
                chunks.append(wg_d[:, 256 * s:256 * s + 256].rearrange("(kc p) n -> p kc n", p=128))
                chunks.append(wu_d[:, 256 * s:256 * s + 256].rearrange("(kc p) n -> p kc n", p=128))
            for s in grp:
                chunks.append(wd_d[256 * s:256 * s + 256, :].rearrange("(fc p) (a n) -> p fc a n", p=128, n=256))
    CPP = len(chunks) // NPASS
    stream = {"issued": 0}

    def slot_of(c):
        return c % NSLOT

    def issue_chunk():
        c = stream["issued"]
        if c >= len(chunks):
            return
        s = slot_of(c)
        dst = ring[:, s * 8:(s + 1) * 8, :]
        if len(chunks[c].shape) == 4:
            dst = dst.rearrange("p (fc a) n -> p fc a n", fc=2)
        P.dma("pool", dst, chunks[c], writes=[("ring", s)], dsem="ring%d" % s)
        stream["issued"] = c + 1

    def release_chunk(c):
        assert stream["issued"] == c + NSLOT or stream["issued"] == len(chunks), (stream["issued"], c)
        issue_chunk()

    def slab(c, kc, n0, n):
        s = slot_of(c)
        return ring[:, s * 8 + kc, n0:n0 + n]

    def dchunk(c, fc, hf):
        s = slot_of(c)
        r0 = s * 8 + fc * 4 + hf * 2
        return ring[:, r0:r0 + 2, :]

    P.add("pool", lambda e: e.memset(ident[:], 1.0), writes=[("ident",)])
    P.add("pool", lambda e: e.affine_select(out=ident[:], in_=ident[:], pattern=[[-1, 128]], compare_op=ALU.is_equal,
                                            fill=0.0, base=0, channel_multiplier=1), reads=[("ident",)], writes=[("ident",)])
    for c in range(NSLOT):
        issue_chunk()
    if ENABLE_SAMPLE:
        for h in range(2):
            P.dma("pool", kwb[:, 8 * h:8 * h + 8, :], ckT_d[8 * h:8 * h + 8, :, :].rearrange("n r c -> r n c"),
                  writes=[("kwb", h)], dsem="kwl%d" % h)
            P.dma("pool", vwb[:, 8 * h:8 * h + 8, :], cv_d[8 * h:8 * h + 8, :, :].rearrange("n r c -> r n c"),
                  writes=[("vwb", h)], dsem="vwl%d" % h)
    P.dma("pool", wmix[:], pm_d.rearrange("g c d -> c g d"), writes=[("wmix",)], dsem="wsmall")
    for k in range(4):
        P.dma("pool", wo[:, 2 * k:2 * k + 2, :], wo_d[256 * k:256 * k + 256, :].rearrange("(kc p) n -> p kc n", p=128),
              writes=[("wo", k)], dsem="wo%d" % k)
    P.add("pool", lambda e: e.memset(maskP[:], 0.0), writes=[("maskP",)])
    P.add("pool", lambda e: e.affine_select(out=maskP[:], in_=maskP[:], pattern=[[-1, 128]], compare_op=ALU.is_ge,
                                            fill=MASK_NEG, base=0, channel_multiplier=1), reads=[("maskP",)], writes=[("maskP",)])
    P.add("pool", lambda e: e.memset(maskC[:], 0.0), writes=[("maskC",)])
    P.add("pool", lambda e: e.affine_select(out=maskC[:], in_=maskC[:], pattern=[[1, 128]], compare_op=ALU.is_ge,
                                            fill=MASK_NEG, base=0, channel_multiplier=-1), reads=[("maskC",)], writes=[("maskC",)])
    P.add("pool", lambda e: e.memset(ones64[:], 1.0), writes=[("ones64",)])
    for g, w in enumerate(POOL_W):
        P.add("pool", lambda e, g=g, w=w: e.memset(sel[:, g, :], 1.0 / w), writes=[("sel", g)])
        P.add("pool", lambda e, g=g, w=w: e.affine_select(out=sel[:, g, :], in_=sel[:, g, :], pattern=[[-15, 8]],
                                                          compare_op=ALU.is_ge, fill=0.0, base=-(16 - w),
                                                          channel_multiplier=1), reads=[("sel", g)], writes=[("sel", g)])
        P.add("pool", lambda e, g=g, w=w: e.affine_select(out=sel[:, g, :], in_=sel[:, g, :], pattern=[[15, 8]],
                                                          compare_op=ALU.is_ge, fill=0.0, base=14,
                                                          channel_multiplier=-1), reads=[("sel", g)], writes=[("sel", g)])

    P.dma("sp", gqb[:], qn_d.broadcast_to([128, 64]), writes=[("gqb",)], dsem="sm_gq")
    P.dma("sp", gkb[:], kn_d.broadcast_to([128, 64]), writes=[("gkb",)], dsem="sm_gk")
    P.dma("sp", skb[:], sk_d.broadcast_to([128, 8]), writes=[("skb",)], dsem="sm_sk")
    P.dma("sp", sk4[0:64, :], sk_d[:, 0:4].broadcast_to([64, 4]), writes=[("sk4",)], dsem="sm_sk4")
    P.dma("sp", sk4[64:128, :], sk_d[:, 4:8].broadcast_to([64, 4]), writes=[("sk4",)], dsem="sm_sk4")
    for g in range(4):
        P.dma("sp", psc[:, g:g + 1], psc_d[128 * g:128 * g + 128].rearrange("(c o) -> c o", o=1), writes=[("psc", g)],
              dsem="sm_psc%d" % g)

    P.add("dve", lambda e: e.memset(acc[:], 0.0), writes=[("acc",)])
    P.add("dve", lambda e: e.tensor_tensor(out=gkq[:], in0=gkb[:], in1=gqb[:], op=ALU.mult),
          reads=[("gkb",), ("gqb",)], writes=[("gkq",)])
    P.add("dve", lambda e: e.memset(uT[:, :, 0:16], 0.0), writes=[("uThist",)])
    for g, w in enumerate(POOL_W):
        P.add("dve", lambda e, g=g, w=w: e.memset(rc[:, g, :], 1.0 / w), writes=[("rc",)])
        for pos in range(w - 1):
            P.add("dve", lambda e, g=g, pos=pos: e.memset(rc[:, g, pos:pos + 1], 1.0 / (pos + 1)), writes=[("rc",)])

    P.add("dve", lambda e: e.tensor_reduce(out=mtmp[:, 0:1], in_=gqb[:], axis=AX.X, op=ALU.max, apply_absolute_value=True),
          reads=[("gqb",)], writes=[("mtmp", 0)])
    P.add("dve", lambda e: e.tensor_reduce(out=mtmp[:, 1:2], in_=gkb[:], axis=AX.X, op=ALU.max, apply_absolute_value=True),
          reads=[("gkb",)], writes=[("mtmp", 1)])
    P.add("dve", lambda e: e.tensor_reduce(out=mtmp[:, 2:3], in_=skb[:], axis=AX.X, op=ALU.max),
          reads=[("skb",)], writes=[("mtmp", 2)])
    P.add("dve", lambda e: e.scalar_tensor_tensor(out=mtmp[:, 3:4], in0=mtmp[:, 0:1], scalar=8.0, in1=mtmp[:, 1:2],
                                                  op0=ALU.mult, op1=ALU.mult),
          reads=[("mtmp", 0), ("mtmp", 1)], writes=[("mtmp", 3)])
    P.add("dve", lambda e: e.tensor_tensor(out=mtmp[:, 4:5], in0=mtmp[:, 3:4], in1=mtmp[:, 2:3], op=ALU.max),
          reads=[("mtmp", 3), ("mtmp", 2)], writes=[("mtmp", 4)])
    P.add("dve", lambda e: e.tensor_scalar(out=negM[:], in0=mtmp[:, 4:5], scalar1=-1.0, scalar2=None, op0=ALU.mult),
          reads=[("mtmp", 4)], writes=[("negM",)])
    P.add("act", lambda e: e.activation(out=E4[:], in_=sk4[:], func=AF.Exp, bias=negM[:, 0:1], scale=1.0),
          reads=[("sk4",), ("negM",)], writes=[("E4",)])
    P.add("act", lambda e: e.activation(out=E8[:], in_=skb[:], func=AF.Exp, bias=negM[:, 0:1], scale=1.0),
          reads=[("skb",), ("negM",)], writes=[("E8",)])
    P.add("act", lambda e: e.activation(out=E8b[:], in_=E8[:], func=AF.Copy), reads=[("E8",)], writes=[("E8b",)])

    out_stores = []
    prefetched_x = set()

    def pass_tiles(p):
        tl = [(i, 128, 128 * i) for i in range(NT)]
        if p == SAMPLE_PASS and ENABLE_SAMPLE:
            tl.append((NT, NSMP, T))
        return tl

    def pass_subs(p):
        if p == SAMPLE_PASS and ENABLE_SAMPLE:
            return [(0, 348), (348, 348), (696, 344)]
        return [(0, 512), (512, 512)]

    def load_gain(src_d):
        P.dma("sp", gb[:], src_d.broadcast_to([128, D]), writes=[("gb",)], dsem="gbl")

    def norm_pre(p, i, R):
        c_ss = nstat(3)
        c_ac = nacc()
        b = i % 2
        P.add("act", lambda e: e.activation(out=junk[:R, :], in_=xres[:R, i, :], func=AF.Square,
                                            accum_out=acc[:R, c_ac:c_ac + 1]),
              reads=[("xres", i, 0), ("xres", i, 1), ("acc",)], writes=[("ac", c_ac)])
        P.add("act", lambda e: e.activation(out=stat[:R, c_ss + 1:c_ss + 2], in_=acc[:R, c_ac:c_ac + 1], func=AF.Ln,
                                            scale=1.0 / D, bias=EPS),
              reads=[("ac", c_ac)], writes=[("sg", c_ss)])
        P.add("act", lambda e: e.activation(out=stat[:R, c_ss + 2:c_ss + 3], in_=stat[:R, c_ss + 1:c_ss + 2], func=AF.Exp,
                                            scale=-0.5),
              reads=[("sg", c_ss)], writes=[("sg", c_ss)])
        P.add("dve", lambda e: e.scalar_tensor_tensor(out=hn[:R, b, :], in0=xres[:R, i, :], scalar=stat[:R, c_ss + 2:c_ss + 3],
                                                      in1=gb[:R, :], op0=ALU.mult, op1=ALU.mult),
              reads=[("xres", i, 0), ("xres", i, 1), ("sg", c_ss), ("gb",)], writes=[("hn", b)])
        return b

    def norm_post(p, i, R, c0, b):
        t = ntp()
        tpv = tp[t].rearrange("p (k n) -> p k n", k=8)
        for kc in range(8):
            P.add("pe", lambda e, kc=kc: e.transpose(tpv[:, kc, :R], hn[:R, b, 128 * kc:128 * kc + 128], ident[:R, :R]),
                  reads=[("hn", b), ("ident",)], writes=[("tp", t)])
        if i % 2 == 0:
            P.add("act", lambda e: e.activation(out=hT[:, :, c0:c0 + R], in_=tpv[:, :, :R], func=AF.Copy),
                  reads=[("tp", t)], writes=[("hT", i)])
        else:
            P.add("dve", lambda e: e.tensor_copy(out=hT[:, :, c0:c0 + R], in_=tpv[:, :, :R]),
                  reads=[("tp", t)], writes=[("hT", i)])

    def sample_key_transposes():
        for h in range(2):
            t = ntp()
            tpv = tp[t].rearrange("p (k n) -> p k n", k=8)
            for nn in range(8):
                n = 8 * h + nn
                P.add("pe", lambda e, nn=nn, n=n: e.transpose(tpv[:, nn, :], kwb[:, n, :], ident[:, :]),
                      reads=[("kwb", h), ("ident",)], writes=[("tp", t)])
            P.add("act", lambda e, h=h: e.activation(out=kwb[:, 8 * h:8 * h + 8, :], in_=tpv[:, :, :], func=AF.Copy),
                  reads=[("tp", t)], writes=[("kwb", h)])

    def stage_A0(p, cb):
        tiles = pass_tiles(p)
        for (i, R, c0) in tiles:
            if i < NT:
                src = x_d[p * T + 128 * i:p * T + 128 * i + 128, :]
            else:
                src = xs_d
            if (p, i) in prefetched_x:
                continue
            if i == 1 and (p, 0) not in prefetched_x:
                load_gain(n1_d)
            P.dma("sp", xres[:R, i, :], src, writes=[("xres", i, 0), ("xres", i, 1)], dsem="xl%d" % i)
        a1 = stage_A1(p, cb)
        pend = None
        for (i, R, c0) in tiles:
            b = norm_pre(p, i, R)
            if pend is not None:
                norm_post(p, *pend)
                a1.tile(pend[0] - 1)
            pend = (i, R, c0, b)
        norm_post(p, *pend)
        a1.tile(pend[0] - 1)
        a1.tile(pend[0])
        a1.finish()

    def stage_A1(p, cb):
        tiles = pass_tiles(p)
        subs = pass_subs(p)
        last_pass = (p == NPASS - 1)
        u_done = set()

        def emit_u(upto_tile):
            for si, (s0, n) in enumerate(subs):
                if si in u_done:
                    continue
                need = max(k[-1] for k in ckeys("hT", s0, n))
                if need > upto_tile:
                    continue
                u_done.add(si)
                for g in range(4):
                    ch = cb + 3 + g // 2
                    bank = nbank()
                    for kc in range(8):
                        P.add("pe", lambda e, bank=bank, kc=kc, s0=s0, n=n, ch=ch, g=g: e.matmul(
                            pb[bank][:, 0:n], lhsT=slab(ch, kc, 128 * (g % 2), 128), rhs=hT[:, kc, s0:s0 + n],
                            start=(kc == 0), stop=(kc == 7)),
                            reads=ckeys("hT", s0, n) + [("ring", slot_of(ch))], writes=[("pb", bank)])
                    P.add("act", lambda e, bank=bank, s0=s0, n=n, g=g: e.activation(out=uT[:, g, 16 + s0:16 + s0 + n],
                                                                                  in_=pb[bank][:, 0:n], func=AF.Copy),
                          reads=[("pb", bank)], writes=ckeys("uT", s0, n, g))

        pend_b = []
        tile_by_idx = {t_[0]: t_ for t_ in tiles}

        def tile_body(i, R, c0):
            if i >= 1:
                emit_u(i - 1)
            bq0, bq1, bkv = nbank(), nbank(), nbank()
            for (bank, ch) in ((bq0, cb + 0), (bq1, cb + 1), (bkv, cb + 2)):
                for kc in range(8):
                    P.add("pe", lambda e, bank=bank, ch=ch, kc=kc: e.matmul(pb[bank][:R, 0:256], lhsT=hT[:, kc, c0:c0 + R],
                                                                          rhs=slab(ch, kc, 0, 256), start=(kc == 0), stop=(kc == 7)),
                          reads=[("hT", i), ("ring", slot_of(ch))], writes=[("pb", bank)])
            f_sq, f_q = ntf(), ntf()
            c_st = nstat(30)
            P.add("act", lambda e: e.activation(out=tmpF[:R, f_sq, 0:256], in_=pb[bq0][:R, 0:256], func=AF.Square),
                  reads=[("pb", bq0)], writes=[("tmpF", f_sq)])
            P.add("act", lambda e: e.activation(out=tmpF[:R, f_sq, 256:512], in_=pb[bq1][:R, 0:256], func=AF.Square),
                  reads=[("pb", bq1)], writes=[("tmpF", f_sq)])
            P.add("dve", lambda e: e.tensor_reduce(out=stat[:R, c_st:c_st + 8],
                                                   in_=tmpF[:R, f_sq, :].rearrange("p (h d) -> p h d", d=64), axis=AX.X, op=ALU.add),
                  reads=[("tmpF", f_sq)], writes=[("sg", c_st)])
            f_k = ntf()
            P.add("act", lambda e: e.activation(out=tmpF[:R, f_k, 0:128], in_=pb[bkv][:R, 0:128], func=AF.Square),
                  reads=[("pb", bkv)], writes=[("tmpF", f_k)])
            P.add("dve", lambda e: e.tensor_reduce(out=stat[:R, c_st + 8:c_st + 10],
                                                   in_=tmpF[:R, f_k, 0:128].rearrange("p (h d) -> p h d", d=64), axis=AX.X, op=ALU.add),
                  reads=[("tmpF", f_k)], writes=[("sg", c_st)])
            P.add("act", lambda e: e.activation(out=stat[:R, c_st + 10:c_st + 20], in_=stat[:R, c_st:c_st + 10], func=AF.Ln,
                                                scale=1.0 / 64, bias=EPS),
                  reads=[("sg", c_st), ("sg", c_st)], writes=[("sg", c_st)])
            P.add("act", lambda e: e.activation(out=stat[:R, c_st + 20:c_st + 30], in_=stat[:R, c_st + 10:c_st + 20], func=AF.Exp,
                                                scale=-0.5),
                  reads=[("sg", c_st)], writes=[("sg", c_st)])
            b_qn = ntb()
            is_smp = (i == NT)
            qn_dst = qn_s[:R, :] if is_smp else tmpB[:R, b_qn, :]
            qn_key = ("qn_s",) if is_smp else ("tmpB", b_qn)
            if not is_smp:
                for kvh, bank in ((0, bq0), (1, bq1)):
                    P.add("dve", lambda e, kvh=kvh, bank=bank: e.tensor_tensor(
                        out=qn_dst.rearrange("p (j k d) -> p j k d", j=4, k=2)[:, :, kvh, :],
                        in0=pb[bank][:R, 0:256].rearrange("p (j d) -> p j d", d=64),
                        in1=stat[:R, c_st + 20 + 4 * kvh:c_st + 24 + 4 * kvh].unsqueeze(2).broadcast_to([R, 4, 64]), op=ALU.mult),
                        reads=[("pb", bank), ("sg", c_st)], writes=[qn_key])
            for half, bank in (((0, bq0), (1, bq1)) if is_smp else ()):
                P.add("dve", lambda e, half=half, bank=bank: e.tensor_tensor(
                    out=tmpF[:R, f_q, 256 * half:256 * half + 256].rearrange("p (h d) -> p h d", d=64),
                    in0=pb[bank][:R, 0:256].rearrange("p (h d) -> p h d", d=64),
                    in1=stat[:R, c_st + 20 + 4 * half:c_st + 24 + 4 * half].unsqueeze(2).broadcast_to([R, 4, 64]), op=ALU.mult),
                    reads=[("pb", bank), ("sg", c_st)], writes=[("tmpF", f_q)])
            for kvh in (range(2) if is_smp else ()):
                P.add("dve", lambda e, kvh=kvh: e.tensor_tensor(
                    out=qn_dst.rearrange("p (j k d) -> p j k d", j=4, k=2)[:, :, kvh, :],
                    in0=tmpF[:R, f_q, 256 * kvh:256 * kvh + 256].rearrange("p (j d) -> p j d", d=64),
                    in1=gqb[:R, :].unsqueeze(1).broadcast_to([R, 4, 64]), op=ALU.mult),
                    reads=[("tmpF", f_q), ("gqb",)], writes=[qn_key])
            is_out_tile = (last_pass and i == NT - 1)
            is_smp = (i == NT)
            f_kn = ntf()
            b_kn = ntb()
            kdst = knf_s[:R, :] if is_smp else tmpF[:R, f_kn, 128:256]
            kdst_keys = [("knf_s",)] if is_smp else [("tmpF", f_kn)]
            plain_tile = (not is_out_tile) and (not is_smp)
            if plain_tile:
                for h in range(2):
                    P.add("dve", lambda e, h=h: e.scalar_tensor_tensor(
                        out=tmpB[:R, b_kn, 64 * h:64 * h + 64], in0=pb[bkv][:R, 64 * h:64 * h + 64],
                        scalar=stat[:R, c_st + 28 + h:c_st + 29 + h], in1=gkq[:R, :], op0=ALU.mult, op1=ALU.mult),
                        reads=[("pb", bkv), ("sg", c_st), ("gkq",)], writes=[("tmpB", b_kn)])
            else:
                P.add("dve", lambda e: e.tensor_tensor(
                    out=tmpF[:R, f_kn, 0:128].rearrange("p (h d) -> p h d", d=64),
                    in0=pb[bkv][:R, 0:128].rearrange("p (h d) -> p h d", d=64),
                    in1=stat[:R, c_st + 28:c_st + 30].unsqueeze(2).broadcast_to([R, 2, 64]), op=ALU.mult),
                    reads=[("pb", bkv), ("sg", c_st)], writes=[("tmpF", f_kn)])
            if not is_smp:
                P.add("act", lambda e: e.activation(out=vv[:R, i + 1, :], in_=pb[bkv][:R, 128:256], func=AF.Copy),
                      reads=[("pb", bkv)], writes=[("vv", i + 1)])
            if is_out_tile:
                P.add("dve", lambda e: e.tensor_tensor(
                    out=kdst.rearrange("p (h d) -> p h d", d=64),
                    in0=tmpF[:R, f_kn, 0:128].rearrange("p (h d) -> p h d", d=64),
                    in1=gkb[:R, :].unsqueeze(1).broadcast_to([R, 2, 64]), op=ALU.mult),
                    reads=[("tmpF", f_kn), ("gkb",)], writes=kdst_keys)
                P.add("dve", lambda e: e.tensor_tensor(
                    out=tmpB[:R, b_kn, 0:128].rearrange("p (h d) -> p h d", d=64),
                    in0=tmpF[:R, f_kn, 0:128].rearrange("p (h d) -> p h d", d=64),
                    in1=gkq[:R, :].unsqueeze(1).broadcast_to([R, 2, 64]), op=ALU.mult),
                    reads=[("tmpF", f_kn), ("gkq",)], writes=[("tmpB", b_kn)])
            elif is_smp:
                P.add("dve", lambda e: e.tensor_tensor(
                    out=kdst.rearrange("p (h d) -> p h d", d=64),
                    in0=tmpF[:R, f_kn, 0:128].rearrange("p (h d) -> p h d", d=64),
                    in1=gkb[:R, :].unsqueeze(1).broadcast_to([R, 2, 64]), op=ALU.mult),
                    reads=[("tmpF", f_kn), ("gkb",)], writes=kdst_keys)
                P.add("act", lambda e: e.activation(out=tmpB[:R, b_kn, 0:128], in_=kdst, func=AF.Copy),
                      reads=kdst_keys, writes=[("tmpB", b_kn)])
            elif False:
                P.add("dve", lambda e: e.tensor_tensor(
                    out=tmpB[:R, b_kn, 0:128].rearrange("p (h d) -> p h d", d=64),
                    in0=tmpF[:R, f_kn, 0:128].rearrange("p (h d) -> p h d", d=64),
                    in1=gkq[:R, :].unsqueeze(1).broadcast_to([R, 2, 64]), op=ALU.mult),
                    reads=[("tmpF", f_kn), ("gkq",)], writes=[("tmpB", b_kn)])
            if is_out_tile:
                P.add("act", lambda e: e.activation(out=tmpF[:R, f_kn, 256:384], in_=pb[bkv][:R, 128:256], func=AF.Copy),
                      reads=[("pb", bkv)], writes=[("tmpF", f_kn)])
                out_stores.append(P.dma("sp", kp_d, tmpF[:R, f_kn, 128:256], reads=[("tmpF", f_kn)], dsem="st_tf%d" % f_kn))
                out_stores.append(P.dma("sp", vp_d, tmpF[:R, f_kn, 256:384], reads=[("tmpF", f_kn)], dsem="st_tf%d" % f_kn))
            if is_smp:
                P.add("act", lambda e: e.activation(out=vf_s[:R, :], in_=pb[bkv][:R, 128:256], func=AF.Copy),
                      reads=[("pb", bkv)], writes=[("vf_s",)])
                P.add("act", lambda e: e.activation(out=v0b[:R, :], in_=pb[bkv][:R, 128:256], func=AF.Copy),
                      reads=[("pb", bkv)], writes=[("v0b",)])
                out_stores.append(P.dma("sp", ksn_d[:, 0:127, :], ck_d[:, 1:128, :], dsem="d2dk"))
                out_stores.append(P.dma("sp", vsn_d[:, 0:127, :], cv_d[:, 1:128, :], dsem="d2dv"))
                out_stores.append(P.dma("sp", ksn_d[:, 127, :], knf_s[:R, :], reads=[("knf_s",)], dsem="st_kn"))
                out_stores.append(P.dma("sp", vsn_d[:, 127, :], vf_s[:R, :], reads=[("vf_s",)], dsem="st_vn"))
            def part_b(i=i, R=R, c0=c0, qn_dst=qn_dst, qn_key=qn_key, b_kn=b_kn, is_smp=is_smp):
                t = ntp()
                tpv = tp[t].rearrange("p (k n) -> p k n", k=8)
                for j in range(4):
                    P.add("pe", lambda e, j=j: e.transpose(tpv[:, j, :R], qn_dst[:, 128 * j:128 * j + 128], ident[:R, :R]),
                          reads=[qn_key, ("ident",)], writes=[("tp", t)])
                P.add("pe", lambda e: e.transpose(tpv[:, 4, :R], tmpB[:R, b_kn, 0:128], ident[:R, :R]),
                      reads=[("tmpB", b_kn), ("ident",)], writes=[("tp", t)])
                P.add("act", lambda e: e.activation(out=qT[:, :, c0:c0 + R], in_=tpv[:, 0:4, :R], func=AF.Copy),
                      reads=[("tp", t)], writes=[("qT", i)])
                if not is_smp:
                    P.add("dve", lambda e: e.tensor_copy(out=kT[:, 128 + c0:128 + c0 + R], in_=tpv[:, 4, :R]),
                          reads=[("tp", t)], writes=[("kT", i + 1)])
            pend_b.append(part_b)
            while len(pend_b) > 2:
                pend_b.pop(0)()
            if is_out_tile or is_smp:
                bu0, bu1 = nbank(), nbank()
                for (bank, ch) in ((bu0, cb + 3), (bu1, cb + 4)):
                    for kc in range(8):
                        P.add("pe", lambda e, bank=bank, ch=ch, kc=kc: e.matmul(pb[bank][:R, 0:256], lhsT=hT[:, kc, c0:c0 + R],
                                                                              rhs=slab(ch, kc, 0, 256), start=(kc == 0), stop=(kc == 7)),
                              reads=[("hT", i), ("ring", slot_of(ch))], writes=[("pb", bank)])
                f_u = ntf()
                P.add("act", lambda e: e.activation(out=tmpF[:R, f_u, 0:256], in_=pb[bu0][:R, 0:256], func=AF.Copy),
                      reads=[("pb", bu0)], writes=[("tmpF", f_u)])
                P.add("act", lambda e: e.activation(out=tmpF[:R, f_u, 256:512], in_=pb[bu1][:R, 0:256], func=AF.Copy),
                      reads=[("pb", bu1)], writes=[("tmpF", f_u)])
                if is_out_tile:
                    out_stores.append(P.dma("sp", pp_d, tmpF[113:128, f_u, :], reads=[("tmpF", f_u)], dsem="st_tf%d" % f_u))
                else:
                    out_stores.append(P.dma("sp", psn_d[:, 14, :], tmpF[:R, f_u, :], reads=[("tmpF", f_u)], dsem="st_tf%d" % f_u))
        class _A1:
            def tile(self, i):
                if i in tile_by_idx:
                    tile_body(*tile_by_idx[i])

            def finish(self):
                while pend_b:
                    pend_b.pop(0)()
                emit_u(NT + 1)
                assert len(u_done) == len(subs)
                for c in range(cb, cb + 5):
                    release_chunk(c)

        return _A1()

    def pool_groups(p):
        first = (p == 0)
        has_smp = (p == SAMPLE_PASS and ENABLE_SAMPLE)
        groups = []
        ctx = {}

        def prologue():
            if not first:
                P.add("dve", lambda e: e.tensor_copy(out=uT[:, :, 0:16], in_=uhist[:]), reads=[("uhist",)], writes=[("uThist",)])
            if has_smp:
                f_s = [ntf(), ntf()]
                for cix in range(2):
                    P.dma("sp", tmpF[0:120, f_s[cix], :], st_d[8 * cix:8 * cix + 8].rearrange("n j c -> (n j) c"),
                          writes=[("tmpF", f_s[cix])], dsem="ld_tf%d" % f_s[cix])
                out_stores.append(P.dma("sp", psn_d[:, 0:14, :], st_d[:, 1:15, :], dsem="d2d"))
                bsm = 5
                ctx["bsm"] = bsm
                for g in range(4):
                    for cix in range(2):
                        P.add("pe", lambda e, g=g, cix=cix: e.matmul(pb[bsm][:, 16 * g + 8 * cix:16 * g + 8 * cix + 8],
                                                                   lhsT=tmpF[0:120, f_s[cix], 128 * g:128 * g + 128], rhs=sel[0:120, g, :],
                                                                   start=True, stop=True),
                              reads=[("tmpF", f_s[cix]), ("sel", g)], writes=[("pb", bsm)])
                P.add("act", lambda e: e.activation(out=ssum[:, :], in_=pb[bsm][:, 0:64], func=AF.Copy),
                      reads=[("pb", bsm)], writes=[("ssum",)])

        def one_group(hb, g):
            w = POOL_W[g]
            lo, hi = 512 * hb, 512 * hb + 512
            W = hi - lo
            k = g + 1
            src_is_u = True
            cur = None
            ckey = None
            for j in range(1, k + 1):
                ext = (1 << k) - (1 << j)
                sh = 1 << (j - 1)
                dst = poolA if (j % 2 == 1) else poolB
                dkey = ("poolA",) if (j % 2 == 1) else ("poolB",)
                a0 = lo - ext
                n = hi - a0
                if src_is_u:
                    in0 = uT[:, g, 16 + a0:16 + a0 + n]
                    in1 = uT[:, g, 16 + a0 - sh:16 + a0 - sh + n]
                    rk = ckeys("uT", max(a0 - sh, 0), n + sh, g) + [("uThist",)]
                else:
                    in0 = cur[:, 16 + a0 - lo:16 + a0 - lo + n]
                    in1 = cur[:, 16 + a0 - lo - sh:16 + a0 - lo - sh + n]
                    rk = [ckey]
                P.add("dve", lambda e, dst=dst, a0=a0, n=n, in0=in0, in1=in1: e.tensor_tensor(
                    out=dst[:, 16 + a0 - lo:16 + a0 - lo + n], in0=in0, in1=in1, op=ALU.add),
                    reads=rk, writes=[dkey])
                cur, ckey, src_is_u = dst, dkey, False
            P.add("dve", lambda e: e.scalar_tensor_tensor(
                out=pooledT[:, g, 0:W], in0=cur[:, 16:16 + W], scalar=1.0 / w, in1=uT[:, g, 16 + lo:16 + hi],
                op0=ALU.mult, op1=ALU.subtract),
                reads=[ckey] + ckeys("uT", lo, W, g), writes=[("pooledT", g)])
            if first and hb == 0:
                P.add("dve", lambda e: e.tensor_tensor(out=cur[:, 0:16], in0=cur[:, 16:32], in1=rc[:, g, :], op=ALU.mult),
                      reads=[ckey, ("rc",)], writes=[ckey])
                P.add("dve", lambda e: e.tensor_tensor(out=pooledT[:, g, 0:16], in0=cur[:, 0:16], in1=uT[:, g, 16:32], op=ALU.subtract),
                      reads=[ckey] + ckeys("uT", 0, 16, g), writes=[("pooledT", g)])
            if has_smp and hb == 1:
                P.add("dve", lambda e: e.scalar_tensor_tensor(
                    out=pooledT[:, g, 512:528], in0=uT[:, g, 16 + T:16 + T + NSMP], scalar=(1.0 / w - 1.0),
                    in1=ssum[:, 16 * g:16 * g + 16], op0=ALU.mult, op1=ALU.add),
                    reads=[("uT", g, NT), ("ssum",)], writes=[("pooledT", g)])

        def mix(hb):
            lo, hi = 512 * hb, 512 * hb + 512
            W = hi - lo
            if hb == 1 and p < NPASS - 1:
                P.add("dve", lambda e: e.tensor_copy(out=uhist[:], in_=uT[:, :, T:T + 16]),
                      reads=[("uT", g, NT - 1) for g in range(4)], writes=[("uhist",)])
            width = W + (NSMP if (has_smp and hb == 1) else 0)
            msubs = [(0, width)] if width <= 512 else [(0, width // 2), (width // 2, width - width // 2)]
            mbanks = [5] if hb == 0 else [0, 1, 4, 5]
            mctr = 0
            for g in range(4):
                for (m0, n) in msubs:
                    bank = mbanks[mctr % len(mbanks)]
                    mctr += 1
                    P.add("pe", lambda e, bank=bank, g=g, m0=m0, n=n: e.matmul(pb[bank][:, 0:n], lhsT=wmix[:, g, :],
                                                                             rhs=pooledT[:, g, m0:m0 + n], start=True, stop=True),
                          reads=[("pooledT", g), ("wmix",)], writes=[("pb", bank)])
                    P.add("act", lambda e, bank=bank, g=g, m0=m0, n=n: e.activation(
                        out=ypT[:, g, lo + m0:lo + m0 + n], in_=pb[bank][:, 0:n], func=AF.Identity, scale=psc[:, g:g + 1]),
                        reads=[("pb", bank), ("psc", g)], writes=ckeys("ypT", lo + m0, n, g))

        groups.append(prologue)
        for hb in range(2):
            for g in range(4):
                groups.append(lambda hb=hb, g=g: one_group(hb, g))
            groups.append(lambda hb=hb: mix(hb))
        return groups

    def stage_attn(p, fillers):
        first = (p == 0)
        bO, bD = 4, 5
        sctr = [0]

        def s_phase(i):
            c0 = 128 * i
            kbs = [1] if (first and i == 0) else [0, 1]
            pts = {}
            sdef = []
            for kvh in range(2):
                for kb in kbs:
                    blk = i + kb
                    bS = sctr[0] % 4
                    sctr[0] += 1
                    msk = maskC if kb == 1 else maskP
                    mkey = ("maskC",) if kb == 1 else ("maskP",)
                    P.add("pe", lambda e, bS=bS, kvh=kvh, blk=blk: e.matmul(
                        pb[bS][:, :], lhsT=kT[64 * kvh:64 * kvh + 64, 128 * blk:128 * blk + 128],
                        rhs=qT[64 * kvh:64 * kvh + 64, :, c0:c0 + 128], start=True, stop=False, tile_position=(64 * kvh, 0)),
                        reads=[("kT", blk), ("qT", i)], writes=[("pb", bS)])
                    sdef.append((bS, msk, mkey, kvh, kb, blk))
            for (bS, msk, mkey, kvh, kb, blk) in sdef:
                    P.add("pe", lambda e, bS=bS, msk=msk: e.matmul(
                        pb[bS][:, :], lhsT=ident[:, :], rhs=msk[:, :].unsqueeze(1).broadcast_to([128, 4, 128]),
                        start=False, stop=True),
                        reads=[("ident",), mkey], writes=[("pb", bS)])
            for (bS, msk, mkey, kvh, kb, blk) in sdef:
                    bp = 4 * (i % 2) + 2 * kvh + kb
                    P.add("act", lambda e, bS=bS, bp=bp: e.activation(out=tmpB[:, bp, :], in_=pb[bS][:, :], func=AF.Exp,
                                                                      bias=negM[:, 0:1], scale=0.125),
                          reads=[("pb", bS), ("negM",)], writes=[("tmpB", bp)])
                    pts[(kvh, kb)] = (bp, blk)
            return (kbs, pts)

        def pv_phase(i, kbs, pts, mid=None):
            c0 = 128 * i
            for kvh in range(2):
                for n_, kb in enumerate(kbs):
                    bp, blk = pts[(kvh, kb)]
                    P.add("pe", lambda e, kvh=kvh, bp=bp, n_=n_: e.matmul(
                        pb[bD][64 * kvh:64 * kvh + 64, :], lhsT=ones64[:, :], rhs=tmpB[:, bp, :],
                        start=(n_ == 0), stop=(n_ == len(kbs) - 1 and not SINK_VIA_PE), tile_position=(0, 64 * kvh),
                        skip_group_check=SINK_VIA_PE),
                        reads=[("ones64",), ("tmpB", bp)], writes=[("pb", bD)])
            f_d = ntf()
            if SINK_VIA_PE:
                for kvh in range(2):
                    P.add("pe", lambda e, kvh=kvh: e.matmul(
                        pb[bD][64 * kvh:64 * kvh + 64, :], lhsT=ones64[0:1, :],
                        rhs=E8b[0:1, 4 * kvh:4 * kvh + 4].unsqueeze(2).broadcast_to([1, 4, 128]),
                        start=False, stop=True, tile_position=(0, 64 * kvh), skip_group_check=True),
                        reads=[("ones64",), ("E8b",)], writes=[("pb", bD)])
            else:
                P.add("dve", lambda e: e.tensor_tensor(
                    out=tmpF[:, f_d, :].rearrange("p (j q) -> p j q", j=4), in0=pb[bD][:, :].rearrange("p (j q) -> p j q", j=4),
                    in1=E4[:, :].unsqueeze(2).broadcast_to([128, 4, 128]), op=ALU.add),
                    reads=[("pb", bD), ("E4",)], writes=[("tmpF", f_d)])
            for kvh in range(2):
                for n_, kb in enumerate(kbs):
                    bp, blk = pts[(kvh, kb)]
                    P.add("pe", lambda e, kvh=kvh, bp=bp, blk=blk, n_=n_: e.matmul(
                        pb[bO][64 * kvh:64 * kvh + 64, :], lhsT=vv[:, blk, 64 * kvh:64 * kvh + 64], rhs=tmpB[:, bp, :],
                        start=(n_ == 0), stop=(n_ == len(kbs) - 1), tile_position=(0, 64 * kvh)),
                        reads=[("vv", blk), ("tmpB", bp)], writes=[("pb", bO)])
            if SINK_VIA_PE:
                P.add("act", lambda e: e.activation(out=tmpF[:, f_d, :], in_=pb[bD][:, :], func=AF.Ln),
                      reads=[("pb", bD)], writes=[("tmpF", f_d)])
                P.add("act", lambda e: e.activation(out=tmpF[:, f_d, :], in_=tmpF[:, f_d, :], func=AF.Exp, scale=-1.0),
                      reads=[("tmpF", f_d)], writes=[("tmpF", f_d)])
            elif USE_ACT_RECIP:
                P.add("act", lambda e: e.activation(out=tmpF[:, f_d, :], in_=tmpF[:, f_d, :], func=AF.Ln),
                      reads=[("tmpF", f_d)], writes=[("tmpF", f_d)])
                P.add("act", lambda e: e.activation(out=tmpF[:, f_d, :], in_=tmpF[:, f_d, :], func=AF.Exp, scale=-1.0),
                      reads=[("tmpF", f_d)], writes=[("tmpF", f_d)])
            else:
                P.add("dve", lambda e: e.reciprocal(out=tmpF[:, f_d, :], in_=tmpF[:, f_d, :]),
                      reads=[("tmpF", f_d)], writes=[("tmpF", f_d)])
            if mid is not None:
                mid()
            P.add("dve", lambda e: e.tensor_tensor(
                out=yaT[:, :, c0:c0 + 128], in0=pb[bO][:, :].rearrange("p (j q) -> p j q", j=4),
                in1=tmpF[:, f_d, :].rearrange("p (j q) -> p j q", j=4), op=ALU.mult),
                reads=[("pb", bO), ("tmpF", f_d)], writes=[("yaT", i)])

        fillers = list(fillers)

        def fill(k):
            for _ in range(k):
                if fillers:
                    fillers.pop(0)()

        fill(1)
        nxt = s_phase(0)
        for i in range(NT):
            cur = nxt
            if i + 1 < NT:
                nxt = s_phase(i + 1)
            if i == 4:
                fill(1)
            pv_phase(i, *cur, mid=lambda: fill(1))
        if p < NPASS - 1:
            P.add("dve", lambda e: e.tensor_copy(out=kT[:, 0:128], in_=kT[:, T:T + 128]), reads=[("kT", NT)], writes=[("kT", 0)])
            P.add("dve", lambda e: e.tensor_copy(out=vv[:, 0, :], in_=vv[:, NT, :]), reads=[("vv", NT)], writes=[("vv", 0)])
        return fillers

    def stage_attn_sample(p):
        R = NSMP
        bO, bD, bS = 2, 3, 1
        f_pr = ntf()
        c_s0 = nstat(16)
        P.add("dve", lambda e: e.tensor_tensor(
            out=tmpF[:R, f_pr, :].rearrange("p (j k d) -> p j k d", j=4, k=2),
            in0=qn_s[:R, :].rearrange("p (j k d) -> p j k d", j=4, k=2),
            in1=knf_s[:R, :].rearrange("p (k d) -> p k d", k=2).unsqueeze(1).broadcast_to([R, 4, 2, 64]), op=ALU.mult),
            reads=[("qn_s",), ("knf_s",)], writes=[("tmpF", f_pr)])
        P.add("dve", lambda e: e.tensor_reduce(out=stat[:R, c_s0:c_s0 + 8], in_=tmpF[:R, f_pr, :].rearrange("p (h d) -> p h d", d=64),
                                               axis=AX.X, op=ALU.add),
              reads=[("tmpF", f_pr)], writes=[("sg", c_s0)])
        P.add("act", lambda e: e.activation(out=stat[:R, c_s0 + 8:c_s0 + 16], in_=stat[:R, c_s0:c_s0 + 8], func=AF.Exp,
                                            bias=negM[:R, 0:1], scale=0.125),
              reads=[("sg", c_s0), ("negM",)], writes=[("sg", c_s0)])
        b_p0 = 4
        P.add("dve", lambda e: e.tensor_tensor(
            out=tmpB[:R, b_p0, 0:8 * R].rearrange("p (k n j) -> p k n j", k=2, j=4),
            in0=stat[:R, c_s0 + 8:c_s0 + 16].rearrange("p (j k) -> p k j", k=2).unsqueeze(2).broadcast_to([R, 2, R, 4]),
            in1=ident[:R, :R].unsqueeze(1).unsqueeze(3).broadcast_to([R, 2, R, 4]), op=ALU.mult),
            reads=[("sg", c_s0), ("ident",)], writes=[("tmpB", b_p0)])
        for kvh in range(2):
            for (bank, lw) in ((bO, v0b[:R, 64 * kvh:64 * kvh + 64]), (bD, ones64[:R, :])):
                P.add("pe", lambda e, bank=bank, lw=lw, kvh=kvh: e.matmul(
                    pb[bank][0:64, 64 * kvh:64 * kvh + 64], lhsT=lw, rhs=tmpB[:R, b_p0, 64 * kvh:64 * kvh + 64],
                    start=(kvh == 0), stop=False, skip_group_check=True),
                    reads=[("v0b",), ("ones64",), ("tmpB", b_p0)], writes=[("pb", bank)])
        def st_S(n):
            bk = n % 4
            bSn = (0, 1, 4, 5)[n % 4]
            for kvh in range(2):
                P.add("pe", lambda e, kvh=kvh: e.matmul(
                    pb[bSn][:, 4 * kvh:4 * kvh + 4], lhsT=kwb[64 * kvh:64 * kvh + 64, n, :],
                    rhs=qT[64 * kvh:64 * kvh + 64, :, T + n], start=True, stop=True, tile_position=(64 * kvh, 0)),
                    reads=[("kwb", n // 8), ("qT", NT)], writes=[("pb", bSn)])
            P.add("act", lambda e: e.activation(out=tmpB[:, bk, 256:264], in_=pb[bSn][:, 0:8], func=AF.Exp,
                                                bias=negM[:, 0:1], scale=0.125),
                  reads=[("pb", bSn), ("negM",)], writes=[("tmpB", bk)])

        def st_PV(n):
            bk = n % 4
            for kvh in range(2):
                oc = 64 * kvh + 4 * n
                last = (n == R - 1 and kvh == 1)
                for (bank, lw) in ((bO, vwb[:, n, 64 * kvh:64 * kvh + 64]), (bD, ones64[:, :])):
                    P.add("pe", lambda e, bank=bank, lw=lw, oc=oc, last=last, kvh=kvh: e.matmul(
                        pb[bank][0:64, oc:oc + 4], lhsT=lw, rhs=tmpB[:, bk, 256 + 4 * kvh:260 + 4 * kvh], start=False, stop=last,
                        skip_group_check=True),
                        reads=[("vwb", n // 8), ("tmpB", bk), ("ones64",)], writes=[("pb", bank)])

        if SMP_MODE == "serial":
            for n in range(R):
                st_S(n)
                st_PV(n)
        else:
            lag = 2 if SMP_MODE == "pipe" else R
            for step in range(R + lag):
                if step < R:
                    st_S(step)
                if 0 <= step - lag < R:
                    st_PV(step - lag)
        if DBG_SATTN_LEVEL < 5:
            return
        f_d = ntf()
        P.add("dve", lambda e: e.tensor_tensor(
            out=tmpF[0:64, f_d, 0:8 * R].rearrange("p (k n j) -> p k n j", k=2, j=4),
            in0=pb[bD][0:64, 0:8 * R].rearrange("p (k n j) -> p k n j", k=2, j=4),
            in1=E8[0:64, :].rearrange("p (k j) -> p k j", k=2).unsqueeze(2).broadcast_to([64, 2, R, 4]), op=ALU.add),
            reads=[("pb", bD), ("E8",)], writes=[("tmpF", f_d)])
        P.add("dve", lambda e: e.reciprocal(out=tmpF[0:64, f_d, 0:8 * R], in_=tmpF[0:64, f_d, 0:8 * R]),
              reads=[("tmpF", f_d)], writes=[("tmpF", f_d)])
        b_o = 5
        P.add("dve", lambda e: e.tensor_tensor(
            out=tmpB[0:64, b_o, 0:8 * R].rearrange("p (k j n) -> p k n j", k=2, j=4),
            in0=pb[bO][0:64, 0:8 * R].rearrange("p (k n j) -> p k n j", k=2, j=4),
            in1=tmpF[0:64, f_d, 0:8 * R].rearrange("p (k n j) -> p k n j", k=2, j=4), op=ALU.mult),
            reads=[("pb", bO), ("tmpF", f_d)], writes=[("tmpB", b_o)])
        src = tmpB[0:64, b_o, 0:8 * R].rearrange("p (k j n) -> p k j n", k=2, j=4)
        for kvh in range(2):
            P.dma("sp", yaT[64 * kvh:64 * kvh + 64, :, T:T + R], src[:, kvh, :, :], reads=[("tmpB", b_o)], writes=[("yaT", NT)],
                  dsem="yas")

    def stage_merge(p, cb):
        subs = pass_subs(p)
        for m in range(4):
            cga, cgb = cb + 5 + 2 * m, cb + 6 + 2 * m
            for cc in range(2):
                c = 2 * m + cc
                for (s0, n) in subs:
                    bP, bA, bGA, bGB = nbank(), nbank(), nbank(), nbank()
                    for kc in range(4):
                        P.add("pe", lambda e, bP=bP, kc=kc, s0=s0, n=n, c=c: e.matmul(
                            pb[bP][:, 0:n], lhsT=wp[:, kc, 128 * c:128 * c + 128], rhs=ypT[:, kc, s0:s0 + n],
                            start=(kc == 0), stop=(kc == 3)),
                            reads=ckeys("ypT", s0, n, kc) + [("wp",)], writes=[("pb", bP)])
                    for kc in range(4):
                        P.add("pe", lambda e, bA=bA, kc=kc, s0=s0, n=n, c=c: e.matmul(
                            pb[bA][:, 0:n], lhsT=wa[:, kc, 128 * c:128 * c + 128], rhs=yaT[:, kc, s0:s0 + n],
                            start=(kc == 0), stop=(kc == 3)),
                            reads=ckeys("yaT", s0, n) + [("wa",)], writes=[("pb", bA)])
                    for (bank, ch) in ((bGA, cga), (bGB, cgb)):
                        for kc in range(8):
                            P.add("pe", lambda e, bank=bank, ch=ch, kc=kc, s0=s0, n=n, cc=cc: e.matmul(
                                pb[bank][:, 0:n], lhsT=slab(ch, kc, 128 * cc, 128), rhs=hT[:, kc, s0:s0 + n],
                                start=(kc == 0), stop=(kc == 7)),
                                reads=ckeys("hT", s0, n) + [("ring", slot_of(ch))], writes=[("pb", bank)])
                    fa, fb_, f1 = ntf(), ntf(), ntf()
                    P.add("act", lambda e, bGA=bGA, fa=fa, n=n: e.activation(out=tmpF[:, fa, 0:n], in_=pb[bGA][:, 0:n], func=AF.Sigmoid),
                          reads=[("pb", bGA)], writes=[("tmpF", fa)])
                    P.add("dve", lambda e, bP=bP, fa=fa, f1=f1, n=n: e.tensor_tensor(out=tmpF[:, f1, 0:n], in0=pb[bP][:, 0:n],
                                                                                    in1=tmpF[:, fa, 0:n], op=ALU.mult),
                          reads=[("pb", bP), ("tmpF", fa)], writes=[("tmpF", f1)])
                    P.add("act", lambda e, bGB=bGB, fb_=fb_, n=n: e.activation(out=tmpF[:, fb_, 0:n], in_=pb[bGB][:, 0:n], func=AF.Sigmoid),
                          reads=[("pb", bGB)], writes=[("tmpF", fb_)])
                    P.add("dve", lambda e, bA=bA, fb_=fb_, n=n: e.tensor_tensor(out=tmpF[:, fb_, 0:n], in0=pb[bA][:, 0:n],
                                                                               in1=tmpF[:, fb_, 0:n], op=ALU.mult),
                          reads=[("pb", bA), ("tmpF", fb_)], writes=[("tmpF", fb_)])
                    P.add("dve", lambda e, fb_=fb_, f1=f1, s0=s0, n=n, c=c: e.tensor_tensor(out=mT[:, c, s0:s0 + n], in0=tmpF[:, f1, 0:n],
                                                                                           in1=tmpF[:, fb_, 0:n], op=ALU.add),
                          reads=[("tmpF", f1), ("tmpF", fb_)], writes=ckeys("mT", s0, n, c))
            release_chunk(cga)
            release_chunk(cgb)

    def stage_wout_norm2(p):
        tiles = pass_tiles(p)
        load_gain(n2_d)
        pend = None
        for (i, R, c0) in tiles:
            for hf in range(2):
                bank = nbank()
                for kc in range(8):
                    P.add("pe", lambda e, bank=bank, kc=kc, hf=hf, i=i, R=R, c0=c0: e.matmul(
                        pb[bank][:R, :], lhsT=mT[:, kc, c0:c0 + R], rhs=wo[:, kc, 512 * hf:512 * hf + 512],
                        start=(kc == 0), stop=(kc == 7)),
                        reads=[("mT", kc, i), ("wo", kc // 2)], writes=[("pb", bank)])
                P.add("dve", lambda e, bank=bank, hf=hf, i=i, R=R: e.tensor_tensor(
                    out=xres[:R, i, 512 * hf:512 * hf + 512], in0=pb[bank][:R, :], in1=xres[:R, i, 512 * hf:512 * hf + 512], op=ALU.add),
                    reads=[("pb", bank), ("xres", i, hf)], writes=[("xres", i, hf)])
            b = norm_pre(p, i, R)
            if pend is not None:
                norm_post(p, *pend)
            pend = (i, R, c0, b)
        norm_post(p, *pend)

    def stage_ffn(p, cb):
        tiles = pass_tiles(p)
        subs = pass_subs(p)
        c = cb + 13
        ngrp = len(FFN_GROUPS)
        for gi, grp in enumerate(FFN_GROUPS):
            gchunks = []
            for si, s in enumerate(grp):
                cg, cu = c, c + 1
                c += 2
                for fcl in range(2):
                    fc = 2 * si + fcl
                    for (s0, n) in subs:
                        bG, bU = nbank(), nbank()
                        for (bank, ch) in ((bG, cg), (bU, cu)):
                            for kc in range(8):
                                P.add("pe", lambda e, bank=bank, ch=ch, kc=kc, s0=s0, n=n, fcl=fcl: e.matmul(
                                    pb[bank][:, 0:n], lhsT=slab(ch, kc, 128 * fcl, 128), rhs=hT[:, kc, s0:s0 + n],
                                    start=(kc == 0), stop=(kc == 7)),
                                    reads=ckeys("hT", s0, n) + [("ring", slot_of(ch))], writes=[("pb", bank)])
                        fg = ntf()
                        P.add("act", lambda e, bG=bG, fg=fg, n=n: e.activation(out=tmpF[:, fg, 0:n], in_=pb[bG][:, 0:n], func=AF.Silu),
                              reads=[("pb", bG)], writes=[("tmpF", fg)])
                        P.add("dve", lambda e, bU=bU, fg=fg, fc=fc, s0=s0, n=n: e.tensor_tensor(
                            out=actT[:, fc, s0:s0 + n], in0=pb[bU][:, 0:n], in1=tmpF[:, fg, 0:n], op=ALU.mult),
                            reads=[("pb", bU), ("tmpF", fg)], writes=ckeys("actT", s0, n, fc))
                release_chunk(cg)
                release_chunk(cu)
            dch = list(range(c, c + len(grp)))
            c += len(grp)
            nfc = 2 * len(grp)
            for (i, R, c0) in tiles:
                for hf in range(2):
                    bank = nbank()
                    for fc in range(nfc):
                        ch = dch[fc // 2]
                        P.add("pe", lambda e, bank=bank, fc=fc, ch=ch, hf=hf, R=R, c0=c0: e.matmul(
                            pb[bank][:R, :], lhsT=actT[:, fc, c0:c0 + R], rhs=dchunk(ch, fc % 2, hf),
                            start=(fc == 0), stop=(fc == nfc - 1)),
                            reads=[("actT", fc, i), ("ring", slot_of(ch))], writes=[("pb", bank)])
                    P.add("dve", lambda e, bank=bank, hf=hf, i=i, R=R: e.tensor_tensor(
                        out=xres[:R, i, 512 * hf:512 * hf + 512], in0=pb[bank][:R, :], in1=xres[:R, i, 512 * hf:512 * hf + 512], op=ALU.add),
                        reads=[("pb", bank), ("xres", i, hf)], writes=[("xres", i, hf)])
                if gi == ngrp - 1:
                    if i < NT:
                        dst = y_d[p * T + 128 * i:p * T + 128 * i + 128, :]
                    else:
                        dst = ys_d
                    out_stores.append(P.dma("sp", dst, xres[:R, i, :], reads=[("xres", i, 0), ("xres", i, 1)], dsem="ys%d" % i))
                    if p + 1 < NPASS and i < NT:
                        if i == 0:
                            load_gain(n1_d)
                        P.dma("sp", xres[:128, i, :], x_d[(p + 1) * T + 128 * i:(p + 1) * T + 128 * i + 128, :],
                              writes=[("xres", i, 0), ("xres", i, 1)], dsem="xl%d" % i)
                        prefetched_x.add((p + 1, i))
            for ch in dch:
                release_chunk(ch)

    all_r1 = [("wp",), ("wa",)]
    all_act = [("actT", fc, i) for fc in range(8) for i in range(NT + 1)]
    all_uT = [("uT", g, i) for g in range(4) for i in range(NT + 1)] + [("uThist",)]
    all_mT = [("mT", c, i) for c in range(8) for i in range(NT + 1)]
    for p in range(NPASS):
        cb = p * CPP
        fence("dve", all_mT, all_uT)
        stage_A0(p, cb)
        fence("dve", all_act, all_r1)
        for k in range(2):
            P.dma("pool", r1[:, 2 * k:2 * k + 2, 0:D], wp_d[256 * k:256 * k + 256, :].rearrange("(kc p) n -> p kc n", p=128),
                  reads=[], writes=[("wp",)], dsem="wpl")
        war = wa_d.rearrange("(k j d) n -> k d j n", k=2, j=4)
        for kvh in range(2):
            P.dma("pool", r1[64 * kvh:64 * kvh + 64, 4:8, 0:D], war[kvh], writes=[("wa",)], dsem="wal")
        rest = stage_attn(p, pool_groups(p))
        if p == SAMPLE_PASS and ENABLE_SAMPLE and not DBG_SKIP_SATTN:
            stage_attn_sample(p)
        for f_ in rest:
            f_()
        fence("dve", all_uT, all_mT)
        stage_merge(p, cb)
        stage_wout_norm2(p)
        fence("dve", all_r1, all_act)
        stage_ffn(p, cb)

    P.add("sp", None, extra=out_stores)
    P.emit()
    return nc


_CACHE = {}


def _get_nc():
    if "nc" not in _CACHE:
        _CACHE["nc"] = build_program()
    return _CACHE["nc"]


def kernel(x_prompt, x_sample, cache_k, cache_v, state_pool, norm1, w_in, q_norm, k_norm, sinks,
           pool_mix_w, pool_scale, w_pool_proj, w_attn_proj, w_out, norm2, w_gate, w_up, w_down):
    f = lambda a: np.ascontiguousarray(np.asarray(a, dtype=np.float32))
    x_prompt, x_sample, cache_k, cache_v, state_pool = map(f, (x_prompt, x_sample, cache_k, cache_v, state_pool))
    shared = {
        "norm1": f(norm1).reshape(1, D), "w_in": f(w_in).reshape(D, INW), "q_norm": f(q_norm).reshape(1, 64),
        "k_norm": f(k_norm).reshape(1, 64), "sinks": f(sinks).reshape(1, 8), "pool_mix_w": f(pool_mix_w).reshape(4, 128, 128),
        "pool_scale": f(pool_scale).reshape(512), "w_pool_proj": f(w_pool_proj).reshape(512, D),
        "w_attn_proj": f(w_attn_proj).reshape(512, D), "w_out": f(w_out).reshape(D, D), "norm2": f(norm2).reshape(1, D),
        "w_gate": f(w_gate).reshape(D, DFF), "w_up": f(w_up).reshape(D, DFF), "w_down": f(w_down).reshape(DFF, D),
    }
    in_maps = []
    for c in range(NCORES):
        m = dict(shared)
        m["x"] = x_prompt[c]
        m["xs"] = x_sample[NSMP * c:NSMP * c + NSMP, 0, :]
        m["ck"] = cache_k[0, NSMP * c:NSMP * c + NSMP].reshape(NSMP, 128, 128)
        m["cv"] = cache_v[0, NSMP * c:NSMP * c + NSMP].reshape(NSMP, 128, 128)
        m["ckT"] = m["ck"].transpose(0, 2, 1)
        m["st"] = state_pool[0, NSMP * c:NSMP * c + NSMP]
        in_maps.append({k: np.ascontiguousarray(v) for k, v in m.items()})
    nc = _get_nc()
    res = run_bass_kernel_spmd(nc, in_maps, core_ids=list(range(NCORES)))
    r = res.results
    y = np.stack([r[c]["y"] for c in range(NCORES)], 0).astype(np.float32)
    ys = np.concatenate([r[c]["ys"] for c in range(NCORES)], 0).reshape(NCORES * NSMP, 1, D).astype(np.float32)
    kp = np.stack([r[c]["kp"].reshape(128, 2, 64) for c in range(NCORES)], 0)[None].astype(np.float32)
    vp = np.stack([r[c]["vp"].reshape(128, 2, 64) for c in range(NCORES)], 0)[None].astype(np.float32)
    pp = np.stack([r[c]["pp"] for c in range(NCORES)], 0)[None].astype(np.float32)
    ksn = np.concatenate([r[c]["ksn"].reshape(NSMP, 128, 2, 64) for c in range(NCORES)], 0)[None].astype(np.float32)
    vsn = np.concatenate([r[c]["vsn"].reshape(NSMP, 128, 2, 64) for c in range(NCORES)], 0)[None].astype(np.float32)
    psn = np.concatenate([r[c]["psn"] for c in range(NCORES)], 0)[None].astype(np.float32)
    return (y, ys, kp, vp, pp, ksn, vsn, psn)
```

```python
import numpy as np
import concourse.bass as bass
import concourse.mybir as mybir
from concourse.bass_utils import run_bass_kernel_spmd

F32 = mybir.dt.float32
BF16 = mybir.dt.bfloat16
AF = mybir.ActivationFunctionType
ALU = mybir.AluOpType
AX = mybir.AxisListType

NCORES = 8
D = 1024
SEQ = 2048
NSMP = 16
T = 1024
NPASS = SEQ // T
NT = T // 128
DFF = 2816
INW = 3328
EPS = 1e-6
SAMPLE_PASS = 0
NSLOT = 7
SLOT = 2048
POOL_W = (2, 4, 8, 16)
NCOLMAX = T + NSMP
FFN_GROUPS = ((0, 1, 2, 3), (4, 5, 6, 7), (8, 9, 10))
ENABLE_SAMPLE = True
SINK_VIA_PE = True
SMP_MODE = "serial"
USE_ACT_RECIP = True
MASK_NEG = -30000.0
DBG_SKIP_SATTN = False
DBG_SATTN_LEVEL = 9


class _Op:
    __slots__ = ("eng", "fn", "reads", "writes", "dsem", "idx", "deps", "signal", "token", "extra")


class _Recorder:
    def __init__(self):
        self.call = None

    def __getattr__(self, name):
        def f(*args, **kwargs):
            assert self.call is None
            self.call = (name, args, kwargs)
            return None
        return f


class Prog:
    ENGS = ("pe", "act", "dve", "pool", "sp")

    def __init__(self, nc):
        self.nc = nc
        self.ops = []
        self.sems = {}
        self.dcount = {}

    def sem(self, name):
        if name not in self.sems:
            self.sems[name] = self.nc.alloc_semaphore(name)
        return self.sems[name]

    def add(self, eng, fn, reads=(), writes=(), dsem=None, extra=()):
        if fn is not None:
            rec = _Recorder()
            fn(rec)
            assert rec.call is not None
            name, args, kwargs = rec.call
            fn = lambda e, name=name, args=args, kwargs=kwargs: getattr(e, name)(*args, **kwargs)
        op = _Op()
        op.eng, op.fn, op.reads, op.writes, op.dsem = eng, fn, tuple(reads), tuple(writes), dsem
        op.extra = tuple(extra)
        op.idx = len(self.ops)
        op.signal = dsem is not None
        op.token = None
        self.ops.append(op)
        return op

    def dma(self, eng, out, in_, reads=(), writes=(), dsem=None, extra=()):
        return self.add(eng, lambda e, o=out, i=in_: e.dma_start(out=o, in_=i), reads, writes, dsem=dsem, extra=extra)

    def analyze(self):
        last_w = {}
        readers = {}
        for op in self.ops:
            deps = set(o.idx for o in op.extra)
            for k in op.reads:
                if k in last_w:
                    deps.add(last_w[k])
            for k in op.writes:
                if k in last_w:
                    deps.add(last_w[k])
                deps.update(readers.get(k, ()))
            deps.discard(op.idx)
            if op.eng == "pe":
                deps = set(d for d in deps if self.ops[d].eng != "pe" or self.ops[d].dsem is not None)
            latest = {}
            pruned = set()
            for d in deps:
                od = self.ops[d]
                if od.dsem is not None:
                    pruned.add(d)
                elif latest.get(od.eng, -1) < d:
                    latest[od.eng] = d
            pruned.update(latest.values())
            deps = pruned
            op.deps = deps
            for d in deps:
                self.ops[d].signal = True
            for k in op.reads:
                readers.setdefault(k, []).append(op.idx)
            for k in op.writes:
                last_w[k] = op.idx
                readers[k] = []
        cnt = {e: 0 for e in self.ENGS}
        for op in self.ops:
            if op.dsem is not None:
                self.dcount[op.dsem] = self.dcount.get(op.dsem, 0) + 16
                op.token = (op.dsem, self.dcount[op.dsem])
            elif op.signal:
                cnt[op.eng] += 1
                op.token = ("eng_" + op.eng, cnt[op.eng])

    def emit_engine(self, eng, e):
        waited = {}
        for op in self.ops:
            if op.eng != eng:
                continue
            need = {}
            for d in op.deps:
                s, v = self.ops[d].token
                if need.get(s, 0) < v:
                    need[s] = v
            for s, v in need.items():
                if waited.get(s, 0) < v:
                    e.wait_ge(self.sem(s), v)
                    waited[s] = v
            if op.fn is None:
                continue
            ins = op.fn(e)
            if op.token is not None:
                ins.then_inc(self.sem(op.token[0]), 16 if op.dsem is not None else 1)

    def emit(self):
        self.analyze()
        for op in self.ops:
            if op.token is not None:
                self.sem(op.token[0])
        nc = self.nc
        with nc.Block() as block:
            @block.tensor
            def _(e):
                self.emit_engine("pe", e)

            @block.scalar
            def _(e):
                self.emit_engine("act", e)

            @block.vector
            def _(e):
                self.emit_engine("dve", e)

            @block.gpsimd
            def _(e):
                self.emit_engine("pool", e)

            @block.sync
            def _(e):
                self.emit_engine("sp", e)


def _tile_cols(i):
    return (128 * i, 128) if i < NT else (T, NSMP)


def ckeys(name, c0, n, *pre):
    keys = []
    for i in range(NT + 1):
        lo, w = _tile_cols(i)
        if lo < c0 + n and c0 < lo + w:
            keys.append((name,) + pre + (i,))
    return keys


def build_program():
    nc = bass.Bass("TRN2", target_bir_lowering=False)
    P = Prog(nc)

    def din(name, shape):
        return nc.dram_tensor(name, list(shape), F32, kind="ExternalInput").ap()

    def dout(name, shape):
        return nc.dram_tensor(name, list(shape), F32, kind="ExternalOutput").ap()

    x_d = din("x", [SEQ, D])
    xs_d = din("xs", [NSMP, D])
    ck_d = din("ck", [NSMP, 128, 128])
    cv_d = din("cv", [NSMP, 128, 128])
    ckT_d = din("ckT", [NSMP, 128, 128])
    st_d = din("st", [NSMP, 15, 512])
    n1_d = din("norm1", [1, D])
    win_d = din("w_in", [D, INW])
    qn_d = din("q_norm", [1, 64])
    kn_d = din("k_norm", [1, 64])
    sk_d = din("sinks", [1, 8])
    pm_d = din("pool_mix_w", [4, 128, 128])
    psc_d = din("pool_scale", [512])
    wp_d = din("w_pool_proj", [512, D])
    wa_d = din("w_attn_proj", [512, D])
    wo_d = din("w_out", [D, D])
    n2_d = din("norm2", [1, D])
    wg_d = din("w_gate", [D, DFF])
    wu_d = din("w_up", [D, DFF])
    wd_d = din("w_down", [DFF, D])

    y_d = dout("y", [SEQ, D])
    ys_d = dout("ys", [NSMP, D])
    kp_d = dout("kp", [128, 128])
    vp_d = dout("vp", [128, 128])
    pp_d = dout("pp", [15, 512])
    ksn_d = dout("ksn", [NSMP, 128, 128])
    vsn_d = dout("vsn", [NSMP, 128, 128])
    psn_d = dout("psn", [NSMP, 15, 512])

    def sb(name, shape, dt):
        return nc.alloc_sbuf_tensor(name, list(shape), dt)

    xres = sb("xres", [128, NT + 1, D], F32)
    hT = sb("hT", [128, 8, NCOLMAX], BF16)
    gb = sb("gb", [128, D], F32)
    ident = sb("ident", [128, 128], BF16)
    maskP = sb("maskP", [128, 128], BF16)
    maskC = sb("maskC", [128, 128], BF16)
    ones64 = sb("ones64", [128, 64], BF16)
    ring = sb("ring", [128, NSLOT * 8, 256], BF16)
    wo = sb("wo", [128, 8, D], BF16)
    wmix = sb("wmix", [128, 4, 128], BF16)
    r1 = sb("r1", [128, 8, NCOLMAX], BF16)
    hn = sb("hn", [128, 2, D], BF16)
    UW = 16 + NCOLMAX
    um = sb("um", [128, 4 * UW], F32)
    uT = um[:, :].rearrange("p (g n) -> p g n", g=4)
    mT = um[:, :].bitcast(BF16).rearrange("p (c n) -> p c n", c=8)
    qT = sb("qT", [128, 4, NCOLMAX], BF16)
    kT = sb("kT", [128, 128 + T], BF16)
    vv = sb("vv", [128, NT + 1, 128], BF16)
    poolA = sb("poolA", [128, 16 + 512], F32)
    poolB = sb("poolB", [128, 16 + 512], F32)
    pooledT = sb("pooledT", [128, 4, 528], BF16)
    junk = poolA[:, 0:512].bitcast(BF16)
    kwb = sb("kwb", [128, NSMP, 128], BF16)
    vwb = sb("vwb", [128, NSMP, 128], BF16)
    ypT = sb("ypT", [128, 4, NCOLMAX], BF16)
    yaT = sb("yaT", [128, 4, NCOLMAX], BF16)
    NTF, NTB = 5, 8
    tmpF = sb("tmpF", [128, NTF, 512], F32)
    tmpB = sb("tmpB", [128, NTB, 512], BF16)
    stat = sb("stat", [128, 128], F32)
    acc = sb("acc", [128, 64], F32)
    uhist = sb("uhist", [128, 4, 16], F32)
    rc = sb("rc", [128, 4, 16], F32)
    psc = sb("psc", [128, 4], F32)
    gqb = sb("gqb", [128, 64], F32)
    gkb = sb("gkb", [128, 64], F32)
    skb = sb("skb", [128, 8], F32)
    sk4 = sb("sk4", [128, 4], F32)
    E4 = sb("E4", [128, 4], F32)
    E8 = sb("E8", [128, 8], F32)
    E8b = sb("E8b", [128, 8], BF16)
    negM = sb("negM", [128, 1], F32)
    mtmp = sb("mtmp", [128, 8], F32)
    sel = sb("sel", [128, 4, 8], F32)
    v0b = sb("v0b", [NSMP, 128], BF16)
    qn_s = sb("qn_s", [NSMP, 512], BF16)
    gkq = sb("gkq", [128, 64], F32)
    ssum = sb("ssum", [128, 64], F32)
    knf_s = sb("knf_s", [NSMP, 128], F32)
    vf_s = sb("vf_s", [NSMP, 128], F32)

    wp = r1[:, 0:4, 0:D]
    wa = r1
    wa = r1[:, 4:8, 0:D]
    actT = r1

    NB = 6
    pb = [nc.alloc_psum_tensor("pb%d" % i, [128, 512], F32) for i in range(NB)]
    tpf = [nc.alloc_psum_tensor("tp%d" % i, [128, 512], F32) for i in range(2)]
    tp = [t[:].bitcast(BF16) for t in tpf]

    state = {"bank": 0, "tf": 0, "tb": 0, "stat": 0, "tp": 0, "acc": 0}

    def nbank(exclude=()):
        while True:
            b = state["bank"]
            state["bank"] = (b + 1) % NB
            if b not in exclude:
                return b

    def ntf():
        b = state["tf"]
        state["tf"] = (b + 1) % NTF
        return b

    def ntb():
        b = state["tb"]
        state["tb"] = (b + 1) % NTB
        return b

    def nstat(n=1):
        assert n <= 32
        b = state["stat"]
        state["stat"] = (b + 1) % 4
        return 32 * b

    def nacc():
        b = state["acc"]
        state["acc"] = b + 1
        assert state["acc"] <= 64
        return b

    def ntp():
        b = state["tp"]
        state["tp"] = 1 - b
        return b

    def fence(eng, reads, writes):
        if eng == "dve":
            P.add("dve", lambda e: e.engine_nop(), reads, writes)
        else:
            raise ValueError(eng)

    chunks = []

    def win_slab(c0):
        return win_d[:, c0:c0 + 256].rearrange("(kc p) n -> p kc n", p=128)

    for p in range(NPASS):
        for c0 in (512, 768, 1024, 0, 256):
            chunks.append(win_slab(c0))
        for m in range(4):
            chunks.append(win_slab(1280 + 256 * m))
            chunks.append(win_slab(2304 + 256 * m))
        for grp in FFN_GROUPS:
            for s in grp:
                chunks.append(wg_d[:, 256 * s:256 * s + 256].rearrange("(kc p) n -> p kc n", p=128))
                chunks.append(wu_d[:, 256 * s:256 * s + 256].rearrange("(kc p) n -> p kc n", p=128))
            for s in grp:
                chunks.append(wd_d[256 * s:256 * s + 256, :].rearrange("(fc p) (a n) -> p fc a n", p=128, n=256))
    CPP = len(chunks) // NPASS
    stream = {"issued": 0}

    def slot_of(c):
        return c % NSLOT

    def issue_chunk():
        c = stream["issued"]
        if c >= len(chunks):
            return
        s = slot_of(c)
        dst = ring[:, s * 8:(s + 1) * 8, :]
        if len(chunks[c].shape) == 4:
            dst = dst.rearrange("p (fc a) n -> p fc a n", fc=2)
        P.dma("pool", dst, chunks[c], writes=[("ring", s)], dsem="ring%d" % s)
        stream["issued"] = c + 1

    def release_chunk(c):
        assert stream["issued"] == c + NSLOT or stream["issued"] == len(chunks), (stream["issued"], c)
        issue_chunk()

    def slab(c, kc, n0, n):
        s = slot_of(c)
        return ring[:, s * 8 + kc, n0:n0 + n]

    def dchunk(c, fc, hf):
        s = slot_of(c)
        r0 = s * 8 + fc * 4 + hf * 2
        return ring[:, r0:r0 + 2, :]

    P.add("pool", lambda e: e.memset(ident[:], 1.0), writes=[("ident",)])
    P.add("pool", lambda e: e.affine_select(out=ident[:], in_=ident[:], pattern=[[-1, 128]], compare_op=ALU.is_equal,
                                            fill=0.0, base=0, channel_multiplier=1), reads=[("ident",)], writes=[("ident",)])
    for c in range(NSLOT):
        issue_chunk()
    if ENABLE_SAMPLE:
        for h in range(2):
            P.dma("pool", kwb[:, 8 * h:8 * h + 8, :], ckT_d[8 * h:8 * h + 8, :, :].rearrange("n r c -> r n c"),
                  writes=[("kwb", h)], dsem="kwl%d" % h)
            P.dma("pool", vwb[:, 8 * h:8 * h + 8, :], cv_d[8 * h:8 * h + 8, :, :].rearrange("n r c -> r n c"),
                  writes=[("vwb", h)], dsem="vwl%d" % h)
    P.dma("pool", wmix[:], pm_d.rearrange("g c d -> c g d"), writes=[("wmix",)], dsem="wsmall")
    for k in range(4):
        P.dma("pool", wo[:, 2 * k:2 * k + 2, :], wo_d[256 * k:256 * k + 256, :].rearrange("(kc p) n -> p kc n", p=128),
              writes=[("wo", k)], dsem="wo%d" % k)
    P.add("pool", lambda e: e.memset(maskP[:], 0.0), writes=[("maskP",)])
    P.add("pool", lambda e: e.affine_select(out=maskP[:], in_=maskP[:], pattern=[[-1, 128]], compare_op=ALU.is_ge,
                                            fill=MASK_NEG, base=0, channel_multiplier=1), reads=[("maskP",)], writes=[("maskP",)])
    P.add("pool", lambda e: e.memset(maskC[:], 0.0), writes=[("maskC",)])
    P.add("pool", lambda e: e.affine_select(out=maskC[:], in_=maskC[:], pattern=[[1, 128]], compare_op=ALU.is_ge,
                                            fill=MASK_NEG, base=0, channel_multiplier=-1), reads=[("maskC",)], writes=[("maskC",)])
    P.add("pool", lambda e: e.memset(ones64[:], 1.0), writes=[("ones64",)])
    for g, w in enumerate(POOL_W):
        P.add("pool", lambda e, g=g, w=w: e.memset(sel[:, g, :], 1.0 / w), writes=[("sel", g)])
        P.add("pool", lambda e, g=g, w=w: e.affine_select(out=sel[:, g, :], in_=sel[:, g, :], pattern=[[-15, 8]],
                                                          compare_op=ALU.is_ge, fill=0.0, base=-(16 - w),
                                                          channel_multiplier=1), reads=[("sel", g)], writes=[("sel", g)])
        P.add("pool", lambda e, g=g, w=w: e.affine_select(out=sel[:, g, :], in_=sel[:, g, :], pattern=[[15, 8]],
                                                          compare_op=ALU.is_ge, fill=0.0, base=14,
                                                          channel_multiplier=-1), reads=[("sel", g)], writes=[("sel", g)])

    P.dma("sp", gqb[:], qn_d.broadcast_to([128, 64]), writes=[("gqb",)], dsem="sm_gq")
    P.dma("sp", gkb[:], kn_d.broadcast_to([128, 64]), writes=[("gkb",)], dsem="sm_gk")
    P.dma("sp", skb[:], sk_d.broadcast_to([128, 8]), writes=[("skb",)], dsem="sm_sk")
    P.dma("sp", sk4[0:64, :], sk_d[:, 0:4].broadcast_to([64, 4]), writes=[("sk4",)], dsem="sm_sk4")
    P.dma("sp", sk4[64:128, :], sk_d[:, 4:8].broadcast_to([64, 4]), writes=[("sk4",)], dsem="sm_sk4")
    for g in range(4):
        P.dma("sp", psc[:, g:g + 1], psc_d[128 * g:128 * g + 128].rearrange("(c o) -> c o", o=1), writes=[("psc", g)],
              dsem="sm_psc%d" % g)

    P.add("dve", lambda e: e.memset(acc[:], 0.0), writes=[("acc",)])
    P.add("dve", lambda e: e.tensor_tensor(out=gkq[:], in0=gkb[:], in1=gqb[:], op=ALU.mult),
          reads=[("gkb",), ("gqb",)], writes=[("gkq",)])
    P.add("dve", lambda e: e.memset(uT[:, :, 0:16], 0.0), writes=[("uThist",)])
    for g, w in enumerate(POOL_W):
        P.add("dve", lambda e, g=g, w=w: e.memset(rc[:, g, :], 1.0 / w), writes=[("rc",)])
        for pos in range(w - 1):
            P.add("dve", lambda e, g=g, pos=pos: e.memset(rc[:, g, pos:pos + 1], 1.0 / (pos + 1)), writes=[("rc",)])

    P.add("dve", lambda e: e.tensor_reduce(out=mtmp[:, 0:1], in_=gqb[:], axis=AX.X, op=ALU.max, apply_absolute_value=True),
          reads=[("gqb",)], writes=[("mtmp", 0)])
    P.add("dve", lambda e: e.tensor_reduce(out=mtmp[:, 1:2], in_=gkb[:], axis=AX.X, op=ALU.max, apply_absolute_value=True),
          reads=[("gkb",)], writes=[("mtmp", 1)])
    P.add("dve", lambda e: e.tensor_reduce(out=mtmp[:, 2:3], in_=skb[:], axis=AX.X, op=ALU.max),
          reads=[("skb",)], writes=[("mtmp", 2)])
    P.add("dve", lambda e: e.scalar_tensor_tensor(out=mtmp[:, 3:4], in0=mtmp[:, 0:1], scalar=8.0, in1=mtmp[:, 1:2],
                                                  op0=ALU.mult, op1=ALU.mult),
          reads=[("mtmp", 0), ("mtmp", 1)], writes=[("mtmp", 3)])
    P.add("dve", lambda e: e.tensor_tensor(out=mtmp[:, 4:5], in0=mtmp[:, 3:4], in1=mtmp[:, 2:3], op=ALU.max),
          reads=[("mtmp", 3), ("mtmp", 2)], writes=[("mtmp", 4)])
    P.add("dve", lambda e: e.tensor_scalar(out=negM[:], in0=mtmp[:, 4:5], scalar1=-1.0, scalar2=None, op0=ALU.mult),
          reads=[("mtmp", 4)], writes=[("negM",)])
    P.add("act", lambda e: e.activation(out=E4[:], in_=sk4[:], func=AF.Exp, bias=negM[:, 0:1], scale=1.0),
          reads=[("sk4",), ("negM",)], writes=[("E4",)])
    P.add("act", lambda e: e.activation(out=E8[:], in_=skb[:], func=AF.Exp, bias=negM[:, 0:1], scale=1.0),
          reads=[("skb",), ("negM",)], writes=[("E8",)])
    P.add("act", lambda e: e.activation(out=E8b[:], in_=E8[:], func=AF.Copy), reads=[("E8",)], writes=[("E8b",)])

    out_stores = []
    prefetched_x = set()

    def pass_tiles(p):
        tl = [(i, 128, 128 * i) for i in range(NT)]
        if p == SAMPLE_PASS and ENABLE_SAMPLE:
            tl.append((NT, NSMP, T))
        return tl

    def pass_subs(p):
        if p == SAMPLE_PASS and ENABLE_SAMPLE:
            return [(0, 348), (348, 348), (696, 344)]
        return [(0, 512), (512, 512)]

    def load_gain(src_d):
        P.dma("sp", gb[:], src_d.broadcast_to([128, D]), writes=[("gb",)], dsem="gbl")

    def norm_pre(p, i, R):
        c_ss = nstat(3)
        c_ac = nacc()
        b = i % 2
        P.add("act", lambda e: e.activation(out=junk[:R, :], in_=xres[:R, i, :], func=AF.Square,
                                            accum_out=acc[:R, c_ac:c_ac + 1]),
              reads=[("xres", i, 0), ("xres", i, 1), ("acc",)], writes=[("ac", c_ac)])
        P.add("act", lambda e: e.activation(out=stat[:R, c_ss + 1:c_ss + 2], in_=acc[:R, c_ac:c_ac + 1], func=AF.Ln,
                                            scale=1.0 / D, bias=EPS),
              reads=[("ac", c_ac)], writes=[("sg", c_ss)])
        P.add("act", lambda e: e.activation(out=stat[:R, c_ss + 2:c_ss + 3], in_=stat[:R, c_ss + 1:c_ss + 2], func=AF.Exp,
                                            scale=-0.5),
              reads=[("sg", c_ss)], writes=[("sg", c_ss)])
        P.add("dve", lambda e: e.scalar_tensor_tensor(out=hn[:R, b, :], in0=xres[:R, i, :], scalar=stat[:R, c_ss + 2:c_ss + 3],
                                                      in1=gb[:R, :], op0=ALU.mult, op1=ALU.mult),
              reads=[("xres", i, 0), ("xres", i, 1), ("sg", c_ss), ("gb",)], writes=[("hn", b)])
        return b

    def norm_post(p, i, R, c0, b, act_only=False):
        t = ntp()
        tpv = tp[t].rearrange("p (k n) -> p k n", k=8)
        for kc in range(8):
            P.add("pe", lambda e, kc=kc: e.transpose(tpv[:, kc, :R], hn[:R, b, 128 * kc:128 * kc + 128], ident[:R, :R]),
                  reads=[("hn", b), ("ident",)], writes=[("tp", t)])
        if i % 2 == 0 or act_only:
            P.add("act", lambda e: e.activation(out=hT[:, :, c0:c0 + R], in_=tpv[:, :, :R], func=AF.Copy),
                  reads=[("tp", t)], writes=[("hT", i)])
        else:
            P.add("dve", lambda e: e.tensor_copy(out=hT[:, :, c0:c0 + R], in_=tpv[:, :, :R]),
                  reads=[("tp", t)], writes=[("hT", i)])

    def sample_key_transposes():
        for h in range(2):
            t = ntp()
            tpv = tp[t].rearrange("p (k n) -> p k n", k=8)
            for nn in range(8):
                n = 8 * h + nn
                P.add("pe", lambda e, nn=nn, n=n: e.transpose(tpv[:, nn, :], kwb[:, n, :], ident[:, :]),
                      reads=[("kwb", h), ("ident",)], writes=[("tp", t)])
            P.add("act", lambda e, h=h: e.activation(out=kwb[:, 8 * h:8 * h + 8, :], in_=tpv[:, :, :], func=AF.Copy),
                  reads=[("tp", t)], writes=[("kwb", h)])

    def stage_A0(p, cb):
        tiles = pass_tiles(p)
        for (i, R, c0) in tiles:
            if i < NT:
                src = x_d[p * T + 128 * i:p * T + 128 * i + 128, :]
            else:
                src = xs_d
            if (p, i) in prefetched_x:
                continue
            if i == 1 and (p, 0) not in prefetched_x:
                load_gain(n1_d)
            P.dma("sp", xres[:R, i, :], src, writes=[("xres", i, 0), ("xres", i, 1)], dsem="xl%d" % i)
        a1 = stage_A1(p, cb)
        pend = None
        for (i, R, c0) in tiles:
            b = norm_pre(p, i, R)
            if pend is not None:
                norm_post(p, *pend, act_only=(pend[0] % 4 != 3))
                a1.tile(pend[0] - 1)
            pend = (i, R, c0, b)
        norm_post(p, *pend, act_only=(pend[0] % 4 != 3))
        a1.tile(pend[0] - 1)
        a1.tile(pend[0])
        a1.finish()

    def stage_A1(p, cb):
        tiles = pass_tiles(p)
        subs = pass_subs(p)
        last_pass = (p == NPASS - 1)
        u_done = set()

        def emit_u(upto_tile):
            for si, (s0, n) in enumerate(subs):
                if si in u_done:
                    continue
                need = max(k[-1] for k in ckeys("hT", s0, n))
                if need > upto_tile:
                    continue
                u_done.add(si)
                for g in range(4):
                    ch = cb + 3 + g // 2
                    bank = nbank()
                    for kc in range(8):
                        P.add("pe", lambda e, bank=bank, kc=kc, s0=s0, n=n, ch=ch, g=g: e.matmul(
                            pb[bank][:, 0:n], lhsT=slab(ch, kc, 128 * (g % 2), 128), rhs=hT[:, kc, s0:s0 + n],
                            start=(kc == 0), stop=(kc == 7)),
                            reads=ckeys("hT", s0, n) + [("ring", slot_of(ch))], writes=[("pb", bank)])
                    P.add("act", lambda e, bank=bank, s0=s0, n=n, g=g: e.activation(out=uT[:, g, 16 + s0:16 + s0 + n],
                                                                                  in_=pb[bank][:, 0:n], func=AF.Copy),
                          reads=[("pb", bank)], writes=ckeys("uT", s0, n, g))

        pend_b = []
        tile_by_idx = {t_[0]: t_ for t_ in tiles}

        def tile_body(i, R, c0):
            if i >= 1:
                emit_u(i - 1)
            bq0, bq1, bkv = nbank(), nbank(), nbank()
            for (bank, ch) in ((bq0, cb + 0), (bq1, cb + 1), (bkv, cb + 2)):
                for kc in range(8):
                    P.add("pe", lambda e, bank=bank, ch=ch, kc=kc: e.matmul(pb[bank][:R, 0:256], lhsT=hT[:, kc, c0:c0 + R],
                                                                          rhs=slab(ch, kc, 0, 256), start=(kc == 0), stop=(kc == 7)),
                          reads=[("hT", i), ("ring", slot_of(ch))], writes=[("pb", bank)])
            f_sq, f_q = ntf(), ntf()
            c_st = nstat(30)
            P.add("act", lambda e: e.activation(out=tmpF[:R, f_sq, 0:256], in_=pb[bq0][:R, 0:256], func=AF.Square),
                  reads=[("pb", bq0)], writes=[("tmpF", f_sq)])
            P.add("act", lambda e: e.activation(out=tmpF[:R, f_sq, 256:512], in_=pb[bq1][:R, 0:256], func=AF.Square),
                  reads=[("pb", bq1)], writes=[("tmpF", f_sq)])
            P.add("dve", lambda e: e.tensor_reduce(out=stat[:R, c_st:c_st + 8],
                                                   in_=tmpF[:R, f_sq, :].rearrange("p (h d) -> p h d", d=64), axis=AX.X, op=ALU.add),
                  reads=[("tmpF", f_sq)], writes=[("sg", c_st)])
            f_k = ntf()
            P.add("act", lambda e: e.activation(out=tmpF[:R, f_k, 0:128], in_=pb[bkv][:R, 0:128], func=AF.Square),
                  reads=[("pb", bkv)], writes=[("tmpF", f_k)])
            P.add("dve", lambda e: e.tensor_reduce(out=stat[:R, c_st + 8:c_st + 10],
                                                   in_=tmpF[:R, f_k, 0:128].rearrange("p (h d) -> p h d", d=64), axis=AX.X, op=ALU.add),
                  reads=[("tmpF", f_k)], writes=[("sg", c_st)])
            P.add("act", lambda e: e.activation(out=stat[:R, c_st + 10:c_st + 20], in_=stat[:R, c_st:c_st + 10], func=AF.Ln,
                                                scale=1.0 / 64, bias=EPS),
                  reads=[("sg", c_st), ("sg", c_st)], writes=[("sg", c_st)])
            P.add("act", lambda e: e.activation(out=stat[:R, c_st + 20:c_st + 30], in_=stat[:R, c_st + 10:c_st + 20], func=AF.Exp,
                                                scale=-0.5),
                  reads=[("sg", c_st)], writes=[("sg", c_st)])
            b_qn = ntb()
            is_smp = (i == NT)
            qn_dst = qn_s[:R, :] if is_smp else tmpB[:R, b_qn, :]
            qn_key = ("qn_s",) if is_smp else ("tmpB", b_qn)
            if not is_smp:
                for kvh, bank in ((0, bq0), (1, bq1)):
                    P.add("dve", lambda e, kvh=kvh, bank=bank: e.tensor_tensor(
                        out=qn_dst.rearrange("p (j k d) -> p j k d", j=4, k=2)[:, :, kvh, :],
                        in0=pb[bank][:R, 0:256].rearrange("p (j d) -> p j d", d=64),
                        in1=stat[:R, c_st + 20 + 4 * kvh:c_st + 24 + 4 * kvh].unsqueeze(2).broadcast_to([R, 4, 64]), op=ALU.mult),
                        reads=[("pb", bank), ("sg", c_st)], writes=[qn_key])
            for half, bank in (((0, bq0), (1, bq1)) if is_smp else ()):
                P.add("dve", lambda e, half=half, bank=bank: e.tensor_tensor(
                    out=tmpF[:R, f_q, 256 * half:256 * half + 256].rearrange("p (h d) -> p h d", d=64),
                    in0=pb[bank][:R, 0:256].rearrange("p (h d) -> p h d", d=64),
                    in1=stat[:R, c_st + 20 + 4 * half:c_st + 24 + 4 * half].unsqueeze(2).broadcast_to([R, 4, 64]), op=ALU.mult),
                    reads=[("pb", bank), ("sg", c_st)], writes=[("tmpF", f_q)])
            for kvh in (range(2) if is_smp else ()):
                P.add("dve", lambda e, kvh=kvh: e.tensor_tensor(
                    out=qn_dst.rearrange("p (j k d) -> p j k d", j=4, k=2)[:, :, kvh, :],
                    in0=tmpF[:R, f_q, 256 * kvh:256 * kvh + 256].rearrange("p (j d) -> p j d", d=64),
                    in1=gqb[:R, :].unsqueeze(1).broadcast_to([R, 4, 64]), op=ALU.mult),
                    reads=[("tmpF", f_q), ("gqb",)], writes=[qn_key])
            is_out_tile = (last_pass and i == NT - 1)
            is_smp = (i == NT)
            f_kn = ntf()
            b_kn = ntb()
            kdst = knf_s[:R, :] if is_smp else tmpF[:R, f_kn, 128:256]
            kdst_keys = [("knf_s",)] if is_smp else [("tmpF", f_kn)]
            plain_tile = (not is_out_tile) and (not is_smp)
            if plain_tile:
                for h in range(2):
                    P.add("dve", lambda e, h=h: e.scalar_tensor_tensor(
                        out=tmpB[:R, b_kn, 64 * h:64 * h + 64], in0=pb[bkv][:R, 64 * h:64 * h + 64],
                        scalar=stat[:R, c_st + 28 + h:c_st + 29 + h], in1=gkq[:R, :], op0=ALU.mult, op1=ALU.mult),
                        reads=[("pb", bkv), ("sg", c_st), ("gkq",)], writes=[("tmpB", b_kn)])
            else:
                P.add("dve", lambda e: e.tensor_tensor(
                    out=tmpF[:R, f_kn, 0:128].rearrange("p (h d) -> p h d", d=64),
                    in0=pb[bkv][:R, 0:128].rearrange("p (h d) -> p h d", d=64),
                    in1=stat[:R, c_st + 28:c_st + 30].unsqueeze(2).broadcast_to([R, 2, 64]), op=ALU.mult),
                    reads=[("pb", bkv), ("sg", c_st)], writes=[("tmpF", f_kn)])
            if not is_smp:
                P.add("act", lambda e: e.activation(out=vv[:R, i + 1, :], in_=pb[bkv][:R, 128:256], func=AF.Copy),
                      reads=[("pb", bkv)], writes=[("vv", i + 1)])
            if is_out_tile:
                P.add("dve", lambda e: e.tensor_tensor(
                    out=kdst.rearrange("p (h d) -> p h d", d=64),
                    in0=tmpF[:R, f_kn, 0:128].rearrange("p (h d) -> p h d", d=64),
                    in1=gkb[:R, :].unsqueeze(1).broadcast_to([R, 2, 64]), op=ALU.mult),
                    reads=[("tmpF", f_kn), ("gkb",)], writes=kdst_keys)
                P.add("dve", lambda e: e.tensor_tensor(
                    out=tmpB[:R, b_kn, 0:128].rearrange("p (h d) -> p h d", d=64),
                    in0=tmpF[:R, f_kn, 0:128].rearrange("p (h d) -> p h d", d=64),
                    in1=gkq[:R, :].unsqueeze(1).broadcast_to([R, 2, 64]), op=ALU.mult),
                    reads=[("tmpF", f_kn), ("gkq",)], writes=[("tmpB", b_kn)])
            elif is_smp:
                P.add("dve", lambda e: e.tensor_tensor(
                    out=kdst.rearrange("p (h d) -> p h d", d=64),
                    in0=tmpF[:R, f_kn, 0:128].rearrange("p (h d) -> p h d", d=64),
                    in1=gkb[:R, :].unsqueeze(1).broadcast_to([R, 2, 64]), op=ALU.mult),
                    reads=[("tmpF", f_kn), ("gkb",)], writes=kdst_keys)
                P.add("act", lambda e: e.activation(out=tmpB[:R, b_kn, 0:128], in_=kdst, func=AF.Copy),
                      reads=kdst_keys, writes=[("tmpB", b_kn)])
            elif False:
                P.add("dve", lambda e: e.tensor_tensor(
                    out=tmpB[:R, b_kn, 0:128].rearrange("p (h d) -> p h d", d=64),
                    in0=tmpF[:R, f_kn, 0:128].rearrange("p (h d) -> p h d", d=64),
                    in1=gkq[:R, :].unsqueeze(1).broadcast_to([R, 2, 64]), op=ALU.mult),
                    reads=[("tmpF", f_kn), ("gkq",)], writes=[("tmpB", b_kn)])
            if is_out_tile:
                P.add("act", lambda e: e.activation(out=tmpF[:R, f_kn, 256:384], in_=pb[bkv][:R, 128:256], func=AF.Copy),
                      reads=[("pb", bkv)], writes=[("tmpF", f_kn)])
                out_stores.append(P.dma("sp", kp_d, tmpF[:R, f_kn, 128:256], reads=[("tmpF", f_kn)], dsem="st_tf%d" % f_kn))
                out_stores.append(P.dma("sp", vp_d, tmpF[:R, f_kn, 256:384], reads=[("tmpF", f_kn)], dsem="st_tf%d" % f_kn))
            if is_smp:
                P.add("act", lambda e: e.activation(out=vf_s[:R, :], in_=pb[bkv][:R, 128:256], func=AF.Copy),
                      reads=[("pb", bkv)], writes=[("vf_s",)])
                P.add("act", lambda e: e.activation(out=v0b[:R, :], in_=pb[bkv][:R, 128:256], func=AF.Copy),
                      reads=[("pb", bkv)], writes=[("v0b",)])
                out_stores.append(P.dma("sp", ksn_d[:, 0:127, :], ck_d[:, 1:128, :], dsem="d2dk"))
                out_stores.append(P.dma("sp", vsn_d[:, 0:127, :], cv_d[:, 1:128, :], dsem="d2dv"))
                out_stores.append(P.dma("sp", ksn_d[:, 127, :], knf_s[:R, :], reads=[("knf_s",)], dsem="st_kn"))
                out_stores.append(P.dma("sp", vsn_d[:, 127, :], vf_s[:R, :], reads=[("vf_s",)], dsem="st_vn"))
            def part_b(i=i, R=R, c0=c0, qn_dst=qn_dst, qn_key=qn_key, b_kn=b_kn, is_smp=is_smp):
                t = ntp()
                tpv = tp[t].rearrange("p (k n) -> p k n", k=8)
                for j in range(4):
                    P.add("pe", lambda e, j=j: e.transpose(tpv[:, j, :R], qn_dst[:, 128 * j:128 * j + 128], ident[:R, :R]),
                          reads=[qn_key, ("ident",)], writes=[("tp", t)])
                P.add("pe", lambda e: e.transpose(tpv[:, 4, :R], tmpB[:R, b_kn, 0:128], ident[:R, :R]),
                      reads=[("tmpB", b_kn), ("ident",)], writes=[("tp", t)])
                P.add("act", lambda e: e.activation(out=qT[:, :, c0:c0 + R], in_=tpv[:, 0:4, :R], func=AF.Copy),
                      reads=[("tp", t)], writes=[("qT", i)])
                if not is_smp:
                    P.add("dve", lambda e: e.tensor_copy(out=kT[:, 128 + c0:128 + c0 + R], in_=tpv[:, 4, :R]),
                          reads=[("tp", t)], writes=[("kT", i + 1)])
            pend_b.append(part_b)
            while len(pend_b) > 2:
                pend_b.pop(0)()
            if is_out_tile or is_smp:
                bu0, bu1 = nbank(), nbank()
                for (bank, ch) in ((bu0, cb + 3), (bu1, cb + 4)):
                    for kc in range(8):
                        P.add("pe", lambda e, bank=bank, ch=ch, kc=kc: e.matmul(pb[bank][:R, 0:256], lhsT=hT[:, kc, c0:c0 + R],
                                                                              rhs=slab(ch, kc, 0, 256), start=(kc == 0), stop=(kc == 7)),
                              reads=[("hT", i), ("ring", slot_of(ch))], writes=[("pb", bank)])
                f_u = ntf()
                P.add("act", lambda e: e.activation(out=tmpF[:R, f_u, 0:256], in_=pb[bu0][:R, 0:256], func=AF.Copy),
                      reads=[("pb", bu0)], writes=[("tmpF", f_u)])
                P.add("act", lambda e: e.activation(out=tmpF[:R, f_u, 256:512], in_=pb[bu1][:R, 0:256], func=AF.Copy),
                      reads=[("pb", bu1)], writes=[("tmpF", f_u)])
                if is_out_tile:
                    out_stores.append(P.dma("sp", pp_d, tmpF[113:128, f_u, :], reads=[("tmpF", f_u)], dsem="st_tf%d" % f_u))
                else:
                    out_stores.append(P.dma("sp", psn_d[:, 14, :], tmpF[:R, f_u, :], reads=[("tmpF", f_u)], dsem="st_tf%d" % f_u))
        class _A1:
            def tile(self, i):
                if i in tile_by_idx:
                    tile_body(*tile_by_idx[i])

            def finish(self):
                while pend_b:
                    pend_b.pop(0)()
                emit_u(NT + 1)
                assert len(u_done) == len(subs)
                for c in range(cb, cb + 5):
                    release_chunk(c)

        return _A1()

    def pool_groups(p):
        first = (p == 0)
        has_smp = (p == SAMPLE_PASS and ENABLE_SAMPLE)
        groups = []
        ctx = {}

        def prologue():
            if not first:
                P.add("dve", lambda e: e.tensor_copy(out=uT[:, :, 0:16], in_=uhist[:]), reads=[("uhist",)], writes=[("uThist",)])
            if has_smp:
                f_s = [ntf(), ntf()]
                for cix in range(2):
                    P.dma("sp", tmpF[0:120, f_s[cix], :], st_d[8 * cix:8 * cix + 8].rearrange("n j c -> (n j) c"),
                          writes=[("tmpF", f_s[cix])], dsem="ld_tf%d" % f_s[cix])
                out_stores.append(P.dma("sp", psn_d[:, 0:14, :], st_d[:, 1:15, :], dsem="d2d"))
                bsm = 5
                ctx["bsm"] = bsm
                for g in range(4):
                    for cix in range(2):
                        P.add("pe", lambda e, g=g, cix=cix: e.matmul(pb[bsm][:, 16 * g + 8 * cix:16 * g + 8 * cix + 8],
                                                                   lhsT=tmpF[0:120, f_s[cix], 128 * g:128 * g + 128], rhs=sel[0:120, g, :],
                                                                   start=True, stop=True),
                              reads=[("tmpF", f_s[cix]), ("sel", g)], writes=[("pb", bsm)])
                P.add("act", lambda e: e.activation(out=ssum[:, :], in_=pb[bsm][:, 0:64], func=AF.Copy),
                      reads=[("pb", bsm)], writes=[("ssum",)])

        def one_group(hb, g):
            w = POOL_W[g]
            lo, hi = 512 * hb, 512 * hb + 512
            W = hi - lo
            k = g + 1
            src_is_u = True
            cur = None
            ckey = None
            for j in range(1, k + 1):
                ext = (1 << k) - (1 << j)
                sh = 1 << (j - 1)
                dst = poolA if (j % 2 == 1) else poolB
                dkey = ("poolA",) if (j % 2 == 1) else ("poolB",)
                a0 = lo - ext
                n = hi - a0
                if src_is_u:
                    in0 = uT[:, g, 16 + a0:16 + a0 + n]
                    in1 = uT[:, g, 16 + a0 - sh:16 + a0 - sh + n]
                    rk = ckeys("uT", max(a0 - sh, 0), n + sh, g) + [("uThist",)]
                else:
                    in0 = cur[:, 16 + a0 - lo:16 + a0 - lo + n]
                    in1 = cur[:, 16 + a0 - lo - sh:16 + a0 - lo - sh + n]
                    rk = [ckey]
                P.add("dve", lambda e, dst=dst, a0=a0, n=n, in0=in0, in1=in1: e.tensor_tensor(
                    out=dst[:, 16 + a0 - lo:16 + a0 - lo + n], in0=in0, in1=in1, op=ALU.add),
                    reads=rk, writes=[dkey])
                cur, ckey, src_is_u = dst, dkey, False
            P.add("dve", lambda e: e.scalar_tensor_tensor(
                out=pooledT[:, g, 0:W], in0=cur[:, 16:16 + W], scalar=1.0 / w, in1=uT[:, g, 16 + lo:16 + hi],
                op0=ALU.mult, op1=ALU.subtract),
                reads=[ckey] + ckeys("uT", lo, W, g), writes=[("pooledT", g)])
            if first and hb == 0:
                P.add("dve", lambda e: e.tensor_tensor(out=cur[:, 0:16], in0=cur[:, 16:32], in1=rc[:, g, :], op=ALU.mult),
                      reads=[ckey, ("rc",)], writes=[ckey])
                P.add("dve", lambda e: e.tensor_tensor(out=pooledT[:, g, 0:16], in0=cur[:, 0:16], in1=uT[:, g, 16:32], op=ALU.subtract),
                      reads=[ckey] + ckeys("uT", 0, 16, g), writes=[("pooledT", g)])
            if has_smp and hb == 1:
                P.add("dve", lambda e: e.scalar_tensor_tensor(
                    out=pooledT[:, g, 512:528], in0=uT[:, g, 16 + T:16 + T + NSMP], scalar=(1.0 / w - 1.0),
                    in1=ssum[:, 16 * g:16 * g + 16], op0=ALU.mult, op1=ALU.add),
                    reads=[("uT", g, NT), ("ssum",)], writes=[("pooledT", g)])

        def mix(hb):
            lo, hi = 512 * hb, 512 * hb + 512
            W = hi - lo
            if hb == 1 and p < NPASS - 1:
                P.add("dve", lambda e: e.tensor_copy(out=uhist[:], in_=uT[:, :, T:T + 16]),
                      reads=[("uT", g, NT - 1) for g in range(4)], writes=[("uhist",)])
            width = W + (NSMP if (has_smp and hb == 1) else 0)
            msubs = [(0, width)] if width <= 512 else [(0, width // 2), (width // 2, width - width // 2)]
            mbanks = [5] if hb == 0 else [0, 1, 4, 5]
            mctr = 0
            for g in range(4):
                for (m0, n) in msubs:
                    bank = mbanks[mctr % len(mbanks)]
                    mctr += 1
                    P.add("pe", lambda e, bank=bank, g=g, m0=m0, n=n: e.matmul(pb[bank][:, 0:n], lhsT=wmix[:, g, :],
                                                                             rhs=pooledT[:, g, m0:m0 + n], start=True, stop=True),
                          reads=[("pooledT", g), ("wmix",)], writes=[("pb", bank)])
                    P.add("act", lambda e, bank=bank, g=g, m0=m0, n=n: e.activation(
                        out=ypT[:, g, lo + m0:lo + m0 + n], in_=pb[bank][:, 0:n], func=AF.Identity, scale=psc[:, g:g + 1]),
                        reads=[("pb", bank), ("psc", g)], writes=ckeys("ypT", lo + m0, n, g))

        groups.append(prologue)
        for hb in range(2):
            for g in range(4):
                groups.append(lambda hb=hb, g=g: one_group(hb, g))
            groups.append(lambda hb=hb: mix(hb))
        return groups

    def stage_attn(p, fillers):
        first = (p == 0)
        bO, bD = 4, 5
        sctr = [0]

        def s_phase(i):
            c0 = 128 * i
            kbs = [1] if (first and i == 0) else [0, 1]
            pts = {}
            sdef = []
            for kvh in range(2):
                for kb in kbs:
                    blk = i + kb
                    bS = sctr[0] % 4
                    sctr[0] += 1
                    msk = maskC if kb == 1 else maskP
                    mkey = ("maskC",) if kb == 1 else ("maskP",)
                    P.add("pe", lambda e, bS=bS, kvh=kvh, blk=blk: e.matmul(
                        pb[bS][:, :], lhsT=kT[64 * kvh:64 * kvh + 64, 128 * blk:128 * blk + 128],
                        rhs=qT[64 * kvh:64 * kvh + 64, :, c0:c0 + 128], start=True, stop=False, tile_position=(64 * kvh, 0)),
                        reads=[("kT", blk), ("qT", i)], writes=[("pb", bS)])
                    sdef.append((bS, msk, mkey, kvh, kb, blk))
            for (bS, msk, mkey, kvh, kb, blk) in sdef:
                    P.add("pe", lambda e, bS=bS, msk=msk: e.matmul(
                        pb[bS][:, :], lhsT=ident[:, :], rhs=msk[:, :].unsqueeze(1).broadcast_to([128, 4, 128]),
                        start=False, stop=True),
                        reads=[("ident",), mkey], writes=[("pb", bS)])
            for (bS, msk, mkey, kvh, kb, blk) in sdef:
                    bp = 4 * (i % 2) + 2 * kvh + kb
                    P.add("act", lambda e, bS=bS, bp=bp: e.activation(out=tmpB[:, bp, :], in_=pb[bS][:, :], func=AF.Exp,
                                                                      bias=negM[:, 0:1], scale=0.125),
                          reads=[("pb", bS), ("negM",)], writes=[("tmpB", bp)])
                    pts[(kvh, kb)] = (bp, blk)
            return (kbs, pts)

        def pv_phase(i, kbs, pts, mid=None):
            c0 = 128 * i
            for kvh in range(2):
                for n_, kb in enumerate(kbs):
                    bp, blk = pts[(kvh, kb)]
                    P.add("pe", lambda e, kvh=kvh, bp=bp, n_=n_: e.matmul(
                        pb[bD][64 * kvh:64 * kvh + 64, :], lhsT=ones64[:, :], rhs=tmpB[:, bp, :],
                        start=(n_ == 0), stop=(n_ == len(kbs) - 1 and not SINK_VIA_PE), tile_position=(0, 64 * kvh),
                        skip_group_check=SINK_VIA_PE),
                        reads=[("ones64",), ("tmpB", bp)], writes=[("pb", bD)])
            f_d = ntf()
            if SINK_VIA_PE:
                for kvh in range(2):
                    P.add("pe", lambda e, kvh=kvh: e.matmul(
                        pb[bD][64 * kvh:64 * kvh + 64, :], lhsT=ones64[0:1, :],
                        rhs=E8b[0:1, 4 * kvh:4 * kvh + 4].unsqueeze(2).broadcast_to([1, 4, 128]),
                        start=False, stop=True, tile_position=(0, 64 * kvh), skip_group_check=True),
                        reads=[("ones64",), ("E8b",)], writes=[("pb", bD)])
            else:
                P.add("dve", lambda e: e.tensor_tensor(
                    out=tmpF[:, f_d, :].rearrange("p (j q) -> p j q", j=4), in0=pb[bD][:, :].rearrange("p (j q) -> p j q", j=4),
                    in1=E4[:, :].unsqueeze(2).broadcast_to([128, 4, 128]), op=ALU.add),
                    reads=[("pb", bD), ("E4",)], writes=[("tmpF", f_d)])
            for kvh in range(2):
                for n_, kb in enumerate(kbs):
                    bp, blk = pts[(kvh, kb)]
                    P.add("pe", lambda e, kvh=kvh, bp=bp, blk=blk, n_=n_: e.matmul(
                        pb[bO][64 * kvh:64 * kvh + 64, :], lhsT=vv[:, blk, 64 * kvh:64 * kvh + 64], rhs=tmpB[:, bp, :],
                        start=(n_ == 0), stop=(n_ == len(kbs) - 1), tile_position=(0, 64 * kvh)),
                        reads=[("vv", blk), ("tmpB", bp)], writes=[("pb", bO)])
            if SINK_VIA_PE:
                P.add("act", lambda e: e.activation(out=tmpF[:, f_d, :], in_=pb[bD][:, :], func=AF.Ln),
                      reads=[("pb", bD)], writes=[("tmpF", f_d)])
                P.add("act", lambda e: e.activation(out=tmpF[:, f_d, :], in_=tmpF[:, f_d, :], func=AF.Exp, scale=-1.0),
                      reads=[("tmpF", f_d)], writes=[("tmpF", f_d)])
            elif USE_ACT_RECIP:
                P.add("act", lambda e: e.activation(out=tmpF[:, f_d, :], in_=tmpF[:, f_d, :], func=AF.Ln),
                      reads=[("tmpF", f_d)], writes=[("tmpF", f_d)])
                P.add("act", lambda e: e.activation(out=tmpF[:, f_d, :], in_=tmpF[:, f_d, :], func=AF.Exp, scale=-1.0),
                      reads=[("tmpF", f_d)], writes=[("tmpF", f_d)])
            else:
                P.add("dve", lambda e: e.reciprocal(out=tmpF[:, f_d, :], in_=tmpF[:, f_d, :]),
                      reads=[("tmpF", f_d)], writes=[("tmpF", f_d)])
            if mid is not None:
                mid()
            P.add("dve", lambda e: e.tensor_tensor(
                out=yaT[:, :, c0:c0 + 128], in0=pb[bO][:, :].rearrange("p (j q) -> p j q", j=4),
                in1=tmpF[:, f_d, :].rearrange("p (j q) -> p j q", j=4), op=ALU.mult),
                reads=[("pb", bO), ("tmpF", f_d)], writes=[("yaT", i)])

        fillers = list(fillers)

        def fill(k):
            for _ in range(k):
                if fillers:
                    fillers.pop(0)()

        fill(1)
        nxt = s_phase(0)
        for i in range(NT):
            cur = nxt
            if i + 1 < NT:
                nxt = s_phase(i + 1)
            if i == 4:
                fill(1)
            pv_phase(i, *cur, mid=lambda: fill(1))
        if p < NPASS - 1:
            P.add("dve", lambda e: e.tensor_copy(out=kT[:, 0:128], in_=kT[:, T:T + 128]), reads=[("kT", NT)], writes=[("kT", 0)])
            P.add("dve", lambda e: e.tensor_copy(out=vv[:, 0, :], in_=vv[:, NT, :]), reads=[("vv", NT)], writes=[("vv", 0)])
        return fillers

    def stage_attn_sample(p):
        R = NSMP
        bO, bD, bS = 2, 3, 1
        f_pr = ntf()
        c_s0 = nstat(16)
        P.add("dve", lambda e: e.tensor_tensor(
            out=tmpF[:R, f_pr, :].rearrange("p (j k d) -> p j k d", j=4, k=2),
            in0=qn_s[:R, :].rearrange("p (j k d) -> p j k d", j=4, k=2),
            in1=knf_s[:R, :].rearrange("p (k d) -> p k d", k=2).unsqueeze(1).broadcast_to([R, 4, 2, 64]), op=ALU.mult),
            reads=[("qn_s",), ("knf_s",)], writes=[("tmpF", f_pr)])
        P.add("dve", lambda e: e.tensor_reduce(out=stat[:R, c_s0:c_s0 + 8], in_=tmpF[:R, f_pr, :].rearrange("p (h d) -> p h d", d=64),
                                               axis=AX.X, op=ALU.add),
              reads=[("tmpF", f_pr)], writes=[("sg", c_s0)])
        P.add("act", lambda e: e.activation(out=stat[:R, c_s0 + 8:c_s0 + 16], in_=stat[:R, c_s0:c_s0 + 8], func=AF.Exp,
                                            bias=negM[:R, 0:1], scale=0.125),
              reads=[("sg", c_s0), ("negM",)], writes=[("sg", c_s0)])
        b_p0 = 4
        P.add("dve", lambda e: e.tensor_tensor(
            out=tmpB[:R, b_p0, 0:8 * R].rearrange("p (k n j) -> p k n j", k=2, j=4),
            in0=stat[:R, c_s0 + 8:c_s0 + 16].rearrange("p (j k) -> p k j", k=2).unsqueeze(2).broadcast_to([R, 2, R, 4]),
            in1=ident[:R, :R].unsqueeze(1).unsqueeze(3).broadcast_to([R, 2, R, 4]), op=ALU.mult),
            reads=[("sg", c_s0), ("ident",)], writes=[("tmpB", b_p0)])
        for kvh in range(2):
            for (bank, lw) in ((bO, v0b[:R, 64 * kvh:64 * kvh + 64]), (bD, ones64[:R, :])):
                P.add("pe", lambda e, bank=bank, lw=lw, kvh=kvh: e.matmul(
                    pb[bank][0:64, 64 * kvh:64 * kvh + 64], lhsT=lw, rhs=tmpB[:R, b_p0, 64 * kvh:64 * kvh + 64],
                    start=(kvh == 0), stop=False, skip_group_check=True),
                    reads=[("v0b",), ("ones64",), ("tmpB", b_p0)], writes=[("pb", bank)])
        def st_S(n):
            bk = n % 4
            bSn = (0, 1, 4, 5)[n % 4]
            for kvh in range(2):
                P.add("pe", lambda e, kvh=kvh: e.matmul(
                    pb[bSn][:, 4 * kvh:4 * kvh + 4], lhsT=kwb[64 * kvh:64 * kvh + 64, n, :],
                    rhs=qT[64 * kvh:64 * kvh + 64, :, T + n], start=True, stop=True, tile_position=(64 * kvh, 0)),
                    reads=[("kwb", n // 8), ("qT", NT)], writes=[("pb", bSn)])
            P.add("act", lambda e: e.activation(out=tmpB[:, bk, 256:264], in_=pb[bSn][:, 0:8], func=AF.Exp,
                                                bias=negM[:, 0:1], scale=0.125),
                  reads=[("pb", bSn), ("negM",)], writes=[("tmpB", bk)])

        def st_PV(n):
            bk = n % 4
            for kvh in range(2):
                oc = 64 * kvh + 4 * n
                last = (n == R - 1 and kvh == 1)
                for (bank, lw) in ((bO, vwb[:, n, 64 * kvh:64 * kvh + 64]), (bD, ones64[:, :])):
                    P.add("pe", lambda e, bank=bank, lw=lw, oc=oc, last=last, kvh=kvh: e.matmul(
                        pb[bank][0:64, oc:oc + 4], lhsT=lw, rhs=tmpB[:, bk, 256 + 4 * kvh:260 + 4 * kvh], start=False, stop=last,
                        skip_group_check=True),
                        reads=[("vwb", n // 8), ("tmpB", bk), ("ones64",)], writes=[("pb", bank)])

        if SMP_MODE == "serial":
            for n in range(R):
                st_S(n)
                st_PV(n)
        else:
            lag = 2 if SMP_MODE == "pipe" else R
            for step in range(R + lag):
                if step < R:
                    st_S(step)
                if 0 <= step - lag < R:
                    st_PV(step - lag)
        if DBG_SATTN_LEVEL < 5:
            return
        f_d = ntf()
        P.add("dve", lambda e: e.tensor_tensor(
            out=tmpF[0:64, f_d, 0:8 * R].rearrange("p (k n j) -> p k n j", k=2, j=4),
            in0=pb[bD][0:64, 0:8 * R].rearrange("p (k n j) -> p k n j", k=2, j=4),
            in1=E8[0:64, :].rearrange("p (k j) -> p k j", k=2).unsqueeze(2).broadcast_to([64, 2, R, 4]), op=ALU.add),
            reads=[("pb", bD), ("E8",)], writes=[("tmpF", f_d)])
        P.add("dve", lambda e: e.reciprocal(out=tmpF[0:64, f_d, 0:8 * R], in_=tmpF[0:64, f_d, 0:8 * R]),
              reads=[("tmpF", f_d)], writes=[("tmpF", f_d)])
        b_o = 5
        P.add("dve", lambda e: e.tensor_tensor(
            out=tmpB[0:64, b_o, 0:8 * R].rearrange("p (k j n) -> p k n j", k=2, j=4),
            in0=pb[bO][0:64, 0:8 * R].rearrange("p (k n j) -> p k n j", k=2, j=4),
            in1=tmpF[0:64, f_d, 0:8 * R].rearrange("p (k n j) -> p k n j", k=2, j=4), op=ALU.mult),
            reads=[("pb", bO), ("tmpF", f_d)], writes=[("tmpB", b_o)])
        src = tmpB[0:64, b_o, 0:8 * R].rearrange("p (k j n) -> p k j n", k=2, j=4)
        for kvh in range(2):
            P.dma("sp", yaT[64 * kvh:64 * kvh + 64, :, T:T + R], src[:, kvh, :, :], reads=[("tmpB", b_o)], writes=[("yaT", NT)],
                  dsem="yas")

    def stage_merge(p, cb):
        subs = pass_subs(p)
        for m in range(4):
            cga, cgb = cb + 5 + 2 * m, cb + 6 + 2 * m
            for cc in range(2):
                c = 2 * m + cc
                for (s0, n) in subs:
                    bP, bA, bGA, bGB = nbank(), nbank(), nbank(), nbank()
                    for kc in range(4):
                        P.add("pe", lambda e, bP=bP, kc=kc, s0=s0, n=n, c=c: e.matmul(
                            pb[bP][:, 0:n], lhsT=wp[:, kc, 128 * c:128 * c + 128], rhs=ypT[:, kc, s0:s0 + n],
                            start=(kc == 0), stop=(kc == 3)),
                            reads=ckeys("ypT", s0, n, kc) + [("wp",)], writes=[("pb", bP)])
                    for kc in range(4):
                        P.add("pe", lambda e, bA=bA, kc=kc, s0=s0, n=n, c=c: e.matmul(
                            pb[bA][:, 0:n], lhsT=wa[:, kc, 128 * c:128 * c + 128], rhs=yaT[:, kc, s0:s0 + n],
                            start=(kc == 0), stop=(kc == 3)),
                            reads=ckeys("yaT", s0, n) + [("wa",)], writes=[("pb", bA)])
                    for (bank, ch) in ((bGA, cga), (bGB, cgb)):
                        for kc in range(8):
                            P.add("pe", lambda e, bank=bank, ch=ch, kc=kc, s0=s0, n=n, cc=cc: e.matmul(
                                pb[bank][:, 0:n], lhsT=slab(ch, kc, 128 * cc, 128), rhs=hT[:, kc, s0:s0 + n],
                                start=(kc == 0), stop=(kc == 7)),
                                reads=ckeys("hT", s0, n) + [("ring", slot_of(ch))], writes=[("pb", bank)])
                    fa, fb_, f1 = ntf(), ntf(), ntf()
                    P.add("act", lambda e, bGA=bGA, fa=fa, n=n: e.activation(out=tmpF[:, fa, 0:n], in_=pb[bGA][:, 0:n], func=AF.Sigmoid),
                          reads=[("pb", bGA)], writes=[("tmpF", fa)])
                    P.add("dve", lambda e, bP=bP, fa=fa, f1=f1, n=n: e.tensor_tensor(out=tmpF[:, f1, 0:n], in0=pb[bP][:, 0:n],
                                                                                    in1=tmpF[:, fa, 0:n], op=ALU.mult),
                          reads=[("pb", bP), ("tmpF", fa)], writes=[("tmpF", f1)])
                    P.add("act", lambda e, bGB=bGB, fb_=fb_, n=n: e.activation(out=tmpF[:, fb_, 0:n], in_=pb[bGB][:, 0:n], func=AF.Sigmoid),
                          reads=[("pb", bGB)], writes=[("tmpF", fb_)])
                    P.add("dve", lambda e, bA=bA, fb_=fb_, n=n: e.tensor_tensor(out=tmpF[:, fb_, 0:n], in0=pb[bA][:, 0:n],
                                                                               in1=tmpF[:, fb_, 0:n], op=ALU.mult),
                          reads=[("pb", bA), ("tmpF", fb_)], writes=[("tmpF", fb_)])
                    P.add("dve", lambda e, fb_=fb_, f1=f1, s0=s0, n=n, c=c: e.tensor_tensor(out=mT[:, c, s0:s0 + n], in0=tmpF[:, f1, 0:n],
                                                                                           in1=tmpF[:, fb_, 0:n], op=ALU.add),
                          reads=[("tmpF", f1), ("tmpF", fb_)], writes=ckeys("mT", s0, n, c))
            release_chunk(cga)
            release_chunk(cgb)

    def stage_wout_norm2(p):
        tiles = pass_tiles(p)
        load_gain(n2_d)
        pend = None
        for (i, R, c0) in tiles:
            for hf in range(2):
                bank = nbank()
                for kc in range(8):
                    P.add("pe", lambda e, bank=bank, kc=kc, hf=hf, i=i, R=R, c0=c0: e.matmul(
                        pb[bank][:R, :], lhsT=mT[:, kc, c0:c0 + R], rhs=wo[:, kc, 512 * hf:512 * hf + 512],
                        start=(kc == 0), stop=(kc == 7)),
                        reads=[("mT", kc, i), ("wo", kc // 2)], writes=[("pb", bank)])
                P.add("dve", lambda e, bank=bank, hf=hf, i=i, R=R: e.tensor_tensor(
                    out=xres[:R, i, 512 * hf:512 * hf + 512], in0=pb[bank][:R, :], in1=xres[:R, i, 512 * hf:512 * hf + 512], op=ALU.add),
                    reads=[("pb", bank), ("xres", i, hf)], writes=[("xres", i, hf)])
            b = norm_pre(p, i, R)
            if pend is not None:
                norm_post(p, *pend)
            pend = (i, R, c0, b)
        norm_post(p, *pend)

    def stage_ffn(p, cb):
        tiles = pass_tiles(p)
        subs = pass_subs(p)
        c = cb + 13
        ngrp = len(FFN_GROUPS)
        for gi, grp in enumerate(FFN_GROUPS):
            gchunks = []
            for si, s in enumerate(grp):
                cg, cu = c, c + 1
                c += 2
                for fcl in range(2):
                    fc = 2 * si + fcl
                    for (s0, n) in subs:
                        bG, bU = nbank(), nbank()
                        for (bank, ch) in ((bG, cg), (bU, cu)):
                            for kc in range(8):
                                P.add("pe", lambda e, bank=bank, ch=ch, kc=kc, s0=s0, n=n, fcl=fcl: e.matmul(
                                    pb[bank][:, 0:n], lhsT=slab(ch, kc, 128 * fcl, 128), rhs=hT[:, kc, s0:s0 + n],
                                    start=(kc == 0), stop=(kc == 7)),
                                    reads=ckeys("hT", s0, n) + [("ring", slot_of(ch))], writes=[("pb", bank)])
                        fg = ntf()
                        P.add("act", lambda e, bG=bG, fg=fg, n=n: e.activation(out=tmpF[:, fg, 0:n], in_=pb[bG][:, 0:n], func=AF.Silu),
                              reads=[("pb", bG)], writes=[("tmpF", fg)])
                        P.add("dve", lambda e, bU=bU, fg=fg, fc=fc, s0=s0, n=n: e.tensor_tensor(
                            out=actT[:, fc, s0:s0 + n], in0=pb[bU][:, 0:n], in1=tmpF[:, fg, 0:n], op=ALU.mult),
                            reads=[("pb", bU), ("tmpF", fg)], writes=ckeys("actT", s0, n, fc))
                release_chunk(cg)
                release_chunk(cu)
            dch = list(range(c, c + len(grp)))
            c += len(grp)
            nfc = 2 * len(grp)
            for (i, R, c0) in tiles:
                for hf in range(2):
                    bank = nbank()
                    for fc in range(nfc):
                        ch = dch[fc // 2]
                        P.add("pe", lambda e, bank=bank, fc=fc, ch=ch, hf=hf, R=R, c0=c0: e.matmul(
                            pb[bank][:R, :], lhsT=actT[:, fc, c0:c0 + R], rhs=dchunk(ch, fc % 2, hf),
                            start=(fc == 0), stop=(fc == nfc - 1)),
                            reads=[("actT", fc, i), ("ring", slot_of(ch))], writes=[("pb", bank)])
                    P.add("dve", lambda e, bank=bank, hf=hf, i=i, R=R: e.tensor_tensor(
                        out=xres[:R, i, 512 * hf:512 * hf + 512], in0=pb[bank][:R, :], in1=xres[:R, i, 512 * hf:512 * hf + 512], op=ALU.add),
                        reads=[("pb", bank), ("xres", i, hf)], writes=[("xres", i, hf)])
                if gi == ngrp - 1:
                    if i < NT:
                        dst = y_d[p * T + 128 * i:p * T + 128 * i + 128, :]
                    else:
                        dst = ys_d
                    out_stores.append(P.dma("sp", dst, xres[:R, i, :], reads=[("xres", i, 0), ("xres", i, 1)], dsem="ys%d" % i))
                    if p + 1 < NPASS and i < NT:
                        if i == 0:
                            load_gain(n1_d)
                        P.dma("sp", xres[:128, i, :], x_d[(p + 1) * T + 128 * i:(p + 1) * T + 128 * i + 128, :],
                              writes=[("xres", i, 0), ("xres", i, 1)], dsem="xl%d" % i)
                        prefetched_x.add((p + 1, i))
            for ch in dch:
                release_chunk(ch)

    all_r1 = [("wp",), ("wa",)]
    all_act = [("actT", fc, i) for fc in range(8) for i in range(NT + 1)]
    all_uT = [("uT", g, i) for g in range(4) for i in range(NT + 1)] + [("uThist",)]
    all_mT = [("mT", c, i) for c in range(8) for i in range(NT + 1)]
    for p in range(NPASS):
        cb = p * CPP
        fence("dve", all_mT, all_uT)
        stage_A0(p, cb)
        fence("dve", all_act, all_r1)
        for k in range(2):
            P.dma("pool", r1[:, 2 * k:2 * k + 2, 0:D], wp_d[256 * k:256 * k + 256, :].rearrange("(kc p) n -> p kc n", p=128),
                  reads=[], writes=[("wp",)], dsem="wpl")
        war = wa_d.rearrange("(k j d) n -> k d j n", k=2, j=4)
        for kvh in range(2):
            P.dma("pool", r1[64 * kvh:64 * kvh + 64, 4:8, 0:D], war[kvh], writes=[("wa",)], dsem="wal")
        rest = stage_attn(p, pool_groups(p))
        if p == SAMPLE_PASS and ENABLE_SAMPLE and not DBG_SKIP_SATTN:
            stage_attn_sample(p)
        for f_ in rest:
            f_()
        fence("dve", all_uT, all_mT)
        stage_merge(p, cb)
        stage_wout_norm2(p)
        fence("dve", all_r1, all_act)
        stage_ffn(p, cb)

    P.add("sp", None, extra=out_stores)
    P.emit()
    return nc


_CACHE = {}


def _get_nc():
    if "nc" not in _CACHE:
        _CACHE["nc"] = build_program()
    return _CACHE["nc"]


def kernel(x_prompt, x_sample, cache_k, cache_v, state_pool, norm1, w_in, q_norm, k_norm, sinks,
           pool_mix_w, pool_scale, w_pool_proj, w_attn_proj, w_out, norm2, w_gate, w_up, w_down):
    f = lambda a: np.ascontiguousarray(np.asarray(a, dtype=np.float32))
    x_prompt, x_sample, cache_k, cache_v, state_pool = map(f, (x_prompt, x_sample, cache_k, cache_v, state_pool))
    shared = {
        "norm1": f(norm1).reshape(1, D), "w_in": f(w_in).reshape(D, INW), "q_norm": f(q_norm).reshape(1, 64),
        "k_norm": f(k_norm).reshape(1, 64), "sinks": f(sinks).reshape(1, 8), "pool_mix_w": f(pool_mix_w).reshape(4, 128, 128),
        "pool_scale": f(pool_scale).reshape(512), "w_pool_proj": f(w_pool_proj).reshape(512, D),
        "w_attn_proj": f(w_attn_proj).reshape(512, D), "w_out": f(w_out).reshape(D, D), "norm2": f(norm2).reshape(1, D),
        "w_gate": f(w_gate).reshape(D, DFF), "w_up": f(w_up).reshape(D, DFF), "w_down": f(w_down).reshape(DFF, D),
    }
    in_maps = []
    for c in range(NCORES):
        m = dict(shared)
        m["x"] = x_prompt[c]
        m["xs"] = x_sample[NSMP * c:NSMP * c + NSMP, 0, :]
        m["ck"] = cache_k[0, NSMP * c:NSMP * c + NSMP].reshape(NSMP, 128, 128)
        m["cv"] = cache_v[0, NSMP * c:NSMP * c + NSMP].reshape(NSMP, 128, 128)
        m["ckT"] = m["ck"].transpose(0, 2, 1)
        m["st"] = state_pool[0, NSMP * c:NSMP * c + NSMP]
        in_maps.append({k: np.ascontiguousarray(v) for k, v in m.items()})
    nc = _get_nc()
    res = run_bass_kernel_spmd(nc, in_maps, core_ids=list(range(NCORES)))
    r = res.results
    y = np.stack([r[c]["y"] for c in range(NCORES)], 0).astype(np.float32)
    ys = np.concatenate([r[c]["ys"] for c in range(NCORES)], 0).reshape(NCORES * NSMP, 1, D).astype(np.float32)
    kp = np.stack([r[c]["kp"].reshape(128, 2, 64) for c in range(NCORES)], 0)[None].astype(np.float32)
    vp = np.stack([r[c]["vp"].reshape(128, 2, 64) for c in range(NCORES)], 0)[None].astype(np.float32)
    pp = np.stack([r[c]["pp"] for c in range(NCORES)], 0)[None].astype(np.float32)
    ksn = np.concatenate([r[c]["ksn"].reshape(NSMP, 128, 2, 64) for c in range(NCORES)], 0)[None].astype(np.float32)
    vsn = np.concatenate([r[c]["vsn"].reshape(NSMP, 128, 2, 64) for c in range(NCORES)], 0)[None].astype(np.float32)
    psn = np.concatenate([r[c]["psn"] for c in range(NCORES)], 0)[None].astype(np.float32)
    return (y, ys, kp, vp, pp, ksn, vsn, psn)
```

```python
import numpy as np
import concourse.bass as bass
import concourse.mybir as mybir
from concourse.bass_utils import run_bass_kernel_spmd

F32 = mybir.dt.float32
BF16 = mybir.dt.bfloat16
AF = mybir.ActivationFunctionType
ALU = mybir.AluOpType
AX = mybir.AxisListType

NCORES = 8
D = 1024
SEQ = 2048
NSMP = 16
T = 1024
NPASS = SEQ // T
NT = T // 128
DFF = 2816
INW = 3328
EPS = 1e-6
SAMPLE_PASS = 0
NSLOT = 7
SLOT = 2048
POOL_W = (2, 4, 8, 16)
NCOLMAX = T + NSMP
FFN_GROUPS = ((0, 1, 2, 3), (4, 5, 6, 7), (8, 9, 10))
ENABLE_SAMPLE = True
SINK_VIA_PE = True
SMP_MODE = "serial"
USE_ACT_RECIP = True
MASK_NEG = -30000.0
DBG_SKIP_SATTN = False
DBG_SATTN_LEVEL = 9


class _Op:
    __slots__ = ("eng", "fn", "reads", "writes", "dsem", "idx", "deps", "signal", "token", "extra")


class _Recorder:
    def __init__(self):
        self.call = None

    def __getattr__(self, name):
        def f(*args, **kwargs):
            assert self.call is None
            self.call = (name, args, kwargs)
            return None
        return f


class Prog:
    ENGS = ("pe", "act", "dve", "pool", "sp")

    def __init__(self, nc):
        self.nc = nc
        self.ops = []
        self.sems = {}
        self.dcount = {}

    def sem(self, name):
        if name not in self.sems:
            self.sems[name] = self.nc.alloc_semaphore(name)
        return self.sems[name]

    def add(self, eng, fn, reads=(), writes=(), dsem=None, extra=()):
        if fn is not None:
            rec = _Recorder()
            fn(rec)
            assert rec.call is not None
            name, args, kwargs = rec.call
            fn = lambda e, name=name, args=args, kwargs=kwargs: getattr(e, name)(*args, **kwargs)
        op = _Op()
        op.eng, op.fn, op.reads, op.writes, op.dsem = eng, fn, tuple(reads), tuple(writes), dsem
        op.extra = tuple(extra)
        op.idx = len(self.ops)
        op.signal = dsem is not None
        op.token = None
        self.ops.append(op)
        return op

    def dma(self, eng, out, in_, reads=(), writes=(), dsem=None, extra=()):
        return self.add(eng, lambda e, o=out, i=in_: e.dma_start(out=o, in_=i), reads, writes, dsem=dsem, extra=extra)

    def analyze(self):
        last_w = {}
        readers = {}
        for op in self.ops:
            deps = set(o.idx for o in op.extra)
            for k in op.reads:
                if k in last_w:
                    deps.add(last_w[k])
            for k in op.writes:
                if k in last_w:
                    deps.add(last_w[k])
                deps.update(readers.get(k, ()))
            deps.discard(op.idx)
            if op.eng == "pe":
                deps = set(d for d in deps if self.ops[d].eng != "pe" or self.ops[d].dsem is not None)
            latest = {}
            pruned = set()
            for d in deps:
                od = self.ops[d]
                if od.dsem is not None:
                    pruned.add(d)
                elif latest.get(od.eng, -1) < d:
                    latest[od.eng] = d
            pruned.update(latest.values())
            deps = pruned
            op.deps = deps
            for d in deps:
                self.ops[d].signal = True
            for k in op.reads:
                readers.setdefault(k, []).append(op.idx)
            for k in op.writes:
                last_w[k] = op.idx
                readers[k] = []
        cnt = {e: 0 for e in self.ENGS}
        for op in self.ops:
            if op.dsem is not None:
                self.dcount[op.dsem] = self.dcount.get(op.dsem, 0) + 16
                op.token = (op.dsem, self.dcount[op.dsem])
            elif op.signal:
                cnt[op.eng] += 1
                op.token = ("eng_" + op.eng, cnt[op.eng])

    def emit_engine(self, eng, e):
        waited = {}
        for op in self.ops:
            if op.eng != eng:
                continue
            need = {}
            for d in op.deps:
                s, v = self.ops[d].token
                if need.get(s, 0) < v:
                    need[s] = v
            for s, v in need.items():
                if waited.get(s, 0) < v:
                    e.wait_ge(self.sem(s), v)
                    waited[s] = v
            if op.fn is None:
                continue
            ins = op.fn(e)
            if op.token is not None:
                ins.then_inc(self.sem(op.token[0]), 16 if op.dsem is not None else 1)

    def emit(self):
        self.analyze()
        for op in self.ops:
            if op.token is not None:
                self.sem(op.token[0])
        nc = self.nc
        with nc.Block() as block:
            @block.tensor
            def _(e):
                self.emit_engine("pe", e)

            @block.scalar
            def _(e):
                self.emit_engine("act", e)

            @block.vector
            def _(e):
                self.emit_engine("dve", e)

            @block.gpsimd
            def _(e):
                self.emit_engine("pool", e)

            @block.sync
            def _(e):
                self.emit_engine("sp", e)


def _tile_cols(i):
    return (128 * i, 128) if i < NT else (T, NSMP)


def ckeys(name, c0, n, *pre):
    keys = []
    for i in range(NT + 1):
        lo, w = _tile_cols(i)
        if lo < c0 + n and c0 < lo + w:
            keys.append((name,) + pre + (i,))
    return keys


def build_program():
    nc = bass.Bass("TRN2", target_bir_lowering=False)
    P = Prog(nc)

    def din(name, shape):
        return nc.dram_tensor(name, list(shape), F32, kind="ExternalInput").ap()

    def dout(name, shape):
        return nc.dram_tensor(name, list(shape), F32, kind="ExternalOutput").ap()

    x_d = din("x", [SEQ, D])
    xs_d = din("xs", [NSMP, D])
    ck_d = din("ck", [NSMP, 128, 128])
    cv_d = din("cv", [NSMP, 128, 128])
    ckT_d = din("ckT", [NSMP, 128, 128])
    st_d = din("st", [NSMP, 15, 512])
    n1_d = din("norm1", [1, D])
    win_d = din("w_in", [D, INW])
    qn_d = din("q_norm", [1, 64])
    kn_d = din("k_norm", [1, 64])
    sk_d = din("sinks", [1, 8])
    pm_d = din("pool_mix_w", [4, 128, 128])
    psc_d = din("pool_scale", [512])
    wp_d = din("w_pool_proj", [512, D])
    wa_d = din("w_attn_proj", [512, D])
    wo_d = din("w_out", [D, D])
    n2_d = din("norm2", [1, D])
    wg_d = din("w_gate", [D, DFF])
    wu_d = din("w_up", [D, DFF])
    wd_d = din("w_down", [DFF, D])

    y_d = dout("y", [SEQ, D])
    ys_d = dout("ys", [NSMP, D])
    kp_d = dout("kp", [128, 128])
    vp_d = dout("vp", [128, 128])
    pp_d = dout("pp", [15, 512])
    ksn_d = dout("ksn", [NSMP, 128, 128])
    vsn_d = dout("vsn", [NSMP, 128, 128])
    psn_d = dout("psn", [NSMP, 15, 512])

    def sb(name, shape, dt):
        return nc.alloc_sbuf_tensor(name, list(shape), dt)

    xres = sb("xres", [128, NT + 1, D], F32)
    hT = sb("hT", [128, 8, NCOLMAX], BF16)
    gb = sb("gb", [128, D], F32)
    ident = sb("ident", [128, 128], BF16)
    maskP = sb("maskP", [128, 128], BF16)
    maskC = sb("maskC", [128, 128], BF16)
    ones64 = sb("ones64", [128, 64], BF16)
    ring = sb("ring", [128, NSLOT * 8, 256], BF16)
    wo = sb("wo", [128, 8, D], BF16)
    wmix = sb("wmix", [128, 4, 128], BF16)
    r1 = sb("r1", [128, 8, NCOLMAX], BF16)
    hn = sb("hn", [128, 2, D], BF16)
    UW = 16 + NCOLMAX
    um = sb("um", [128, 4 * UW], F32)
    uT = um[:, :].rearrange("p (g n) -> p g n", g=4)
    mT = um[:, :].bitcast(BF16).rearrange("p (c n) -> p c n", c=8)
    qT = sb("qT", [128, 4, NCOLMAX], BF16)
    kT = sb("kT", [128, 128 + T], BF16)
    vv = sb("vv", [128, NT + 1, 128], BF16)
    poolA = sb("poolA", [128, 16 + 512], F32)
    poolB = sb("poolB", [128, 16 + 512], F32)
    pooledT = sb("pooledT", [128, 4, 528], BF16)
    junk = poolA[:, 0:512].bitcast(BF16)
    kwb = sb("kwb", [128, NSMP, 128], BF16)
    vwb = sb("vwb", [128, NSMP, 128], BF16)
    ypT = sb("ypT", [128, 4, NCOLMAX], BF16)
    yaT = sb("yaT", [128, 4, NCOLMAX], BF16)
    NTF, NTB = 5, 8
    tmpF = sb("tmpF", [128, NTF, 512], F32)
    tmpB = sb("tmpB", [128, NTB, 512], BF16)
    stat = sb("stat", [128, 128], F32)
    acc = sb("acc", [128, 64], F32)
    uhist = sb("uhist", [128, 4, 16], F32)
    rc = sb("rc", [128, 4, 16], F32)
    psc = sb("psc", [128, 4], F32)
    gqb = sb("gqb", [128, 64], F32)
    gkb = sb("gkb", [128, 64], F32)
    skb = sb("skb", [128, 8], F32)
    sk4 = sb("sk4", [128, 4], F32)
    E4 = sb("E4", [128, 4], F32)
    E8 = sb("E8", [128, 8], F32)
    E8b = sb("E8b", [128, 8], BF16)
    negM = sb("negM", [128, 1], F32)
    mtmp = sb("mtmp", [128, 8], F32)
    sel = sb("sel", [128, 4, 8], F32)
    v0b = sb("v0b", [NSMP, 128], BF16)
    qn_s = sb("qn_s", [NSMP, 512], BF16)
    gkq = sb("gkq", [128, 64], F32)
    ssum = sb("ssum", [128, 64], F32)
    knf_s = sb("knf_s", [NSMP, 128], F32)
    vf_s = sb("vf_s", [NSMP, 128], F32)

    wp = r1[:, 0:4, 0:D]
    wa = r1
    wa = r1[:, 4:8, 0:D]
    actT = r1

    NB = 6
    pb = [nc.alloc_psum_tensor("pb%d" % i, [128, 512], F32) for i in range(NB)]
    tpf = [nc.alloc_psum_tensor("tp%d" % i, [128, 512], F32) for i in range(2)]
    tp = [t[:].bitcast(BF16) for t in tpf]

    state = {"bank": 0, "tf": 0, "tb": 0, "stat": 0, "tp": 0, "acc": 0}

    def nbank(exclude=()):
        while True:
            b = state["bank"]
            state["bank"] = (b + 1) % NB
            if b not in exclude:
                return b

    def ntf():
        b = state["tf"]
        state["tf"] = (b + 1) % NTF
        return b

    def ntb():
        b = state["tb"]
        state["tb"] = (b + 1) % NTB
        return b

    def nstat(n=1):
        assert n <= 32
        b = state["stat"]
        state["stat"] = (b + 1) % 4
        return 32 * b

    def nacc():
        b = state["acc"]
        state["acc"] = b + 1
        assert state["acc"] <= 64
        return b

    def ntp():
        b = state["tp"]
        state["tp"] = 1 - b
        return b

    def fence(eng, reads, writes):
        if eng == "dve":
            P.add("dve", lambda e: e.engine_nop(), reads, writes)
        else:
            raise ValueError(eng)

    chunks = []

    def win_slab(c0):
        return win_d[:, c0:c0 + 256].rearrange("(kc p) n -> p kc n", p=128)

    for p in range(NPASS):
        for c0 in (512, 768, 1024, 0, 256):
            chunks.append(win_slab(c0))
        for m in range(4):
            chunks.append(win_slab(1280 + 256 * m))
            chunks.append(win_slab(2304 + 256 * m))
        for grp in FFN_GROUPS:
            for s in grp:
                chunks.append(wg_d[:, 256 * s:256 * s + 256].rearrange("(kc p) n -> p kc n", p=128))
                chunks.append(wu_d[:, 256 * s:256 * s + 256].rearrange("(kc p) n -> p kc n", p=128))
            for s in grp:
                chunks.append(wd_d[256 * s:256 * s + 256, :].rearrange("(fc p) (a n) -> p fc a n", p=128, n=256))
    CPP = len(chunks) // NPASS
    stream = {"issued": 0}

    def slot_of(c):
        return c % NSLOT

    def issue_chunk():
        c = stream["issued"]
        if c >= len(chunks):
            return
        s = slot_of(c)
        dst = ring[:, s * 8:(s + 1) * 8, :]
        if len(chunks[c].shape) == 4:
            dst = dst.rearrange("p (fc a) n -> p fc a n", fc=2)
        P.dma("pool", dst, chunks[c], writes=[("ring", s)], dsem="ring%d" % s)
        stream["issued"] = c + 1

    def release_chunk(c):
        assert stream["issued"] == c + NSLOT or stream["issued"] == len(chunks), (stream["issued"], c)
        issue_chunk()

    def slab(c, kc, n0, n):
        s = slot_of(c)
        return ring[:, s * 8 + kc, n0:n0 + n]

    def dchunk(c, fc, hf):
        s = slot_of(c)
        r0 = s * 8 + fc * 4 + hf * 2
        return ring[:, r0:r0 + 2, :]

    P.add("pool", lambda e: e.memset(ident[:], 1.0), writes=[("ident",)])
    P.add("pool", lambda e: e.affine_select(out=ident[:], in_=ident[:], pattern=[[-1, 128]], compare_op=ALU.is_equal,
                                            fill=0.0, base=0, channel_multiplier=1), reads=[("ident",)], writes=[("ident",)])
    for c in range(NSLOT):
        issue_chunk()
    if ENABLE_SAMPLE:
        for h in range(2):
            P.dma("pool", kwb[:, 8 * h:8 * h + 8, :], ckT_d[8 * h:8 * h + 8, :, :].rearrange("n r c -> r n c"),
                  writes=[("kwb", h)], dsem="kwl%d" % h)
            P.dma("pool", vwb[:, 8 * h:8 * h + 8, :], cv_d[8 * h:8 * h + 8, :, :].rearrange("n r c -> r n c"),
                  writes=[("vwb", h)], dsem="vwl%d" % h)
    P.dma("pool", wmix[:], pm_d.rearrange("g c d -> c g d"), writes=[("wmix",)], dsem="wsmall")
    for k in range(4):
        P.dma("pool", wo[:, 2 * k:2 * k + 2, :], wo_d[256 * k:256 * k + 256, :].rearrange("(kc p) n -> p kc n", p=128),
              writes=[("wo", k)], dsem="wo%d" % k)
    P.add("pool", lambda e: e.memset(maskP[:], 0.0), writes=[("maskP",)])
    P.add("pool", lambda e: e.affine_select(out=maskP[:], in_=maskP[:], pattern=[[-1, 128]], compare_op=ALU.is_ge,
                                            fill=MASK_NEG, base=0, channel_multiplier=1), reads=[("maskP",)], writes=[("maskP",)])
    P.add("pool", lambda e: e.memset(maskC[:], 0.0), writes=[("maskC",)])
    P.add("pool", lambda e: e.affine_select(out=maskC[:], in_=maskC[:], pattern=[[1, 128]], compare_op=ALU.is_ge,
                                            fill=MASK_NEG, base=0, channel_multiplier=-1), reads=[("maskC",)], writes=[("maskC",)])
    P.add("pool", lambda e: e.memset(ones64[:], 1.0), writes=[("ones64",)])
    for g, w in enumerate(POOL_W):
        P.add("pool", lambda e, g=g, w=w: e.memset(sel[:, g, :], 1.0 / w), writes=[("sel", g)])
        P.add("pool", lambda e, g=g, w=w: e.affine_select(out=sel[:, g, :], in_=sel[:, g, :], pattern=[[-15, 8]],
                                                          compare_op=ALU.is_ge, fill=0.0, base=-(16 - w),
                                                          channel_multiplier=1), reads=[("sel", g)], writes=[("sel", g)])
        P.add("pool", lambda e, g=g, w=w: e.affine_select(out=sel[:, g, :], in_=sel[:, g, :], pattern=[[15, 8]],
                                                          compare_op=ALU.is_ge, fill=0.0, base=14,
                                                          channel_multiplier=-1), reads=[("sel", g)], writes=[("sel", g)])

    P.dma("sp", gqb[:], qn_d.broadcast_to([128, 64]), writes=[("gqb",)], dsem="sm_gq")
    P.dma("sp", gkb[:], kn_d.broadcast_to([128, 64]), writes=[("gkb",)], dsem="sm_gk")
    P.dma("sp", skb[:], sk_d.broadcast_to([128, 8]), writes=[("skb",)], dsem="sm_sk")
    P.dma("sp", sk4[0:64, :], sk_d[:, 0:4].broadcast_to([64, 4]), writes=[("sk4",)], dsem="sm_sk4")
    P.dma("sp", sk4[64:128, :], sk_d[:, 4:8].broadcast_to([64, 4]), writes=[("sk4",)], dsem="sm_sk4")
    for g in range(4):
        P.dma("sp", psc[:, g:g + 1], psc_d[128 * g:128 * g + 128].rearrange("(c o) -> c o", o=1), writes=[("psc", g)],
              dsem="sm_psc%d" % g)

    P.add("dve", lambda e: e.memset(acc[:], 0.0), writes=[("acc",)])
    P.add("dve", lambda e: e.tensor_tensor(out=gkq[:], in0=gkb[:], in1=gqb[:], op=ALU.mult),
          reads=[("gkb",), ("gqb",)], writes=[("gkq",)])
    P.add("dve", lambda e: e.memset(uT[:, :, 0:16], 0.0), writes=[("uThist",)])
    for g, w in enumerate(POOL_W):
        P.add("dve", lambda e, g=g, w=w: e.memset(rc[:, g, :], 1.0 / w), writes=[("rc",)])
        for pos in range(w - 1):
            P.add("dve", lambda e, g=g, pos=pos: e.memset(rc[:, g, pos:pos + 1], 1.0 / (pos + 1)), writes=[("rc",)])

    P.add("dve", lambda e: e.tensor_reduce(out=mtmp[:, 0:1], in_=gqb[:], axis=AX.X, op=ALU.max, apply_absolute_value=True),
          reads=[("gqb",)], writes=[("mtmp", 0)])
    P.add("dve", lambda e: e.tensor_reduce(out=mtmp[:, 1:2], in_=gkb[:], axis=AX.X, op=ALU.max, apply_absolute_value=True),
          reads=[("gkb",)], writes=[("mtmp", 1)])
    P.add("dve", lambda e: e.tensor_reduce(out=mtmp[:, 2:3], in_=skb[:], axis=AX.X, op=ALU.max),
          reads=[("skb",)], writes=[("mtmp", 2)])
    P.add("dve", lambda e: e.scalar_tensor_tensor(out=mtmp[:, 3:4], in0=mtmp[:, 0:1], scalar=8.0, in1=mtmp[:, 1:2],
                                                  op0=ALU.mult, op1=ALU.mult),
          reads=[("mtmp", 0), ("mtmp", 1)], writes=[("mtmp", 3)])
    P.add("dve", lambda e: e.tensor_tensor(out=mtmp[:, 4:5], in0=mtmp[:, 3:4], in1=mtmp[:, 2:3], op=ALU.max),
          reads=[("mtmp", 3), ("mtmp", 2)], writes=[("mtmp", 4)])
    P.add("dve", lambda e: e.tensor_scalar(out=negM[:], in0=mtmp[:, 4:5], scalar1=-1.0, scalar2=None, op0=ALU.mult),
          reads=[("mtmp", 4)], writes=[("negM",)])
    P.add("act", lambda e: e.activation(out=E4[:], in_=sk4[:], func=AF.Exp, bias=negM[:, 0:1], scale=1.0),
          reads=[("sk4",), ("negM",)], writes=[("E4",)])
    P.add("act", lambda e: e.activation(out=E8[:], in_=skb[:], func=AF.Exp, bias=negM[:, 0:1], scale=1.0),
          reads=[("skb",), ("negM",)], writes=[("E8",)])
    P.add("act", lambda e: e.activation(out=E8b[:], in_=E8[:], func=AF.Copy), reads=[("E8",)], writes=[("E8b",)])

    out_stores = []
    prefetched_x = set()

    def pass_tiles(p):
        tl = [(i, 128, 128 * i) for i in range(NT)]
        if p == SAMPLE_PASS and ENABLE_SAMPLE:
            tl.append((NT, NSMP, T))
        return tl

    def pass_subs(p):
        if p == SAMPLE_PASS and ENABLE_SAMPLE:
            return [(0, 348), (348, 348), (696, 344)]
        return [(0, 512), (512, 512)]

    def load_gain(src_d):
        P.dma("sp", gb[:], src_d.broadcast_to([128, D]), writes=[("gb",)], dsem="gbl")

    def norm_pre(p, i, R):
        c_ss = nstat(3)
        c_ac = nacc()
        b = i % 2
        P.add("act", lambda e: e.activation(out=junk[:R, :], in_=xres[:R, i, :], func=AF.Square,
                                            accum_out=acc[:R, c_ac:c_ac + 1]),
              reads=[("xres", i, 0), ("xres", i, 1), ("acc",)], writes=[("ac", c_ac)])
        P.add("act", lambda e: e.activation(out=stat[:R, c_ss + 1:c_ss + 2], in_=acc[:R, c_ac:c_ac + 1], func=AF.Ln,
                                            scale=1.0 / D, bias=EPS),
              reads=[("ac", c_ac)], writes=[("sg", c_ss)])
        P.add("act", lambda e: e.activation(out=stat[:R, c_ss + 2:c_ss + 3], in_=stat[:R, c_ss + 1:c_ss + 2], func=AF.Exp,
                                            scale=-0.5),
              reads=[("sg", c_ss)], writes=[("sg", c_ss)])
        P.add("dve", lambda e: e.scalar_tensor_tensor(out=hn[:R, b, :], in0=xres[:R, i, :], scalar=stat[:R, c_ss + 2:c_ss + 3],
                                                      in1=gb[:R, :], op0=ALU.mult, op1=ALU.mult),
              reads=[("xres", i, 0), ("xres", i, 1), ("sg", c_ss), ("gb",)], writes=[("hn", b)])
        return b

    def norm_post(p, i, R, c0, b):
        t = ntp()
        tpv = tp[t].rearrange("p (k n) -> p k n", k=8)
        for kc in range(8):
            P.add("pe", lambda e, kc=kc: e.transpose(tpv[:, kc, :R], hn[:R, b, 128 * kc:128 * kc + 128], ident[:R, :R]),
                  reads=[("hn", b), ("ident",)], writes=[("tp", t)])
        if i % 2 == 0:
            P.add("act", lambda e: e.activation(out=hT[:, :, c0:c0 + R], in_=tpv[:, :, :R], func=AF.Copy),
                  reads=[("tp", t)], writes=[("hT", i)])
        else:
            P.add("dve", lambda e: e.tensor_copy(out=hT[:, :, c0:c0 + R], in_=tpv[:, :, :R]),
                  reads=[("tp", t)], writes=[("hT", i)])

    def sample_key_transposes():
        for h in range(2):
            t = ntp()
            tpv = tp[t].rearrange("p (k n) -> p k n", k=8)
            for nn in range(8):
                n = 8 * h + nn
                P.add("pe", lambda e, nn=nn, n=n: e.transpose(tpv[:, nn, :], kwb[:, n, :], ident[:, :]),
                      reads=[("kwb", h), ("ident",)], writes=[("tp", t)])
            P.add("act", lambda e, h=h: e.activation(out=kwb[:, 8 * h:8 * h + 8, :], in_=tpv[:, :, :], func=AF.Copy),
                  reads=[("tp", t)], writes=[("kwb", h)])

    def stage_A0(p, cb):
        tiles = pass_tiles(p)
        for (i, R, c0) in tiles:
            if i < NT:
                src = x_d[p * T + 128 * i:p * T + 128 * i + 128, :]
            else:
                src = xs_d
            if (p, i) in prefetched_x:
                continue
            if i == 1 and (p, 0) not in prefetched_x:
                load_gain(n1_d)
            P.dma("sp", xres[:R, i, :], src, writes=[("xres", i, 0), ("xres", i, 1)], dsem="xl%d" % i)
        a1 = stage_A1(p, cb)
        pend = None
        for (i, R, c0) in tiles:
            b = norm_pre(p, i, R)
            if pend is not None:
                norm_post(p, *pend)
                a1.tile(pend[0] - 1)
            pend = (i, R, c0, b)
        norm_post(p, *pend)
        a1.tile(pend[0] - 1)
        a1.tile(pend[0])
        a1.finish()

    def stage_A1(p, cb):
        tiles = pass_tiles(p)
        subs = pass_subs(p)
        last_pass = (p == NPASS - 1)
        u_done = set()

        def emit_u(upto_tile):
            for si, (s0, n) in enumerate(subs):
                if si in u_done:
                    continue
                need = max(k[-1] for k in ckeys("hT", s0, n))
                if need > upto_tile:
                    continue
                u_done.add(si)
                for g in range(4):
                    ch = cb + 3 + g // 2
                    bank = nbank()
                    for kc in range(8):
                        P.add("pe", lambda e, bank=bank, kc=kc, s0=s0, n=n, ch=ch, g=g: e.matmul(
                            pb[bank][:, 0:n], lhsT=slab(ch, kc, 128 * (g % 2), 128), rhs=hT[:, kc, s0:s0 + n],
                            start=(kc == 0), stop=(kc == 7)),
                            reads=ckeys("hT", s0, n) + [("ring", slot_of(ch))], writes=[("pb", bank)])
                    P.add("act", lambda e, bank=bank, s0=s0, n=n, g=g: e.activation(out=uT[:, g, 16 + s0:16 + s0 + n],
                                                                                  in_=pb[bank][:, 0:n], func=AF.Copy),
                          reads=[("pb", bank)], writes=ckeys("uT", s0, n, g))

        pend_b = []
        tile_by_idx = {t_[0]: t_ for t_ in tiles}

        def tile_body(i, R, c0):
            if i >= 1:
                emit_u(i - 1)
            bq0, bq1, bkv = nbank(), nbank(), nbank()
            for (bank, ch) in ((bq0, cb + 0), (bq1, cb + 1), (bkv, cb + 2)):
                for kc in range(8):
                    P.add("pe", lambda e, bank=bank, ch=ch, kc=kc: e.matmul(pb[bank][:R, 0:256], lhsT=hT[:, kc, c0:c0 + R],
                                                                          rhs=slab(ch, kc, 0, 256), start=(kc == 0), stop=(kc == 7)),
                          reads=[("hT", i), ("ring", slot_of(ch))], writes=[("pb", bank)])
            f_sq, f_q = ntf(), ntf()
            c_st = nstat(30)
            P.add("act", lambda e: e.activation(out=tmpF[:R, f_sq, 0:256], in_=pb[bq0][:R, 0:256], func=AF.Square),
                  reads=[("pb", bq0)], writes=[("tmpF", f_sq)])
            P.add("act", lambda e: e.activation(out=tmpF[:R, f_sq, 256:512], in_=pb[bq1][:R, 0:256], func=AF.Square),
                  reads=[("pb", bq1)], writes=[("tmpF", f_sq)])
            P.add("dve", lambda e: e.tensor_reduce(out=stat[:R, c_st:c_st + 8],
                                                   in_=tmpF[:R, f_sq, :].rearrange("p (h d) -> p h d", d=64), axis=AX.X, op=ALU.add),
                  reads=[("tmpF", f_sq)], writes=[("sg", c_st)])
            f_k = ntf()
            P.add("act", lambda e: e.activation(out=tmpF[:R, f_k, 0:128], in_=pb[bkv][:R, 0:128], func=AF.Square),
                  reads=[("pb", bkv)], writes=[("tmpF", f_k)])
            P.add("dve", lambda e: e.tensor_reduce(out=stat[:R, c_st + 8:c_st + 10],
                                                   in_=tmpF[:R, f_k, 0:128].rearrange("p (h d) -> p h d", d=64), axis=AX.X, op=ALU.add),
                  reads=[("tmpF", f_k)], writes=[("sg", c_st)])
            P.add("act", lambda e: e.activation(out=stat[:R, c_st + 10:c_st + 20], in_=stat[:R, c_st:c_st + 10], func=AF.Ln,
                                                scale=1.0 / 64, bias=EPS),
                  reads=[("sg", c_st), ("sg", c_st)], writes=[("sg", c_st)])
            P.add("act", lambda e: e.activation(out=stat[:R, c_st + 20:c_st + 30], in_=stat[:R, c_st + 10:c_st + 20], func=AF.Exp,
                                                scale=-0.5),
                  reads=[("sg", c_st)], writes=[("sg", c_st)])
            b_qn = ntb()
            is_smp = (i == NT)
            qn_dst = qn_s[:R, :] if is_smp else tmpB[:R, b_qn, :]
            qn_key = ("qn_s",) if is_smp else ("tmpB", b_qn)
            if not is_smp:
                for kvh, bank in ((0, bq0), (1, bq1)):
                    P.add("dve", lambda e, kvh=kvh, bank=bank: e.tensor_tensor(
                        out=qn_dst.rearrange("p (j k d) -> p j k d", j=4, k=2)[:, :, kvh, :],
                        in0=pb[bank][:R, 0:256].rearrange("p (j d) -> p j d", d=64),
                        in1=stat[:R, c_st + 20 + 4 * kvh:c_st + 24 + 4 * kvh].unsqueeze(2).broadcast_to([R, 4, 64]), op=ALU.mult),
                        reads=[("pb", bank), ("sg", c_st)], writes=[qn_key])
            for half, bank in (((0, bq0), (1, bq1)) if is_smp else ()):
                P.add("dve", lambda e, half=half, bank=bank: e.tensor_tensor(
                    out=tmpF[:R, f_q, 256 * half:256 * half + 256].rearrange("p (h d) -> p h d", d=64),
                    in0=pb[bank][:R, 0:256].rearrange("p (h d) -> p h d", d=64),
                    in1=stat[:R, c_st + 20 + 4 * half:c_st + 24 + 4 * half].unsqueeze(2).broadcast_to([R, 4, 64]), op=ALU.mult),
                    reads=[("pb", bank), ("sg", c_st)], writes=[("tmpF", f_q)])
            for kvh in (range(2) if is_smp else ()):
                P.add("dve", lambda e, kvh=kvh: e.tensor_tensor(
                    out=qn_dst.rearrange("p (j k d) -> p j k d", j=4, k=2)[:, :, kvh, :],
                    in0=tmpF[:R, f_q, 256 * kvh:256 * kvh + 256].rearrange("p (j d) -> p j d", d=64),
                    in1=gqb[:R, :].unsqueeze(1).broadcast_to([R, 4, 64]), op=ALU.mult),
                    reads=[("tmpF", f_q), ("gqb",)], writes=[qn_key])
            is_out_tile = (last_pass and i == NT - 1)
            is_smp = (i == NT)
            f_kn = ntf()
            b_kn = ntb()
            kdst = knf_s[:R, :] if is_smp else tmpF[:R, f_kn, 128:256]
            kdst_keys = [("knf_s",)] if is_smp else [("tmpF", f_kn)]
            plain_tile = (not is_out_tile) and (not is_smp)
            if plain_tile:
                for h in range(2):
                    P.add("dve", lambda e, h=h: e.scalar_tensor_tensor(
                        out=tmpB[:R, b_kn, 64 * h:64 * h + 64], in0=pb[bkv][:R, 64 * h:64 * h + 64],
                        scalar=stat[:R, c_st + 28 + h:c_st + 29 + h], in1=gkq[:R, :], op0=ALU.mult, op1=ALU.mult),
                        reads=[("pb", bkv), ("sg", c_st), ("gkq",)], writes=[("tmpB", b_kn)])
            else:
                P.add("dve", lambda e: e.tensor_tensor(
                    out=tmpF[:R, f_kn, 0:128].rearrange("p (h d) -> p h d", d=64),
                    in0=pb[bkv][:R, 0:128].rearrange("p (h d) -> p h d", d=64),
                    in1=stat[:R, c_st + 28:c_st + 30].unsqueeze(2).broadcast_to([R, 2, 64]), op=ALU.mult),
                    reads=[("pb", bkv), ("sg", c_st)], writes=[("tmpF", f_kn)])
            if not is_smp:
                P.add("act", lambda e: e.activation(out=vv[:R, i + 1, :], in_=pb[bkv][:R, 128:256], func=AF.Copy),
                      reads=[("pb", bkv)], writes=[("vv", i + 1)])
            if is_out_tile:
                P.add("dve", lambda e: e.tensor_tensor(
                    out=kdst.rearrange("p (h d) -> p h d", d=64),
                    in0=tmpF[:R, f_kn, 0:128].rearrange("p (h d) -> p h d", d=64),
                    in1=gkb[:R, :].unsqueeze(1).broadcast_to([R, 2, 64]), op=ALU.mult),
                    reads=[("tmpF", f_kn), ("gkb",)], writes=kdst_keys)
                P.add("dve", lambda e: e.tensor_tensor(
                    out=tmpB[:R, b_kn, 0:128].rearrange("p (h d) -> p h d", d=64),
                    in0=tmpF[:R, f_kn, 0:128].rearrange("p (h d) -> p h d", d=64),
                    in1=gkq[:R, :].unsqueeze(1).broadcast_to([R, 2, 64]), op=ALU.mult),
                    reads=[("tmpF", f_kn), ("gkq",)], writes=[("tmpB", b_kn)])
            elif is_smp:
                P.add("dve", lambda e: e.tensor_tensor(
                    out=kdst.rearrange("p (h d) -> p h d", d=64),
                    in0=tmpF[:R, f_kn, 0:128].rearrange("p (h d) -> p h d", d=64),
                    in1=gkb[:R, :].unsqueeze(1).broadcast_to([R, 2, 64]), op=ALU.mult),
                    reads=[("tmpF", f_kn), ("gkb",)], writes=kdst_keys)
                P.add("act", lambda e: e.activation(out=tmpB[:R, b_kn, 0:128], in_=kdst, func=AF.Copy),
                      reads=kdst_keys, writes=[("tmpB", b_kn)])
            elif False:
                P.add("dve", lambda e: e.tensor_tensor(
                    out=tmpB[:R, b_kn, 0:128].rearrange("p (h d) -> p h d", d=64),
                    in0=tmpF[:R, f_kn, 0:128].rearrange("p (h d) -> p h d", d=64),
                    in1=gkq[:R, :].unsqueeze(1).broadcast_to([R, 2, 64]), op=ALU.mult),
                    reads=[("tmpF", f_kn), ("gkq",)], writes=[("tmpB", b_kn)])
            if is_out_tile:
                P.add("act", lambda e: e.activation(out=tmpF[:R, f_kn, 256:384], in_=pb[bkv][:R, 128:256], func=AF.Copy),
                      reads=[("pb", bkv)], writes=[("tmpF", f_kn)])
                out_stores.append(P.dma("sp", kp_d, tmpF[:R, f_kn, 128:256], reads=[("tmpF", f_kn)], dsem="st_tf%d" % f_kn))
                out_stores.append(P.dma("sp", vp_d, tmpF[:R, f_kn, 256:384], reads=[("tmpF", f_kn)], dsem="st_tf%d" % f_kn))
            if is_smp:
                P.add("act", lambda e: e.activation(out=vf_s[:R, :], in_=pb[bkv][:R, 128:256], func=AF.Copy),
                      reads=[("pb", bkv)], writes=[("vf_s",)])
                P.add("act", lambda e: e.activation(out=v0b[:R, :], in_=pb[bkv][:R, 128:256], func=AF.Copy),
                      reads=[("pb", bkv)], writes=[("v0b",)])
                out_stores.append(P.dma("sp", ksn_d[:, 0:127, :], ck_d[:, 1:128, :], dsem="d2dk"))
                out_stores.append(P.dma("sp", vsn_d[:, 0:127, :], cv_d[:, 1:128, :], dsem="d2dv"))
                out_stores.append(P.dma("sp", ksn_d[:, 127, :], knf_s[:R, :], reads=[("knf_s",)], dsem="st_kn"))
                out_stores.append(P.dma("sp", vsn_d[:, 127, :], vf_s[:R, :], reads=[("vf_s",)], dsem="st_vn"))
            def part_b(i=i, R=R, c0=c0, qn_dst=qn_dst, qn_key=qn_key, b_kn=b_kn, is_smp=is_smp):
                t = ntp()
                tpv = tp[t].rearrange("p (k n) -> p k n", k=8)
                for j in range(4):
                    P.add("pe", lambda e, j=j: e.transpose(tpv[:, j, :R], qn_dst[:, 128 * j:128 * j + 128], ident[:R, :R]),
                          reads=[qn_key, ("ident",)], writes=[("tp", t)])
                P.add("pe", lambda e: e.transpose(tpv[:, 4, :R], tmpB[:R, b_kn, 0:128], ident[:R, :R]),
                      reads=[("tmpB", b_kn), ("ident",)], writes=[("tp", t)])
                P.add("act", lambda e: e.activation(out=qT[:, :, c0:c0 + R], in_=tpv[:, 0:4, :R], func=AF.Copy),
                      reads=[("tp", t)], writes=[("qT", i)])
                if not is_smp:
                    P.add("dve", lambda e: e.tensor_copy(out=kT[:, 128 + c0:128 + c0 + R], in_=tpv[:, 4, :R]),
                          reads=[("tp", t)], writes=[("kT", i + 1)])
            pend_b.append(part_b)
            while len(pend_b) > 1:
                pend_b.pop(0)()
            if is_out_tile or is_smp:
                bu0, bu1 = nbank(), nbank()
                for (bank, ch) in ((bu0, cb + 3), (bu1, cb + 4)):
                    for kc in range(8):
                        P.add("pe", lambda e, bank=bank, ch=ch, kc=kc: e.matmul(pb[bank][:R, 0:256], lhsT=hT[:, kc, c0:c0 + R],
                                                                              rhs=slab(ch, kc, 0, 256), start=(kc == 0), stop=(kc == 7)),
                              reads=[("hT", i), ("ring", slot_of(ch))], writes=[("pb", bank)])
                f_u = ntf()
                P.add("act", lambda e: e.activation(out=tmpF[:R, f_u, 0:256], in_=pb[bu0][:R, 0:256], func=AF.Copy),
                      reads=[("pb", bu0)], writes=[("tmpF", f_u)])
                P.add("act", lambda e: e.activation(out=tmpF[:R, f_u, 256:512], in_=pb[bu1][:R, 0:256], func=AF.Copy),
                      reads=[("pb", bu1)], writes=[("tmpF", f_u)])
                if is_out_tile:
                    out_stores.append(P.dma("sp", pp_d, tmpF[113:128, f_u, :], reads=[("tmpF", f_u)], dsem="st_tf%d" % f_u))
                else:
                    out_stores.append(P.dma("sp", psn_d[:, 14, :], tmpF[:R, f_u, :], reads=[("tmpF", f_u)], dsem="st_tf%d" % f_u))
        class _A1:
            def tile(self, i):
                if i in tile_by_idx:
                    tile_body(*tile_by_idx[i])

            def finish(self):
                while pend_b:
                    pend_b.pop(0)()
                emit_u(NT + 1)
                assert len(u_done) == len(subs)
                for c in range(cb, cb + 5):
                    release_chunk(c)

        return _A1()

    def pool_groups(p):
        first = (p == 0)
        has_smp = (p == SAMPLE_PASS and ENABLE_SAMPLE)
        groups = []
        ctx = {}

        def prologue():
            if not first:
                P.add("dve", lambda e: e.tensor_copy(out=uT[:, :, 0:16], in_=uhist[:]), reads=[("uhist",)], writes=[("uThist",)])
            if has_smp:
                f_s = [ntf(), ntf()]
                for cix in range(2):
                    P.dma("sp", tmpF[0:120, f_s[cix], :], st_d[8 * cix:8 * cix + 8].rearrange("n j c -> (n j) c"),
                          writes=[("tmpF", f_s[cix])], dsem="ld_tf%d" % f_s[cix])
                out_stores.append(P.dma("sp", psn_d[:, 0:14, :], st_d[:, 1:15, :], dsem="d2d"))
                bsm = 5
                ctx["bsm"] = bsm
                for g in range(4):
                    for cix in range(2):
                        P.add("pe", lambda e, g=g, cix=cix: e.matmul(pb[bsm][:, 16 * g + 8 * cix:16 * g + 8 * cix + 8],
                                                                   lhsT=tmpF[0:120, f_s[cix], 128 * g:128 * g + 128], rhs=sel[0:120, g, :],
                                                                   start=True, stop=True),
                              reads=[("tmpF", f_s[cix]), ("sel", g)], writes=[("pb", bsm)])
                P.add("act", lambda e: e.activation(out=ssum[:, :], in_=pb[bsm][:, 0:64], func=AF.Copy),
                      reads=[("pb", bsm)], writes=[("ssum",)])

        def one_group(hb, g):
            w = POOL_W[g]
            lo, hi = 512 * hb, 512 * hb + 512
            W = hi - lo
            k = g + 1
            src_is_u = True
            cur = None
            ckey = None
            for j in range(1, k + 1):
                ext = (1 << k) - (1 << j)
                sh = 1 << (j - 1)
                dst = poolA if (j % 2 == 1) else poolB
                dkey = ("poolA",) if (j % 2 == 1) else ("poolB",)
                a0 = lo - ext
                n = hi - a0
                if src_is_u:
                    in0 = uT[:, g, 16 + a0:16 + a0 + n]
                    in1 = uT[:, g, 16 + a0 - sh:16 + a0 - sh + n]
                    rk = ckeys("uT", max(a0 - sh, 0), n + sh, g) + [("uThist",)]
                else:
                    in0 = cur[:, 16 + a0 - lo:16 + a0 - lo + n]
                    in1 = cur[:, 16 + a0 - lo - sh:16 + a0 - lo - sh + n]
                    rk = [ckey]
                P.add("dve", lambda e, dst=dst, a0=a0, n=n, in0=in0, in1=in1: e.tensor_tensor(
                    out=dst[:, 16 + a0 - lo:16 + a0 - lo + n], in0=in0, in1=in1, op=ALU.add),
                    reads=rk, writes=[dkey])
                cur, ckey, src_is_u = dst, dkey, False
            P.add("dve", lambda e: e.scalar_tensor_tensor(
                out=pooledT[:, g, 0:W], in0=cur[:, 16:16 + W], scalar=1.0 / w, in1=uT[:, g, 16 + lo:16 + hi],
                op0=ALU.mult, op1=ALU.subtract),
                reads=[ckey] + ckeys("uT", lo, W, g), writes=[("pooledT", g)])
            if first and hb == 0:
                P.add("dve", lambda e: e.tensor_tensor(out=cur[:, 0:16], in0=cur[:, 16:32], in1=rc[:, g, :], op=ALU.mult),
                      reads=[ckey, ("rc",)], writes=[ckey])
                P.add("dve", lambda e: e.tensor_tensor(out=pooledT[:, g, 0:16], in0=cur[:, 0:16], in1=uT[:, g, 16:32], op=ALU.subtract),
                      reads=[ckey] + ckeys("uT", 0, 16, g), writes=[("pooledT", g)])
            if has_smp and hb == 1:
                P.add("dve", lambda e: e.scalar_tensor_tensor(
                    out=pooledT[:, g, 512:528], in0=uT[:, g, 16 + T:16 + T + NSMP], scalar=(1.0 / w - 1.0),
                    in1=ssum[:, 16 * g:16 * g + 16], op0=ALU.mult, op1=ALU.add),
                    reads=[("uT", g, NT), ("ssum",)], writes=[("pooledT", g)])

        def mix(hb):
            lo, hi = 512 * hb, 512 * hb + 512
            W = hi - lo
            if hb == 1 and p < NPASS - 1:
                P.add("dve", lambda e: e.tensor_copy(out=uhist[:], in_=uT[:, :, T:T + 16]),
                      reads=[("uT", g, NT - 1) for g in range(4)], writes=[("uhist",)])
            width = W + (NSMP if (has_smp and hb == 1) else 0)
            msubs = [(0, width)] if width <= 512 else [(0, width // 2), (width // 2, width - width // 2)]
            mbanks = [5] if hb == 0 else [0, 1, 4, 5]
            mctr = 0
            for g in range(4):
                for (m0, n) in msubs:
                    bank = mbanks[mctr % len(mbanks)]
                    mctr += 1
                    P.add("pe", lambda e, bank=bank, g=g, m0=m0, n=n: e.matmul(pb[bank][:, 0:n], lhsT=wmix[:, g, :],
                                                                             rhs=pooledT[:, g, m0:m0 + n], start=True, stop=True),
                          reads=[("pooledT", g), ("wmix",)], writes=[("pb", bank)])
                    P.add("act", lambda e, bank=bank, g=g, m0=m0, n=n: e.activation(
                        out=ypT[:, g, lo + m0:lo + m0 + n], in_=pb[bank][:, 0:n], func=AF.Identity, scale=psc[:, g:g + 1]),
                        reads=[("pb", bank), ("psc", g)], writes=ckeys("ypT", lo + m0, n, g))

        groups.append(prologue)
        for hb in range(2):
            for g in range(4):
                groups.append(lambda hb=hb, g=g: one_group(hb, g))
            groups.append(lambda hb=hb: mix(hb))
        return groups

    def stage_attn(p, fillers):
        first = (p == 0)
        bO, bD = 4, 5
        sctr = [0]

        def s_phase(i):
            c0 = 128 * i
            kbs = [1] if (first and i == 0) else [0, 1]
            pts = {}
            sdef = []
            for kvh in range(2):
                for kb in kbs:
                    blk = i + kb
                    bS = sctr[0] % 4
                    sctr[0] += 1
                    msk = maskC if kb == 1 else maskP
                    mkey = ("maskC",) if kb == 1 else ("maskP",)
                    P.add("pe", lambda e, bS=bS, kvh=kvh, blk=blk: e.matmul(
                        pb[bS][:, :], lhsT=kT[64 * kvh:64 * kvh + 64, 128 * blk:128 * blk + 128],
                        rhs=qT[64 * kvh:64 * kvh + 64, :, c0:c0 + 128], start=True, stop=False, tile_position=(64 * kvh, 0)),
                        reads=[("kT", blk), ("qT", i)], writes=[("pb", bS)])
                    sdef.append((bS, msk, mkey, kvh, kb, blk))
            for (bS, msk, mkey, kvh, kb, blk) in sdef:
                    P.add("pe", lambda e, bS=bS, msk=msk: e.matmul(
                        pb[bS][:, :], lhsT=ident[:, :], rhs=msk[:, :].unsqueeze(1).broadcast_to([128, 4, 128]),
                        start=False, stop=True),
                        reads=[("ident",), mkey], writes=[("pb", bS)])
            for (bS, msk, mkey, kvh, kb, blk) in sdef:
                    bp = 4 * (i % 2) + 2 * kvh + kb
                    P.add("act", lambda e, bS=bS, bp=bp: e.activation(out=tmpB[:, bp, :], in_=pb[bS][:, :], func=AF.Exp,
                                                                      bias=negM[:, 0:1], scale=0.125),
                          reads=[("pb", bS), ("negM",)], writes=[("tmpB", bp)])
                    pts[(kvh, kb)] = (bp, blk)
            return (kbs, pts)

        def pv_phase(i, kbs, pts, mid=None):
            c0 = 128 * i
            for kvh in range(2):
                for n_, kb in enumerate(kbs):
                    bp, blk = pts[(kvh, kb)]
                    P.add("pe", lambda e, kvh=kvh, bp=bp, n_=n_: e.matmul(
                        pb[bD][64 * kvh:64 * kvh + 64, :], lhsT=ones64[:, :], rhs=tmpB[:, bp, :],
                        start=(n_ == 0), stop=(n_ == len(kbs) - 1 and not SINK_VIA_PE), tile_position=(0, 64 * kvh),
                        skip_group_check=SINK_VIA_PE),
                        reads=[("ones64",), ("tmpB", bp)], writes=[("pb", bD)])
            f_d = ntf()
            if SINK_VIA_PE:
                for kvh in range(2):
                    P.add("pe", lambda e, kvh=kvh: e.matmul(
                        pb[bD][64 * kvh:64 * kvh + 64, :], lhsT=ones64[0:1, :],
                        rhs=E8b[0:1, 4 * kvh:4 * kvh + 4].unsqueeze(2).broadcast_to([1, 4, 128]),
                        start=False, stop=True, tile_position=(0, 64 * kvh), skip_group_check=True),
                        reads=[("ones64",), ("E8b",)], writes=[("pb", bD)])
            else:
                P.add("dve", lambda e: e.tensor_tensor(
                    out=tmpF[:, f_d, :].rearrange("p (j q) -> p j q", j=4), in0=pb[bD][:, :].rearrange("p (j q) -> p j q", j=4),
                    in1=E4[:, :].unsqueeze(2).broadcast_to([128, 4, 128]), op=ALU.add),
                    reads=[("pb", bD), ("E4",)], writes=[("tmpF", f_d)])
            for kvh in range(2):
                for n_, kb in enumerate(kbs):
                    bp, blk = pts[(kvh, kb)]
                    P.add("pe", lambda e, kvh=kvh, bp=bp, blk=blk, n_=n_: e.matmul(
                        pb[bO][64 * kvh:64 * kvh + 64, :], lhsT=vv[:, blk, 64 * kvh:64 * kvh + 64], rhs=tmpB[:, bp, :],
                        start=(n_ == 0), stop=(n_ == len(kbs) - 1), tile_position=(0, 64 * kvh)),
                        reads=[("vv", blk), ("tmpB", bp)], writes=[("pb", bO)])
            if SINK_VIA_PE:
                P.add("act", lambda e: e.activation(out=tmpF[:, f_d, :], in_=pb[bD][:, :], func=AF.Ln),
                      reads=[("pb", bD)], writes=[("tmpF", f_d)])
                P.add("act", lambda e: e.activation(out=tmpF[:, f_d, :], in_=tmpF[:, f_d, :], func=AF.Exp, scale=-1.0),
                      reads=[("tmpF", f_d)], writes=[("tmpF", f_d)])
            elif USE_ACT_RECIP:
                P.add("act", lambda e: e.activation(out=tmpF[:, f_d, :], in_=tmpF[:, f_d, :], func=AF.Ln),
                      reads=[("tmpF", f_d)], writes=[("tmpF", f_d)])
                P.add("act", lambda e: e.activation(out=tmpF[:, f_d, :], in_=tmpF[:, f_d, :], func=AF.Exp, scale=-1.0),
                      reads=[("tmpF", f_d)], writes=[("tmpF", f_d)])
            else:
                P.add("dve", lambda e: e.reciprocal(out=tmpF[:, f_d, :], in_=tmpF[:, f_d, :]),
                      reads=[("tmpF", f_d)], writes=[("tmpF", f_d)])
            if mid is not None:
                mid()
            P.add("dve", lambda e: e.tensor_tensor(
                out=yaT[:, :, c0:c0 + 128], in0=pb[bO][:, :].rearrange("p (j q) -> p j q", j=4),
                in1=tmpF[:, f_d, :].rearrange("p (j q) -> p j q", j=4), op=ALU.mult),
                reads=[("pb", bO), ("tmpF", f_d)], writes=[("yaT", i)])

        fillers = list(fillers)

        def fill(k):
            for _ in range(k):
                if fillers:
                    fillers.pop(0)()

        fill(1)
        nxt = s_phase(0)
        for i in range(NT):
            cur = nxt
            if i + 1 < NT:
                nxt = s_phase(i + 1)
            if i == 4:
                fill(1)
            pv_phase(i, *cur, mid=lambda: fill(1))
        if p < NPASS - 1:
            P.add("dve", lambda e: e.tensor_copy(out=kT[:, 0:128], in_=kT[:, T:T + 128]), reads=[("kT", NT)], writes=[("kT", 0)])
            P.add("dve", lambda e: e.tensor_copy(out=vv[:, 0, :], in_=vv[:, NT, :]), reads=[("vv", NT)], writes=[("vv", 0)])
        return fillers

    def stage_attn_sample(p):
        R = NSMP
        bO, bD, bS = 2, 3, 1
        f_pr = ntf()
        c_s0 = nstat(16)
        P.add("dve", lambda e: e.tensor_tensor(
            out=tmpF[:R, f_pr, :].rearrange("p (j k d) -> p j k d", j=4, k=2),
            in0=qn_s[:R, :].rearrange("p (j k d) -> p j k d", j=4, k=2),
            in1=knf_s[:R, :].rearrange("p (k d) -> p k d", k=2).unsqueeze(1).broadcast_to([R, 4, 2, 64]), op=ALU.mult),
            reads=[("qn_s",), ("knf_s",)], writes=[("tmpF", f_pr)])
        P.add("dve", lambda e: e.tensor_reduce(out=stat[:R, c_s0:c_s0 + 8], in_=tmpF[:R, f_pr, :].rearrange("p (h d) -> p h d", d=64),
                                               axis=AX.X, op=ALU.add),
              reads=[("tmpF", f_pr)], writes=[("sg", c_s0)])
        P.add("act", lambda e: e.activation(out=stat[:R, c_s0 + 8:c_s0 + 16], in_=stat[:R, c_s0:c_s0 + 8], func=AF.Exp,
                                            bias=negM[:R, 0:1], scale=0.125),
              reads=[("sg", c_s0), ("negM",)], writes=[("sg", c_s0)])
        b_p0 = 4
        P.add("dve", lambda e: e.tensor_tensor(
            out=tmpB[:R, b_p0, 0:8 * R].rearrange("p (k n j) -> p k n j", k=2, j=4),
            in0=stat[:R, c_s0 + 8:c_s0 + 16].rearrange("p (j k) -> p k j", k=2).unsqueeze(2).broadcast_to([R, 2, R, 4]),
            in1=ident[:R, :R].unsqueeze(1).unsqueeze(3).broadcast_to([R, 2, R, 4]), op=ALU.mult),
            reads=[("sg", c_s0), ("ident",)], writes=[("tmpB", b_p0)])
        for kvh in range(2):
            for (bank, lw) in ((bO, v0b[:R, 64 * kvh:64 * kvh + 64]), (bD, ones64[:R, :])):
                P.add("pe", lambda e, bank=bank, lw=lw, kvh=kvh: e.matmul(
                    pb[bank][0:64, 64 * kvh:64 * kvh + 64], lhsT=lw, rhs=tmpB[:R, b_p0, 64 * kvh:64 * kvh + 64],
                    start=(kvh == 0), stop=False, skip_group_check=True),
                    reads=[("v0b",), ("ones64",), ("tmpB", b_p0)], writes=[("pb", bank)])
        def st_S(n):
            bk = n % 4
            bSn = (0, 1, 4, 5)[n % 4]
            for kvh in range(2):
                P.add("pe", lambda e, kvh=kvh: e.matmul(
                    pb[bSn][:, 4 * kvh:4 * kvh + 4], lhsT=kwb[64 * kvh:64 * kvh + 64, n, :],
                    rhs=qT[64 * kvh:64 * kvh + 64, :, T + n], start=True, stop=True, tile_position=(64 * kvh, 0)),
                    reads=[("kwb", n // 8), ("qT", NT)], writes=[("pb", bSn)])
            P.add("act", lambda e: e.activation(out=tmpB[:, bk, 256:264], in_=pb[bSn][:, 0:8], func=AF.Exp,
                                                bias=negM[:, 0:1], scale=0.125),
                  reads=[("pb", bSn), ("negM",)], writes=[("tmpB", bk)])

        def st_PV(n):
            bk = n % 4
            for kvh in range(2):
                oc = 64 * kvh + 4 * n
                last = (n == R - 1 and kvh == 1)
                for (bank, lw) in ((bO, vwb[:, n, 64 * kvh:64 * kvh + 64]), (bD, ones64[:, :])):
                    P.add("pe", lambda e, bank=bank, lw=lw, oc=oc, last=last, kvh=kvh: e.matmul(
                        pb[bank][0:64, oc:oc + 4], lhsT=lw, rhs=tmpB[:, bk, 256 + 4 * kvh:260 + 4 * kvh], start=False, stop=last,
                        skip_group_check=True),
                        reads=[("vwb", n // 8), ("tmpB", bk), ("ones64",)], writes=[("pb", bank)])

        if SMP_MODE == "serial":
            for n in range(R):
                st_S(n)
                st_PV(n)
        else:
            lag = 2 if SMP_MODE == "pipe" else R
            for step in range(R + lag):
                if step < R:
                    st_S(step)
                if 0 <= step - lag < R:
                    st_PV(step - lag)
        if DBG_SATTN_LEVEL < 5:
            return
        f_d = ntf()
        P.add("dve", lambda e: e.tensor_tensor(
            out=tmpF[0:64, f_d, 0:8 * R].rearrange("p (k n j) -> p k n j", k=2, j=4),
            in0=pb[bD][0:64, 0:8 * R].rearrange("p (k n j) -> p k n j", k=2, j=4),
            in1=E8[0:64, :].rearrange("p (k j) -> p k j", k=2).unsqueeze(2).broadcast_to([64, 2, R, 4]), op=ALU.add),
            reads=[("pb", bD), ("E8",)], writes=[("tmpF", f_d)])
        P.add("dve", lambda e: e.reciprocal(out=tmpF[0:64, f_d, 0:8 * R], in_=tmpF[0:64, f_d, 0:8 * R]),
              reads=[("tmpF", f_d)], writes=[("tmpF", f_d)])
        b_o = 5
        P.add("dve", lambda e: e.tensor_tensor(
            out=tmpB[0:64, b_o, 0:8 * R].rearrange("p (k j n) -> p k n j", k=2, j=4),
            in0=pb[bO][0:64, 0:8 * R].rearrange("p (k n j) -> p k n j", k=2, j=4),
            in1=tmpF[0:64, f_d, 0:8 * R].rearrange("p (k n j) -> p k n j", k=2, j=4), op=ALU.mult),
            reads=[("pb", bO), ("tmpF", f_d)], writes=[("tmpB", b_o)])
        src = tmpB[0:64, b_o, 0:8 * R].rearrange("p (k j n) -> p k j n", k=2, j=4)
        for kvh in range(2):
            P.dma("sp", yaT[64 * kvh:64 * kvh + 64, :, T:T + R], src[:, kvh, :, :], reads=[("tmpB", b_o)], writes=[("yaT", NT)],
                  dsem="yas")

    def stage_merge(p, cb):
        subs = pass_subs(p)
        for m in range(4):
            cga, cgb = cb + 5 + 2 * m, cb + 6 + 2 * m
            for cc in range(2):
                c = 2 * m + cc
                for (s0, n) in subs:
                    bP, bA, bGA, bGB = nbank(), nbank(), nbank(), nbank()
                    for kc in range(4):
                        P.add("pe", lambda e, bP=bP, kc=kc, s0=s0, n=n, c=c: e.matmul(
                            pb[bP][:, 0:n], lhsT=wp[:, kc, 128 * c:128 * c + 128], rhs=ypT[:, kc, s0:s0 + n],
                            start=(kc == 0), stop=(kc == 3)),
                            reads=ckeys("ypT", s0, n, kc) + [("wp",)], writes=[("pb", bP)])
                    for kc in range(4):
                        P.add("pe", lambda e, bA=bA, kc=kc, s0=s0, n=n, c=c: e.matmul(
                            pb[bA][:, 0:n], lhsT=wa[:, kc, 128 * c:128 * c + 128], rhs=yaT[:, kc, s0:s0 + n],
                            start=(kc == 0), stop=(kc == 3)),
                            reads=ckeys("yaT", s0, n) + [("wa",)], writes=[("pb", bA)])
                    for (bank, ch) in ((bGA, cga), (bGB, cgb)):
                        for kc in range(8):
                            P.add("pe", lambda e, bank=bank, ch=ch, kc=kc, s0=s0, n=n, cc=cc: e.matmul(
                                pb[bank][:, 0:n], lhsT=slab(ch, kc, 128 * cc, 128), rhs=hT[:, kc, s0:s0 + n],
                                start=(kc == 0), stop=(kc == 7)),
                                reads=ckeys("hT", s0, n) + [("ring", slot_of(ch))], writes=[("pb", bank)])
                    fa, fb_, f1 = ntf(), ntf(), ntf()
                    P.add("act", lambda e, bGA=bGA, fa=fa, n=n: e.activation(out=tmpF[:, fa, 0:n], in_=pb[bGA][:, 0:n], func=AF.Sigmoid),
                          reads=[("pb", bGA)], writes=[("tmpF", fa)])
                    P.add("dve", lambda e, bP=bP, fa=fa, f1=f1, n=n: e.tensor_tensor(out=tmpF[:, f1, 0:n], in0=pb[bP][:, 0:n],
                                                                                    in1=tmpF[:, fa, 0:n], op=ALU.mult),
                          reads=[("pb", bP), ("tmpF", fa)], writes=[("tmpF", f1)])
                    P.add("act", lambda e, bGB=bGB, fb_=fb_, n=n: e.activation(out=tmpF[:, fb_, 0:n], in_=pb[bGB][:, 0:n], func=AF.Sigmoid),
                          reads=[("pb", bGB)], writes=[("tmpF", fb_)])
                    P.add("dve", lambda e, bA=bA, fb_=fb_, n=n: e.tensor_tensor(out=tmpF[:, fb_, 0:n], in0=pb[bA][:, 0:n],
                                                                               in1=tmpF[:, fb_, 0:n], op=ALU.mult),
                          reads=[("pb", bA), ("tmpF", fb_)], writes=[("tmpF", fb_)])
                    P.add("dve", lambda e, fb_=fb_, f1=f1, s0=s0, n=n, c=c: e.tensor_tensor(out=mT[:, c, s0:s0 + n], in0=tmpF[:, f1, 0:n],
                                                                                           in1=tmpF[:, fb_, 0:n], op=ALU.add),
                          reads=[("tmpF", f1), ("tmpF", fb_)], writes=ckeys("mT", s0, n, c))
            release_chunk(cga)
            release_chunk(cgb)

    def stage_wout_norm2(p):
        tiles = pass_tiles(p)
        load_gain(n2_d)
        pend = None
        for (i, R, c0) in tiles:
            for hf in range(2):
                bank = nbank()
                for kc in range(8):
                    P.add("pe", lambda e, bank=bank, kc=kc, hf=hf, i=i, R=R, c0=c0: e.matmul(
                        pb[bank][:R, :], lhsT=mT[:, kc, c0:c0 + R], rhs=wo[:, kc, 512 * hf:512 * hf + 512],
                        start=(kc == 0), stop=(kc == 7)),
                        reads=[("mT", kc, i), ("wo", kc // 2)], writes=[("pb", bank)])
                P.add("dve", lambda e, bank=bank, hf=hf, i=i, R=R: e.tensor_tensor(
                    out=xres[:R, i, 512 * hf:512 * hf + 512], in0=pb[bank][:R, :], in1=xres[:R, i, 512 * hf:512 * hf + 512], op=ALU.add),
                    reads=[("pb", bank), ("xres", i, hf)], writes=[("xres", i, hf)])
            b = norm_pre(p, i, R)
            if pend is not None:
                norm_post(p, *pend)
            pend = (i, R, c0, b)
        norm_post(p, *pend)

    def stage_ffn(p, cb):
        tiles = pass_tiles(p)
        subs = pass_subs(p)
        c = cb + 13
        ngrp = len(FFN_GROUPS)
        for gi, grp in enumerate(FFN_GROUPS):
            gchunks = []
            for si, s in enumerate(grp):
                cg, cu = c, c + 1
                c += 2
                for fcl in range(2):
                    fc = 2 * si + fcl
                    for (s0, n) in subs:
                        bG, bU = nbank(), nbank()
                        for (bank, ch) in ((bG, cg), (bU, cu)):
                            for kc in range(8):
                                P.add("pe", lambda e, bank=bank, ch=ch, kc=kc, s0=s0, n=n, fcl=fcl: e.matmul(
                                    pb[bank][:, 0:n], lhsT=slab(ch, kc, 128 * fcl, 128), rhs=hT[:, kc, s0:s0 + n],
                                    start=(kc == 0), stop=(kc == 7)),
                                    reads=ckeys("hT", s0, n) + [("ring", slot_of(ch))], writes=[("pb", bank)])
                        fg = ntf()
                        P.add("act", lambda e, bG=bG, fg=fg, n=n: e.activation(out=tmpF[:, fg, 0:n], in_=pb[bG][:, 0:n], func=AF.Silu),
                              reads=[("pb", bG)], writes=[("tmpF", fg)])
                        P.add("dve", lambda e, bU=bU, fg=fg, fc=fc, s0=s0, n=n: e.tensor_tensor(
                            out=actT[:, fc, s0:s0 + n], in0=pb[bU][:, 0:n], in1=tmpF[:, fg, 0:n], op=ALU.mult),
                            reads=[("pb", bU), ("tmpF", fg)], writes=ckeys("actT", s0, n, fc))
                release_chunk(cg)
                release_chunk(cu)
            dch = list(range(c, c + len(grp)))
            c += len(grp)
            nfc = 2 * len(grp)
            for (i, R, c0) in tiles:
                for hf in range(2):
                    bank = nbank()
                    for fc in range(nfc):
                        ch = dch[fc // 2]
                        P.add("pe", lambda e, bank=bank, fc=fc, ch=ch, hf=hf, R=R, c0=c0: e.matmul(
                            pb[bank][:R, :], lhsT=actT[:, fc, c0:c0 + R], rhs=dchunk(ch, fc % 2, hf),
                            start=(fc == 0), stop=(fc == nfc - 1)),
                            reads=[("actT", fc, i), ("ring", slot_of(ch))], writes=[("pb", bank)])
                    P.add("dve", lambda e, bank=bank, hf=hf, i=i, R=R: e.tensor_tensor(
                        out=xres[:R, i, 512 * hf:512 * hf + 512], in0=pb[bank][:R, :], in1=xres[:R, i, 512 * hf:512 * hf + 512], op=ALU.add),
                        reads=[("pb", bank), ("xres", i, hf)], writes=[("xres", i, hf)])
                if gi == ngrp - 1:
                    if i < NT:
                        dst = y_d[p * T + 128 * i:p * T + 128 * i + 128, :]
                    else:
                        dst = ys_d
                    out_stores.append(P.dma("sp", dst, xres[:R, i, :], reads=[("xres", i, 0), ("xres", i, 1)], dsem="ys%d" % i))
                    if p + 1 < NPASS and i < NT:
                        if i == 0:
                            load_gain(n1_d)
                        P.dma("sp", xres[:128, i, :], x_d[(p + 1) * T + 128 * i:(p + 1) * T + 128 * i + 128, :],
                              writes=[("xres", i, 0), ("xres", i, 1)], dsem="xl%d" % i)
                        prefetched_x.add((p + 1, i))
            for ch in dch:
                release_chunk(ch)

    all_r1 = [("wp",), ("wa",)]
    all_act = [("actT", fc, i) for fc in range(8) for i in range(NT + 1)]
    all_uT = [("uT", g, i) for g in range(4) for i in range(NT + 1)] + [("uThist",)]
    all_mT = [("mT", c, i) for c in range(8) for i in range(NT + 1)]
    for p in range(NPASS):
        cb = p * CPP
        fence("dve", all_mT, all_uT)
        stage_A0(p, cb)
        fence("dve", all_act, all_r1)
        for k in range(2):
            P.dma("pool", r1[:, 2 * k:2 * k + 2, 0:D], wp_d[256 * k:256 * k + 256, :].rearrange("(kc p) n -> p kc n", p=128),
                  reads=[], writes=[("wp",)], dsem="wpl")
        war = wa_d.rearrange("(k j d) n -> k d j n", k=2, j=4)
        for kvh in range(2):
            P.dma("pool", r1[64 * kvh:64 * kvh + 64, 4:8, 0:D], war[kvh], writes=[("wa",)], dsem="wal")
        rest = stage_attn(p, pool_groups(p))
        if p == SAMPLE_PASS and ENABLE_SAMPLE and not DBG_SKIP_SATTN:
            stage_attn_sample(p)
        for f_ in rest:
            f_()
        fence("dve", all_uT, all_mT)
        stage_merge(p, cb)
        stage_wout_norm2(p)
        fence("dve", all_r1, all_act)
        stage_ffn(p, cb)

    P.add("sp", None, extra=out_stores)
    P.emit()
    return nc


_CACHE = {}


def _get_nc():
    if "nc" not in _CACHE:
        _CACHE["nc"] = build_program()
    return _CACHE["nc"]


def kernel(x_prompt, x_sample, cache_k, cache_v, state_pool, norm1, w_in, q_norm, k_norm, sinks,
           pool_mix_w, pool_scale, w_pool_proj, w_attn_proj, w_out, norm2, w_gate, w_up, w_down):
    f = lambda a: np.ascontiguousarray(np.asarray(a, dtype=np.float32))
    x_prompt, x_sample, cache_k, cache_v, state_pool = map(f, (x_prompt, x_sample, cache_k, cache_v, state_pool))
    shared = {
        "norm1": f(norm1).reshape(1, D), "w_in": f(w_in).reshape(D, INW), "q_norm": f(q_norm).reshape(1, 64),
        "k_norm": f(k_norm).reshape(1, 64), "sinks": f(sinks).reshape(1, 8), "pool_mix_w": f(pool_mix_w).reshape(4, 128, 128),
        "pool_scale": f(pool_scale).reshape(512), "w_pool_proj": f(w_pool_proj).reshape(512, D),
        "w_attn_proj": f(w_attn_proj).reshape(512, D), "w_out": f(w_out).reshape(D, D), "norm2": f(norm2).reshape(1, D),
        "w_gate": f(w_gate).reshape(D, DFF), "w_up": f(w_up).reshape(D, DFF), "w_down": f(w_down).reshape(DFF, D),
    }
    in_maps = []
    for c in range(NCORES):
        m = dict(shared)
        m["x"] = x_prompt[c]
        m["xs"] = x_sample[NSMP * c:NSMP * c + NSMP, 0, :]
        m["ck"] = cache_k[0, NSMP * c:NSMP * c + NSMP].reshape(NSMP, 128, 128)
        m["cv"] = cache_v[0, NSMP * c:NSMP * c + NSMP].reshape(NSMP, 128, 128)
        m["ckT"] = m["ck"].transpose(0, 2, 1)
        m["st"] = state_pool[0, NSMP * c:NSMP * c + NSMP]
        in_maps.append({k: np.ascontiguousarray(v) for k, v in m.items()})
    nc = _get_nc()
    res = run_bass_kernel_spmd(nc, in_maps, core_ids=list(range(NCORES)))
    r = res.results
    y = np.stack([r[c]["y"] for c in range(NCORES)], 0).astype(np.float32)
    ys = np.concatenate([r[c]["ys"] for c in range(NCORES)], 0).reshape(NCORES * NSMP, 1, D).astype(np.float32)
    kp = np.stack([r[c]["kp"].reshape(128, 2, 64) for c in range(NCORES)], 0)[None].astype(np.float32)
    vp = np.stack([r[c]["vp"].reshape(128, 2, 64) for c in range(NCORES)], 0)[None].astype(np.float32)
    pp = np.stack([r[c]["pp"] for c in range(NCORES)], 0)[None].astype(np.float32)
    ksn = np.concatenate([r[c]["ksn"].reshape(NSMP, 128, 2, 64) for c in range(NCORES)], 0)[None].astype(np.float32)
    vsn = np.concatenate([r[c]["vsn"].reshape(NSMP, 128, 2, 64) for c in range(NCORES)], 0)[None].astype(np.float32)
    psn = np.concatenate([r[c]["psn"] for c in range(NCORES)], 0)[None].astype(np.float32)
    return (y, ys, kp, vp, pp, ksn, vsn, psn)
```
